# Optimizing a Trainium2 kernel written in Bass

```python
import math
import jax, jax.numpy as jnp
from jax import lax
import numpy as np

D_MODEL = 4096
BATCH = 8
SEQ = 2048
DEPTH = 2
DEC_BATCH = 2
DEC_SEQ = 4096
PAST_LEN = 128

SSM_WIDTH = D_MODEL
SSM_HEAD_DIM = 64
SSM_HEADS = SSM_WIDTH // SSM_HEAD_DIM
SSM_GROUPS = 8
SSM_STATE = 128
SSM_CONV = 5
SSD_CHUNK = 128
SSM_GN = SSM_GROUPS * SSM_STATE
SSM_XBC = SSM_WIDTH + 2 * SSM_GN
SSM_IN = SSM_WIDTH + SSM_XBC + 2 * SSM_HEADS
HY_WIDTH = D_MODEL // 2
HY_SHORT = 3
HY_EMB = 33
HY_ORDER = 64
HY_FAST_DECAY = 0.3
HY_SLOW_DECAY = 1.5
HY_TARGET = 1e-2
HY_IN = 3 * HY_WIDTH
SG_WIDTH = D_MODEL // 2
SG_CHUNK = 128
SG_GROUPS = 16
SG_GROUP_DIM = SG_WIDTH // SG_GROUPS
SG_IN = 2 * SG_WIDTH
N_BRANCH = 3
GATE_IN = N_BRANCH * D_MODEL
MIX_WIDTH = SSM_WIDTH + HY_WIDTH + SG_WIDTH
N_IN = SSM_IN + HY_IN + SG_IN + GATE_IN
D_FF = 256 * math.ceil(8 * D_MODEL / 3 / 256)
FFN_CONV = 3
RMS_EPS = 1e-6
LN_EPS = 1e-5
SPLIT_IN = (SSM_WIDTH, SSM_WIDTH + SSM_XBC, SSM_IN, SSM_IN + HY_IN, SSM_IN + HY_IN + SG_IN)

kernel_name = "hybrid_ssd_hyena_sgu_encoder"


def _rmsnorm(x, g):
    xf = x.astype(jnp.float32)
    y = xf * lax.rsqrt(jnp.mean(xf * xf, axis=-1, keepdims=True) + RMS_EPS)
    return (y * g.astype(jnp.float32)).astype(x.dtype)


def _dwconv(x, w, b):
    k = w.shape[0]
    pad = k // 2
    n = x.shape[1]
    xp = jnp.pad(x, ((0, 0), (pad, pad), (0, 0)))
    out = b
    for j in range(k):
        out = out + w[j] * xp[:, j:j + n]
    return out


def _ssd(x, dt, a, bm, cm):
    bt, n, nh, hp = x.shape
    ng, ns = bm.shape[2], bm.shape[3]
    hg = nh // ng
    nc = n // SSD_CHUNK
    t = SSD_CHUNK
    xc = (x * dt[..., None]).reshape(bt, nc, t, ng, hg, hp)
    a_cs = jnp.cumsum((dt * a).reshape(bt, nc, t, ng, hg), axis=2)
    bc = bm.reshape(bt, nc, t, ng, ns)
    cc = cm.reshape(bt, nc, t, ng, ns)
    lower = jnp.tril(jnp.ones((t, t), dtype=bool))[None, None, :, :, None, None]
    seg = a_cs[:, :, :, None] - a_cs[:, :, None, :]
    decay = jnp.exp(jnp.where(lower, seg, -jnp.inf))
    cb = jnp.einsum('bclgn,bcsgn->bclsg', cc, bc)
    y_diag = jnp.einsum('bclsg,bclsgh,bcsghp->bclghp', cb, decay, xc)
    decay_end = jnp.exp(a_cs[:, :, -1:] - a_cs)
    states = jnp.einsum('bclgn,bclgh,bclghp->bcghpn', bc, decay_end, xc)
    chunk_decay = jnp.exp(a_cs[:, :, -1])

    def step(h, inp):
        s, d = inp
        return h * d[..., None, None] + s, h

    h0 = jnp.zeros((bt, ng, hg, hp, ns), x.dtype)
    _, h_in = lax.scan(step, h0, (jnp.moveaxis(states, 1, 0), jnp.moveaxis(chunk_decay, 1, 0)))
    h_in = jnp.moveaxis(h_in, 0, 1)
    y_off = jnp.einsum('bclgn,bcghpn,bclgh->bclghp', cc, h_in, jnp.exp(a_cs))
    return (y_diag + y_off).reshape(bt, n, nh, hp)


def _ssd_branch(z, xbc, dt_raw, conv_w, conv_b, dt_bias, a_log, d_skip, norm_g):
    f32 = jnp.float32
    xbc = jax.nn.silu(_dwconv(xbc.astype(f32), conv_w.astype(f32), conv_b.astype(f32)))
    bt, n, _ = xbc.shape
    xs = xbc[..., :SSM_WIDTH].reshape(bt, n, SSM_HEADS, SSM_HEAD_DIM)
    bm = xbc[..., SSM_WIDTH:SSM_WIDTH + SSM_GN].reshape(bt, n, SSM_GROUPS, SSM_STATE)
    cm = xbc[..., SSM_WIDTH + SSM_GN:].reshape(bt, n, SSM_GROUPS, SSM_STATE)
    dt = jax.nn.softplus(dt_raw.astype(f32).reshape(bt, n, 2, SSM_HEADS) + dt_bias.astype(f32))
    a = -jnp.exp(a_log.astype(f32))
    flip = lambda v: jnp.flip(v, axis=1)
    y = (_ssd(xs, dt[:, :, 0], a[0], bm, cm)
         + flip(_ssd(flip(xs), flip(dt[:, :, 1]), a[1], flip(bm), flip(cm)))
         + xs * d_skip.astype(f32)[:, None])
    y = y.reshape(bt, n, SSM_WIDTH) * jax.nn.silu(z.astype(f32))
    yg = y.reshape(bt, n, SSM_GROUPS, SSM_WIDTH // SSM_GROUPS)
    yg = yg * lax.rsqrt(jnp.mean(yg * yg, axis=-1, keepdims=True) + RMS_EPS)
    return yg.reshape(bt, n, SSM_WIDTH) * norm_g.astype(f32)


def _hyena_filters(n, w1, b1, w2, b2, w3, b3, freq, w4):
    f32 = jnp.float32
    t = jnp.linspace(0.0, 1.0, n, dtype=f32)[:, None]
    bands = (HY_EMB - 1) // 2
    wpos = (2.0 * math.pi / n) * jnp.arange(n, dtype=f32)[:, None]
    fr = jnp.linspace(1e-4, bands - 1, bands, dtype=f32)[None, :]
    zpos = jnp.concatenate([t, jnp.cos(fr * wpos), -jnp.sin(fr * wpos)], axis=-1)
    fq = freq.astype(f32)
    h = jnp.sin(fq * (zpos @ w1.astype(f32) + b1.astype(f32)))
    h = jnp.sin(fq * (h @ w2.astype(f32) + b2.astype(f32)))
    h = jnp.sin(fq * (h @ w3.astype(f32) + b3.astype(f32)))
    h = h @ w4.astype(f32)
    min_decay = math.log(HY_TARGET) / HY_SLOW_DECAY
    max_decay = math.log(HY_TARGET) / HY_FAST_DECAY
    deltas = jnp.abs(jnp.linspace(min_decay, max_decay, HY_WIDTH, dtype=f32))
    window = jnp.exp(-t * deltas[None, :])
    return h.reshape(n, 2, HY_WIDTH) * window[:, None, :]


def _hyena_branch(proj, conv_w, conv_b, w1, b1, w2, b2, w3, b3, freq, w4, bias):
    f32 = jnp.float32
    u = _dwconv(proj.astype(f32), conv_w.astype(f32), conv_b.astype(f32))
    x0 = u[..., :HY_WIDTH]
    x1 = u[..., HY_WIDTH:2 * HY_WIDTH]
    v = u[..., 2 * HY_WIDTH:]
    n = u.shape[1]
    k = _hyena_filters(n, w1, b1, w2, b2, w3, b3, freq, w4)
    k2 = jnp.concatenate([k[:, 0], jnp.zeros((1, HY_WIDTH), f32), k[:0:-1, 1]], axis=0)
    w = v * x1
    yc = jnp.fft.irfft(jnp.fft.rfft(w, n=2 * n, axis=1) * jnp.fft.rfft(k2, axis=0)[None],
                       n=2 * n, axis=1)[:, :n]
    return x0 * (yc + w * bias.astype(f32))


def _sgu_branch(proj, ln_g, ln_b, ws, bs):
    f32 = jnp.float32
    h = jax.nn.gelu(proj.astype(f32), approximate=False)
    u, v = h[..., :SG_WIDTH], h[..., SG_WIDTH:]
    mu = jnp.mean(v, axis=-1, keepdims=True)
    var = jnp.mean(jnp.square(v - mu), axis=-1, keepdims=True)
    v = (v - mu) * lax.rsqrt(var + LN_EPS) * ln_g.astype(f32) + ln_b.astype(f32)
    bt, n, _ = v.shape
    vc = v.reshape(bt, n // SG_CHUNK, SG_CHUNK, SG_GROUPS, SG_GROUP_DIM)
    mixed = jnp.einsum('gts,bcsgd->bctgd', ws.astype(f32), vc) + bs.astype(f32).T[:, :, None]
    return u * mixed.reshape(bt, n, SG_WIDTH)


def _layer(x, norm1_g, w_in, b_gate, ssm_conv_w, ssm_conv_b, ssm_dt_bias, ssm_a_log, ssm_d,
           ssm_norm_g, hy_conv_w, hy_conv_b, hy_w1, hy_b1, hy_w2, hy_b2, hy_w3, hy_b3, hy_freq,
           hy_w4, hy_bias, sg_ln_g, sg_ln_b, sg_ws, sg_bs, w_br, w_out, norm2_g, w_up,
           ffn_conv_w, ffn_conv_b, w_down):
    dtype = x.dtype
    h = _rmsnorm(x, norm1_g)
    p = h @ w_in
    z, xbc, dt_raw, hy_p, sg_p, gate_l = jnp.split(p, SPLIT_IN, axis=-1)
    y_m = _ssd_branch(z, xbc, dt_raw, ssm_conv_w, ssm_conv_b, ssm_dt_bias, ssm_a_log, ssm_d,
                      ssm_norm_g).astype(dtype)
    y_h = _hyena_branch(hy_p, hy_conv_w, hy_conv_b, hy_w1, hy_b1, hy_w2, hy_b2, hy_w3, hy_b3,
                        hy_freq, hy_w4, hy_bias).astype(dtype)
    y_g = _sgu_branch(sg_p, sg_ln_g, sg_ln_b, sg_ws, sg_bs).astype(dtype)
    bt, n, _ = x.shape
    gates = jax.nn.sigmoid((gate_l + b_gate).astype(jnp.float32)).astype(dtype)
    gates = gates.reshape(bt, n, N_BRANCH, D_MODEL)
    merged = (gates[:, :, 0] * (y_m @ w_br[:SSM_WIDTH])
              + gates[:, :, 1] * (y_h @ w_br[SSM_WIDTH:SSM_WIDTH + HY_WIDTH])
              + gates[:, :, 2] * (y_g @ w_br[SSM_WIDTH + HY_WIDTH:]))
    x = x + merged @ w_out
    h2 = _rmsnorm(x, norm2_g)
    up = _dwconv(h2 @ w_up, ffn_conv_w, ffn_conv_b)
    g, val = up[..., :D_FF], up[..., D_FF:]
    return x + (jax.nn.silu(g) * val) @ w_down


def setup_inputs(seed: int = 0) -> dict:
    key = jax.random.key(seed)
    ks = jax.random.split(key, 40)
    f32 = jnp.float32
    L = DEPTH

    def nrm(k, shape, s):
        return s * jax.random.normal(k, shape, f32)

    def gain(k, shape):
        return 1.0 + 0.01 * jax.random.normal(k, shape, f32)

    dt0 = jnp.exp(jax.random.uniform(ks[8], (L, 2, SSM_HEADS), f32,
                                     minval=math.log(1e-3), maxval=math.log(1e-1)))
    dt_bias = dt0 + jnp.log(-jnp.expm1(-dt0))
    a_log = jnp.log(jax.random.uniform(ks[9], (L, 2, SSM_HEADS), f32, minval=1.0, maxval=16.0))
    w_br = jnp.concatenate([nrm(ks[26], (L, SSM_WIDTH, D_MODEL), SSM_WIDTH ** -0.5),
                            nrm(ks[27], (L, HY_WIDTH, D_MODEL), HY_WIDTH ** -0.5),
                            nrm(ks[28], (L, SG_WIDTH, D_MODEL), SG_WIDTH ** -0.5)], axis=1)
    return {
        "x_prompt": jax.random.normal(ks[0], (BATCH, SEQ, D_MODEL), f32),
        "x_sample": jax.random.normal(ks[1], (DEC_BATCH, DEC_SEQ, D_MODEL), f32),
        "norm1_g": gain(ks[2], (L, D_MODEL)),
        "w_in": nrm(ks[3], (L, D_MODEL, N_IN), D_MODEL ** -0.5),
        "b_gate": nrm(ks[4], (L, GATE_IN), 0.01),
        "ssm_conv_w": nrm(ks[5], (L, SSM_CONV, SSM_XBC), SSM_CONV ** -0.5),
        "ssm_conv_b": nrm(ks[6], (L, SSM_XBC), 0.01),
        "ssm_dt_bias": dt_bias,
        "ssm_a_log": a_log,
        "ssm_d": gain(ks[7], (L, SSM_HEADS)),
        "ssm_norm_g": gain(ks[10], (L, SSM_WIDTH)),
        "hy_conv_w": nrm(ks[11], (L, HY_SHORT, HY_IN), HY_SHORT ** -0.5),
        "hy_conv_b": nrm(ks[12], (L, HY_IN), 0.01),
        "hy_w1": nrm(ks[13], (L, HY_EMB, HY_ORDER), HY_EMB ** -0.5),
        "hy_b1": nrm(ks[14], (L, HY_ORDER), 0.1),
        "hy_w2": nrm(ks[15], (L, HY_ORDER, HY_ORDER), HY_ORDER ** -0.5),
        "hy_b2": nrm(ks[16], (L, HY_ORDER), 0.1),
        "hy_w3": nrm(ks[17], (L, HY_ORDER, HY_ORDER), HY_ORDER ** -0.5),
        "hy_b3": nrm(ks[18], (L, HY_ORDER), 0.1),
        "hy_freq": gain(ks[19], (L, HY_ORDER)),
        "hy_w4": nrm(ks[20], (L, HY_ORDER, 2 * HY_WIDTH), 0.05 * HY_ORDER ** -0.5),
        "hy_bias": nrm(ks[21], (L, HY_WIDTH), 1.0),
        "sg_ln_g": gain(ks[22], (L, SG_WIDTH)),
        "sg_ln_b": nrm(ks[23], (L, SG_WIDTH), 0.01),
        "sg_ws": nrm(ks[24], (L, SG_GROUPS, SG_CHUNK, SG_CHUNK), SG_CHUNK ** -0.5),
        "sg_bs": gain(ks[25], (L, SG_GROUPS, SG_CHUNK)),
        "w_br": w_br,
        "w_out": nrm(ks[29], (L, D_MODEL, D_MODEL), D_MODEL ** -0.5),
        "norm2_g": gain(ks[30], (L, D_MODEL)),
        "w_up": nrm(ks[31], (L, D_MODEL, 2 * D_FF), D_MODEL ** -0.5),
        "ffn_conv_w": nrm(ks[32], (L, FFN_CONV, 2 * D_FF), FFN_CONV ** -0.5),
        "ffn_conv_b": nrm(ks[33], (L, 2 * D_FF), 0.01),
        "w_down": nrm(ks[34], (L, D_FF, D_MODEL), D_FF ** -0.5),
        "normf_g": gain(ks[35], (D_MODEL,)),
    }


def reference(x_prompt, x_sample, norm1_g, w_in, b_gate, ssm_conv_w, ssm_conv_b, ssm_dt_bias,
              ssm_a_log, ssm_d, ssm_norm_g, hy_conv_w, hy_conv_b, hy_w1, hy_b1, hy_w2, hy_b2,
              hy_w3, hy_b3, hy_freq, hy_w4, hy_bias, sg_ln_g, sg_ln_b, sg_ws, sg_bs, w_br, w_out,
              norm2_g, w_up, ffn_conv_w, ffn_conv_b, w_down, normf_g):
    def trunk(x):
        for l in range(DEPTH):
            x = _layer(x, norm1_g[l], w_in[l], b_gate[l], ssm_conv_w[l], ssm_conv_b[l],
                       ssm_dt_bias[l], ssm_a_log[l], ssm_d[l], ssm_norm_g[l], hy_conv_w[l],
                       hy_conv_b[l], hy_w1[l], hy_b1[l], hy_w2[l], hy_b2[l], hy_w3[l], hy_b3[l],
                       hy_freq[l], hy_w4[l], hy_bias[l], sg_ln_g[l], sg_ln_b[l], sg_ws[l], sg_bs[l],
                       w_br[l], w_out[l], norm2_g[l], w_up[l], ffn_conv_w[l], ffn_conv_b[l],
                       w_down[l])
        return _rmsnorm(x, normf_g)

    y_prompt = trunk(x_prompt)
    y_sample = trunk(x_sample)
    return (y_prompt, y_sample)
```

```python
import math
import contextlib
import numpy as np
import ml_dtypes
import concourse.bass as bass
import concourse.mybir as mybir
from concourse.bass_utils import run_bass_kernel_spmd

F32 = mybir.dt.float32
BF16 = mybir.dt.bfloat16
AF = mybir.ActivationFunctionType
ALU = mybir.AluOpType

ENGS = ("pe", "act", "dve", "pool", "sp")
NDMASEM = 12
NEGBIG = -30000.0


class Op:
    __slots__ = ("eng", "fn", "dma", "waits", "inc", "dsem", "dval", "idx", "prewait")

    def __init__(self, eng, fn, dma):
        self.eng = eng
        self.fn = fn
        self.dma = dma
        self.waits = []
        self.inc = False
        self.dsem = None
        self.dval = 0
        self.prewait = None
        self.idx = 0


class Sched:
    def __init__(self, nc):
        self.nc = nc
        self.ops = {e: [] for e in ENGS}
        self.last_w = {}
        self.readers = {}
        self.dma_rr = {e: 0 for e in ENGS}
        self.dma_cum = {}
        self.sbuf_off = 16640
        self.sbuf_base = 16640
        self.nbar = 0
        self.maxbar = 10 ** 9
        self.nalloc = 0

    def sb(self, name, shape, dtype):
        esz = 2 if dtype == BF16 else 4
        n = 1
        for s in shape[1:]:
            n *= s
        nbytes = (n * esz + 63) // 64 * 64
        off = self.sbuf_off
        self.sbuf_off += nbytes
        assert self.sbuf_off <= 229312, ("SBUF overflow", name, self.sbuf_off)
        self.nalloc += 1
        return self.nc.alloc_sbuf_tensor_at(f"{name}_{self.nalloc}", list(shape), dtype, offset=off)

    def sb_reset(self):
        self.sbuf_off = self.sbuf_base

    def _dep(self, op, dep):
        if dep is None or dep is op:
            return
        if dep.dma:
            op.waits.append((("d", dep.eng, dep.dsem), dep.dval))
        else:
            if dep.eng == op.eng and not op.dma and dep.eng == "pe":
                return
            dep.inc = True
            op.waits.append((("e", dep.eng), dep))

    def add(self, eng, fn, reads=(), writes=(), dma=False):
        op = Op(eng, fn, dma)
        if self.nbar >= self.maxbar:
            return op
        for k in reads:
            self._dep(op, self.last_w.get(k))
            if k.startswith("ps"):
                for r in self.readers.get(k, ()):
                    if r.eng != eng:
                        self._dep(op, r)
        for k in writes:
            self._dep(op, self.last_w.get(k))
            for r in self.readers.get(k, ()):
                self._dep(op, r)
        for k in reads:
            self.readers.setdefault(k, []).append(op)
        for k in writes:
            self.last_w[k] = op
            self.readers[k] = []
        if dma:
            slot = self.dma_rr[eng] % NDMASEM
            self.dma_rr[eng] += 1
            key = (eng, slot)
            prev = self.dma_cum.get(key, 0)
            if prev:
                op.prewait = (("d", eng, slot), prev)
            op.dsem = slot
            op.dval = prev + 16
            self.dma_cum[key] = op.dval
        self.ops[eng].append(op)
        return op

    def barrier(self):
        if self.nbar >= self.maxbar:
            return
        self.nbar += 1
        nb = self.nbar
        op = Op("sp", ("bar_signal", nb), False)
        for e in ENGS:
            if e == "sp":
                continue
            for o in reversed(self.ops[e]):
                if not o.dma and not isinstance(o.fn, tuple):
                    o.inc = True
                    op.waits.append((("e", e), o))
                    break
        op.waits.extend([(("d", e, s), v) for (e, s), v in self.dma_cum.items()])
        self.ops["sp"].append(op)
        for e in ENGS:
            if e != "sp":
                self.ops[e].append(Op(e, ("bar_wait", nb), False))
        self.last_w = {}
        self.readers = {}

    def emit(self):
        nc = self.nc
        with contextlib.ExitStack() as st:
            esem = {e: st.enter_context(nc.semaphore(f"s_{e}")) for e in ENGS}
            dsem = {}
            for e in ENGS:
                if any(o.dma for o in self.ops[e]):
                    for s in range(NDMASEM):
                        dsem[(e, s)] = st.enter_context(nc.semaphore(f"d_{e}_{s}"))
            bsem = st.enter_context(nc.semaphore("s_bar"))
            for e in ENGS:
                c = 0
                for o in self.ops[e]:
                    if o.inc and not o.dma:
                        c += 1
                        o.idx = c
            block = st.enter_context(nc.Block())

            def run(e, eng):
                seen = {}
                for o in self.ops[e]:
                    waits = list(o.waits)
                    if o.prewait:
                        waits.append(o.prewait)
                    best = {}
                    for k, v in waits:
                        if not isinstance(v, int):
                            v = v.idx
                        if v > best.get(k, 0):
                            best[k] = v
                    for k, v in best.items():
                        if seen.get(k, 0) >= v:
                            continue
                        seen[k] = v
                        if k[0] == "e":
                            eng.wait_ge(esem[k[1]], v)
                        else:
                            eng.wait_ge(dsem[(k[1], k[2])], v)
                    if isinstance(o.fn, tuple):
                        if o.fn[0] == "bar_signal":
                            eng.sem_inc(bsem, 1)
                        else:
                            eng.wait_ge(bsem, o.fn[1])
                        continue
                    ins = o.fn(eng)
                    if o.dma:
                        ins.then_inc(dsem[(e, o.dsem)], 16)
                    elif o.inc:
                        ins.then_inc(esem[e], 1)

            block.tensor(lambda eng: run("pe", eng))
            block.scalar(lambda eng: run("act", eng))
            block.vector(lambda eng: run("dve", eng))
            block.gpsimd(lambda eng: run("pool", eng))
            block.sync(lambda eng: run("sp", eng))


def make_cfg(D=4096, SEG=2048, L=2, NG=8, DFF=11008):
    c = dict(D=D, SEG=SEG, T=2 * SEG, L=L, NG=NG, DFF=DFF)
    c["NH"] = D // 64
    c["HG"] = c["NH"] // NG
    c["GN"] = NG * 128
    c["XBC"] = D + 2 * c["GN"]
    c["HYW"] = D // 2
    c["SGW"] = D // 2
    c["SGG"] = c["SGW"] // 128
    c["NIN"] = D + c["XBC"] + 2 * c["NH"] + 3 * c["HYW"] + 2 * c["SGW"] + 3 * D
    off = {}
    n = 0
    for name, w in [("g1", D), ("g2", D), ("bgate", 3 * D)] + [(f"scw{j}", c["XBC"]) for j in range(5)] + \
            [("scb", c["XBC"]), ("dskip", D), ("ng", D)] + [(f"hcw{j}", 3 * c["HYW"]) for j in range(3)] + \
            [("hcb", 3 * c["HYW"]), ("hbias", c["HYW"]), ("lng", c["SGW"]), ("lnb", c["SGW"])] + \
            [(f"fcw{j}", 2 * DFF) for j in range(3)] + [("fcb", 2 * DFF), ("gf", D)]:
        off[name] = n
        n += w // 128
    c["pcol_off"] = off
    c["NCOL"] = n
    c["NROW"] = 4 * c["NH"] + c["SGG"] * 128
    return c


CST_NAMES = ["ident", "triU", "triL", "negf", "negb", "ones", "negones"]


def host_consts(cfg):
    SEG = cfg["SEG"]
    k = np.arange(128)[:, None]
    l = np.arange(128)[None, :]
    blocks = {
        "ident": (k == l), "triU": (k <= l), "triL": (k >= l),
        "negf": np.where(l >= k, 0.0, NEGBIG), "negb": np.where(l <= k, 0.0, NEGBIG),
        "ones": np.ones((128, 128)), "negones": -np.ones((128, 128)),
    }
    cst = np.concatenate([np.asarray(blocks[n], np.float32) for n in CST_NAMES] +
                         [np.where(np.arange(128) % 2 == 0, 1.0, -1.0).astype(np.float32)[:, None],
                          (np.arange(128) > 0).astype(np.float32)[:, None]], axis=1)
    N = 2 * SEG
    tau = np.arange(SEG, dtype=np.float64)
    f = np.arange(SEG, dtype=np.float64)
    ang = 2.0 * np.pi * np.outer(tau, f + 0.5) / N
    C = np.cos(ang)
    Sn = np.sin(ang)
    NF = SEG // 128
    fw = np.zeros((NF, 128, NF, 256), np.float32)
    Cr = C.reshape(NF, 128, NF, 128)
    Sr = Sn.reshape(NF, 128, NF, 128)
    fw[:, :, :, 0:128] = Cr.transpose(2, 1, 0, 3)
    fw[:, :, :, 128:256] = Sr.transpose(2, 1, 0, 3)
    TBW = min(512, SEG)
    NTB = SEG // TBW
    iv = np.zeros((NTB, 128, NF, 2, TBW), np.float32)
    Ct = (2.0 / N) * C.T.reshape(NF, 128, NTB, TBW)
    St = (2.0 / N) * Sn.T.reshape(NF, 128, NTB, TBW)
    iv[:, :, :, 0, :] = Ct.transpose(2, 1, 0, 3)
    iv[:, :, :, 1, :] = St.transpose(2, 1, 0, 3)
    HYW = cfg["HYW"]
    min_decay = math.log(1e-2) / 1.5
    max_decay = math.log(1e-2) / 0.3
    deltas = np.abs(np.linspace(min_decay, max_decay, HYW, dtype=np.float32)).astype(np.float32)[None, :]
    return dict(cst=cst, dftf=fw.astype(ml_dtypes.bfloat16), dfti=iv.astype(ml_dtypes.bfloat16), deltas=deltas)


def host_zpos(n, TF):
    t = np.linspace(0.0, 1.0, n, dtype=np.float32)[:, None]
    bands = 16
    wpos = (np.float32(2.0 * math.pi / n) * np.arange(n, dtype=np.float32))[:, None]
    fr = np.linspace(1e-4, bands - 1, bands, dtype=np.float32)[None, :]
    z = np.concatenate([t, np.cos(fr * wpos), -np.sin(fr * wpos)], axis=-1).astype(np.float32)
    zp = np.zeros((TF, 33), np.float32)
    zp[:n] = z
    tt = np.zeros((TF,), np.float32)
    tt[:n] = t[:, 0]
    tneg = (-tt).reshape(TF // 128, 128).T.copy()
    return zp.T.copy(), tneg


def build(cfg, debug=()):
    D, SEG, T, L, NG, DFF = (cfg[k] for k in ("D", "SEG", "T", "L", "NG", "DFF"))
    NH, HG, GN, XBC, HYW, SGW, SGG, NIN = (cfg[k] for k in ("NH", "HG", "GN", "XBC", "HYW", "SGW", "SGG", "NIN"))
    NCOL, NROW, PO = cfg["NCOL"], cfg["NROW"], cfg["pcol_off"]
    KD = D // 128
    NCH = T // 128
    SCH = SEG // 128
    TT = min(512, SEG)
    NTT = T // TT
    NF = SEG // 128
    TBW = min(512, SEG)
    NTB = SEG // TBW
    NCST = 128 * len(CST_NAMES) + 2

    nc = bass.Bass("TRN2", target_bir_lowering=False)
    S = Sched(nc)
    for d_ in debug:
        if d_.startswith("maxbar="):
            S.maxbar = int(d_[7:])

    def din(name, shape, dt=F32):
        return nc.dram_tensor(name, list(shape), dt, kind="ExternalInput").ap()

    def dscr(name, shape, dt=BF16):
        kind = "ExternalOutput" if name in debug else "Internal"
        return nc.dram_tensor(name, list(shape), dt, kind=kind).ap()

    x_in = din("x_in", [T, D])
    w_in = din("w_in", [L, D, NIN])
    w_br = din("w_br", [L, 2 * D, D])
    w_out = din("w_out", [L, D, D])
    w_up = din("w_up", [L, D, 2 * DFF])
    w_down = din("w_down", [L, DFF, D])
    hy_w1 = din("hy_w1", [L, 33, 64])
    hy_w2 = din("hy_w2", [L, 64, 64])
    hy_w3 = din("hy_w3", [L, 64, 64])
    hy_w4 = din("hy_w4", [L, 64, 2 * HYW])
    p64 = din("p64", [L, 64, 4])
    sg_ws = din("sg_ws", [L, SGG, 128, 128])
    pcol = din("pcol", [L, 128, NCOL])
    prow = din("prow", [L, NROW])
    cst_d = din("cst", [128, NCST])
    cpl_d = din("cpl", [128, 1])
    zpos_d = din("zposT", [33, T])
    tneg_d = din("tneg", [128, T // 128])
    deltas_d = din("deltas", [1, HYW])
    dftf_d = din("dftf", [NF, 128, NF, 256], BF16)
    dfti_d = din("dfti", [NTB, 128, NF, 2, TBW], BF16)
    y_out = nc.dram_tensor("y_out", [T, D], F32, kind="ExternalOutput").ap()

    def bigw(name, K_, N_):
        out = []
        for l_ in range(L):
            maxrows = max(128, (200 * 2 ** 20 // (N_ * 2)) // 128 * 128)
            pieces = []
            r0 = 0
            while r0 < K_:
                r1 = min(K_, r0 + maxrows)
                pieces.append((r0, r1, dscr(f"{name}_{l_}_{r0}", [r1 - r0, N_])))
                r0 = r1
            out.append(pieces)
        return out

    wb_in = bigw("wb_in", D, NIN)
    wb_br = bigw("wb_br", 2 * D, D)
    wb_out = bigw("wb_out", D, D)
    wb_up = bigw("wb_up", D, 2 * DFF)
    wb_down = bigw("wb_down", DFF, D)

    def wrows(pieces, a, b):
        out = []
        for (r0, r1, ap) in pieces:
            lo, hi = max(a, r0), min(b, r1)
            if lo < hi:
                out.append((lo - a, hi - a, ap[lo - r0:hi - r0, :]))
        return out
    xT = dscr("xT", [D, T], F32)
    hT = dscr("hT", [D, T])
    pz = dscr("pz", [D, T])
    pxbc = dscr("pxbc", [XBC, T])
    pdt = dscr("pdt", [2 * NH, T], F32)
    phy = dscr("phy", [3 * HYW, T])
    psg = dscr("psg", [2 * SGW, T])
    pgate = dscr("pgate", [3 * D, T])
    xcT = dscr("xcT", [XBC, T])
    x_tok = dscr("x_tok", [T, D])
    b_tok = dscr("b_tok", [T, GN])
    hb_d = dscr("hb_d", [NCH, 128, D])
    ycat = dscr("ycat", [2 * D, T])
    k_tok = dscr("k_tok", [T, 2 * HYW])
    w_tokd = dscr("w_tokd", [T, HYW])
    w_fm = dscr("w_fm", [HYW, T])
    x0_fm = dscr("x0_fm", [HYW, T])
    mergedT = dscr("mergedT", [D, T])
    upT = dscr("upT", [2 * DFF, T])
    actT = dscr("actT", [DFF, T])

    ps = [nc.alloc_psum_tensor(f"psb{i}", [128, 512], F32) for i in range(8)]

    def mm(out, lhsT, rhs, start, stop, r, w):
        S.add("pe", lambda e: e.matmul(out, lhsT=lhsT, rhs=rhs, start=start, stop=stop), r, w)

    def tr(out, in_, ident, r, w):
        S.add("pe", lambda e: e.transpose(out=out, in_=in_, identity=ident), r, w)

    def act(out, in_, func, r, w, bias=None, scale=None):
        kw = {}
        if bias is not None:
            kw["bias"] = bias
        if scale is not None:
            kw["scale"] = scale
        S.add("act", lambda e: e.activation(out=out, in_=in_, func=func, **kw), r, w)

    def tt(eng, out, in0, in1, op, r, w):
        S.add(eng, lambda e: e.tensor_tensor(out=out, in0=in0, in1=in1, op=op), r, w)

    def ts(eng, out, in0, s1, s2, op0, op1, r, w):
        if op1 is None:
            S.add(eng, lambda e: e.tensor_scalar(out=out, in0=in0, scalar1=s1, scalar2=None, op0=op0), r, w)
        else:
            S.add(eng, lambda e: e.tensor_scalar(out=out, in0=in0, scalar1=s1, scalar2=s2, op0=op0, op1=op1), r, w)

    def stt(out, in0, scalar, in1, op0, op1, r, w):
        S.add("dve", lambda e: e.scalar_tensor_tensor(out=out, in0=in0, scalar=scalar, in1=in1, op0=op0, op1=op1), r, w)

    def cp(eng, out, in_, r, w):
        if eng == "act":
            S.add("act", lambda e: e.copy(out=out, in_=in_), r, w)
        else:
            S.add(eng, lambda e: e.tensor_copy(out=out, in_=in_), r, w)

    def recip(out, in_, r, w):
        S.add("dve", lambda e: e.reciprocal(out=out, in_=in_), r, w)

    def mset(eng, ap, v, r, w):
        S.add(eng, lambda e: e.memset(ap, v), r, w)

    def ld(out, in_, w, r=()):
        S.add("sp", lambda e: e.dma_start(out=out, in_=in_), r, w, dma=True)

    def stq(out, in_, r, w=()):
        S.add("pool", lambda e: e.dma_start(out=out, in_=in_), r, w, dma=True)

    cst = S.sb("cst", [128, NCST], F32)
    cplc = S.sb("cplc", [128, 1], F32)
    identb = S.sb("identb", [128, 128], BF16)
    onesb = S.sb("onesb", [128, 128], BF16)
    ld(cst[:], cst_d, ["cst"])
    ld(cplc[:], cpl_d, ["cplc"])

    def C_(name):
        i = CST_NAMES.index(name)
        return cst[:, i * 128:(i + 1) * 128]
    sgn = cst[:, NCST - 2:NCST - 1]
    m0 = cst[:, NCST - 1:NCST]
    cp("dve", identb[:], C_("ident"), ["cst"], ["identb"])
    cp("dve", onesb[:], C_("ones"), ["cst"], ["onesb"])
    S.sbuf_base = S.sbuf_off
    CK = ["cst", "cplc", "identb", "onesb"]

    def rearm():
        pass

    for (src, dst, K_, N_) in ((w_in, wb_in, D, NIN), (w_br, wb_br, 2 * D, D), (w_out, wb_out, D, D),
                               (w_up, wb_up, D, 2 * DFF), (w_down, wb_down, DFF, D)):
        for l in range(L):
            for (p0, p1, pap) in dst[l]:
                for r0 in range(p0, p1, 128):
                    for c0 in range(0, N_, 8192):
                        c1 = min(N_, c0 + 8192)
                        S.add("pool", lambda e, src=src, pap=pap, l=l, r0=r0, p0=p0, c0=c0, c1=c1:
                              e.dma_start(out=pap[r0 - p0:r0 - p0 + 128, c0:c1], in_=src[l, r0:r0 + 128, c0:c1]), dma=True)

    S.sb_reset()
    xin_t = [S.sb("xin", [128, D], F32) for _ in range(2)]
    xst = S.sb("xst", [128, KD, TT], F32)
    for tt_i in range(NTT):
        for tb in range(TT // 128):
            tok0 = tt_i * TT + tb * 128
            b = (tt_i * (TT // 128) + tb) % 2
            ld(xin_t[b][:], x_in[tok0:tok0 + 128, :], [f"xin{b}"])
            for k4 in range(0, KD, 4):
                bank = (k4 // 4) % 4
                nk = min(4, KD - k4)
                for k in range(k4, k4 + nk):
                    tr(ps[bank][:, (k - k4) * 128:(k - k4 + 1) * 128], xin_t[b][:, k * 128:(k + 1) * 128], C_("ident"),
                       [f"xin{b}", "cst"], [f"ps{bank}"])
                eng = "act" if (k4 // 4) % 2 else "dve"
                cp(eng, xst[:, k4:k4 + nk, tb * 128:(tb + 1) * 128],
                   ps[bank][:, 0:nk * 128].rearrange("p (a b) -> p a b", b=128), [f"ps{bank}"], ["xst"])
        stq(xT.rearrange("(k p) t -> p k t", p=128)[:, :, tt_i * TT:(tt_i + 1) * TT], xst[:], ["xst"])
    S.barrier()

    def norm_phase(l, gname, dst, final=False):
        S.sb_reset()
        pc = S.sb("pc", [128, KD], F32)
        ld(pc[:], pcol[l, :, PO[gname]:PO[gname] + KD], ["pc"])
        xt = [S.sb("xt", [128, KD, TT], F32) for _ in range(2)]
        sq = [S.sb("sq", [128, TT], F32) for _ in range(2)]
        rs = S.sb("rs", [128, TT], F32)
        if not final:
            ho = S.sb("ho", [128, KD, TT], BF16)
        else:
            hf = [S.sb("hf", [128, TT], F32) for _ in range(2)]
            yt = [S.sb("yt", [128, D], F32) for _ in range(2)]
        xTv = xT.rearrange("(k p) t -> p k t", p=128)
        for ti in range(NTT):
            b = ti % 2
            ld(xt[b][:], xTv[:, :, ti * TT:(ti + 1) * TT], [f"xt{b}"])
            for k in range(KD):
                act(sq[k % 2][:], xt[b][:, k, :], AF.Square, [f"xt{b}"], [f"sq{k % 2}"])
                mm(ps[0][:, 0:TT], C_("ones"), sq[k % 2][:], k == 0, k == KD - 1, [f"sq{k % 2}", "cst"], ["ps0"])
            ts("dve", rs[:], ps[0][:, 0:TT], 1.0 / D, 1e-6, ALU.mult, ALU.add, ["ps0"], ["rs"])
            act(rs[:], rs[:], AF.Sqrt, ["rs"], ["rs"])
            recip(rs[:], rs[:], ["rs"], ["rs"])
            if not final:
                for k in range(KD):
                    stt(ho[:, k, :], xt[b][:, k, :], pc[:, k:k + 1], rs[:], ALU.mult, ALU.mult, [f"xt{b}", "pc", "rs"], ["ho"])
                stq(dst.rearrange("(k p) t -> p k t", p=128)[:, :, ti * TT:(ti + 1) * TT], ho[:], ["ho"])
            else:
                for tb in range(TT // 128):
                    yb = (ti * (TT // 128) + tb) % 2
                    for k in range(KD):
                        hb_ = k % 2
                        bank = 1 + (k // 4) % 4
                        if tb == 0 or True:
                            stt(hf[hb_][:, 0:128], xt[b][:, k, tb * 128:(tb + 1) * 128], pc[:, k:k + 1],
                                rs[:, tb * 128:(tb + 1) * 128], ALU.mult, ALU.mult, [f"xt{b}", "pc", "rs"], [f"hf{hb_}"])
                        tr(ps[bank][:, (k % 4) * 128:(k % 4 + 1) * 128], hf[hb_][:, 0:128], C_("ident"),
                           [f"hf{hb_}", "cst"], [f"ps{bank}"])
                        if k % 4 == 3 or k == KD - 1:
                            k4 = k - (k % 4)
                            nk = k - k4 + 1
                            cp("act", yt[yb][:, k4 * 128:(k4 + nk) * 128], ps[bank][:, 0:nk * 128], [f"ps{bank}"], [f"yt{yb}"])
                    tok0 = ti * TT + tb * 128
                    stq(y_out[tok0:tok0 + 128, :], yt[yb][:], [f"yt{yb}"])
        S.barrier()

    def gemm(groups, chunks, SW, handler, setup=None):
        S.sb_reset()
        extra = setup() if setup else None
        KCs = [a.shape[0] // 128 for a, _ in groups]
        ng = len(groups)
        at = [S.sb("gact", [128, kc, TT], BF16) for kc in KCs]
        wsl = [[S.sb("gw", [128, kc, SW], BF16) for kc in KCs] for _ in range(2)]
        slabs = []
        cur = []
        for ch in chunks:
            if cur and (ch[0] + ch[1] - cur[0][0] > SW or ch[0] != cur[-1][0] + cur[-1][1]):
                slabs.append(cur)
                cur = []
            cur.append(ch)
        if cur:
            slabs.append(cur)
        nb = 6 // ng
        cnt = 0
        si = 0
        for ti in range(NTT):
            for g, (a, _) in enumerate(groups):
                ld(at[g][:], a.rearrange("(k p) t -> p k t", p=128)[:, :, ti * TT:(ti + 1) * TT], [f"gact{g}"])
            for slab in slabs:
                sb_ = si % 2
                si += 1
                s0 = slab[0][0]
                s1 = slab[-1][0] + slab[-1][1]
                for g, (_, wd) in enumerate(groups):
                    for pi_, (r0, r1, pap) in enumerate(wd):
                        ld(wsl[sb_][g][:, r0 // 128:r1 // 128, 0:s1 - s0], pap.rearrange("(k p) n -> p k n", p=128)[:, :, s0:s1],
                           [f"gw{sb_}_{g}_{pi_}"])
                for (c0, m, tag) in slab:
                    bset = cnt % nb
                    cnt += 1
                    pss = []
                    for g in range(ng):
                        bank = bset * ng + g
                        for kc in range(KCs[g]):
                            pi_ = [i_ for i_, (r0, r1, _) in enumerate(groups[g][1]) if r0 <= kc * 128 < r1][0]
                            mm(ps[bank][0:m, 0:TT], wsl[sb_][g][:, kc, c0 - s0:c0 - s0 + m], at[g][:, kc, :],
                               kc == 0, kc == KCs[g] - 1, [f"gw{sb_}_{g}_{pi_}", f"gact{g}"], [f"ps{bank}"])
                        pss.append((ps[bank][0:m, 0:TT], f"ps{bank}"))
                    handler(tag, c0, m, ti, pss, extra, cnt)
        S.barrier()

    def chunks_of(c0, c1, tag):
        out = []
        c = c0
        while c < c1:
            m = min(128, c1 - c)
            out.append((c, m, tag))
            c += m
        return out

    def sgu_phase(l):
        S.sb_reset()
        SK = SGW // 128
        lg = S.sb("lg", [128, 2 * SK], F32)
        ld(lg[:], pcol[l, :, PO["lng"]:PO["lng"] + 2 * SK], ["lg"])
        bsb = S.sb("bsb", [128, SGG * 128], F32)
        ld(bsb[:], prow[l, 4 * NH:4 * NH + SGG * 128].partition_broadcast(128), ["bsb"])
        wsf = S.sb("wsf", [128, SGG, 128], F32)
        ld(wsf[:], sg_ws[l].rearrange("g t s -> t g s"), ["wsf"])
        wsT = S.sb("wsT", [128, SGG, 128], BF16)
        for g0 in range(0, SGG, 4):
            n4 = min(4, SGG - g0)
            for g in range(g0, g0 + n4):
                tr(ps[0][:, (g - g0) * 128:(g - g0 + 1) * 128], wsf[:, g, :], C_("ident"), ["wsf", "cst"], ["ps0"])
            cp("dve", wsT[:, g0:g0 + n4, :], ps[0][:, 0:n4 * 128].rearrange("p (a b) -> p a b", b=128), ["ps0"], ["wsT"])
        vv = S.sb("vv", [128, SK, TT], BF16)
        uu = S.sb("uu", [128, SK, TT], BF16)
        vn = S.sb("vn", [128, SK, TT], BF16)
        yo = S.sb("yo", [128, SK, TT], BF16)
        sq = [S.sb("sq", [128, TT], F32) for _ in range(2)]
        mean = S.sb("mean", [128, TT], F32)
        msq = S.sb("msq", [128, TT], F32)
        rstd = S.sb("rstd", [128, TT], F32)
        d1 = [S.sb("d1", [128, TT], F32) for _ in range(2)]
        vtk = [S.sb("vtk", [128, 128], BF16) for _ in range(4)]
        mx = [S.sb("mx", [128, TT], F32) for _ in range(2)]
        NTB_ = TT // 128
        cnt = 0
        for ti in range(NTT):
            tsl = slice(ti * TT, (ti + 1) * TT)
            ld(uu[:], psg[0:SGW, :].rearrange("(k p) t -> p k t", p=128)[:, :, tsl], ["uu"])
            ld(vv[:], psg[SGW:2 * SGW, :].rearrange("(k p) t -> p k t", p=128)[:, :, tsl], ["vv"])
            for k in range(SK):
                mm(ps[0][:, 0:TT], onesb[:], vv[:, k, :], k == 0, k == SK - 1, ["onesb", "vv"], ["ps0"])
            for k in range(SK):
                act(sq[k % 2][:], vv[:, k, :], AF.Square, ["vv"], [f"sq{k % 2}"])
                mm(ps[1][:, 0:TT], C_("ones"), sq[k % 2][:], k == 0, k == SK - 1, ["cst", f"sq{k % 2}"], ["ps1"])
            ts("dve", mean[:], ps[0][:, 0:TT], 1.0 / SGW, None, ALU.mult, None, ["ps0"], ["mean"])
            tt("dve", msq[:], mean[:], mean[:], ALU.mult, ["mean"], ["msq"])
            stt(rstd[:], ps[1][:, 0:TT], 1.0 / SGW, msq[:], ALU.mult, ALU.subtract, ["ps1", "msq"], ["rstd"])
            ts("dve", rstd[:], rstd[:], 1e-5, None, ALU.add, None, ["rstd"], ["rstd"])
            act(rstd[:], rstd[:], AF.Sqrt, ["rstd"], ["rstd"])
            recip(rstd[:], rstd[:], ["rstd"], ["rstd"])
            for k in range(SK):
                b = k % 2
                tt("dve", d1[b][:], vv[:, k, :], mean[:], ALU.subtract, ["vv", "mean"], [f"d1{b}"])
                tt("dve", d1[b][:], d1[b][:], rstd[:], ALU.mult, [f"d1{b}", "rstd"], [f"d1{b}"])
                ts("dve", vn[:, k, :], d1[b][:], lg[:, k:k + 1], lg[:, SK + k:SK + k + 1], ALU.mult, ALU.add, [f"d1{b}", "lg"], ["vn"])
            for k in range(SK):
                bank = 2 + k % 2
                pbank = 4 + k % 2
                pst = ps[pbank][:].bitcast(BF16)
                for tb in range(NTB_):
                    vb = cnt % 4
                    cnt += 1
                    tr(pst[:, tb * 128:(tb + 1) * 128], vn[:, k, tb * 128:(tb + 1) * 128], identb[:], ["vn", "identb"], [f"ps{pbank}"])
                    cp("act", vtk[vb][:], pst[:, tb * 128:(tb + 1) * 128], [f"ps{pbank}"], [f"vtk{vb}"])
                    mm(ps[bank][:, tb * 128:(tb + 1) * 128], vtk[vb][:], wsT[:, k, :], True, True, [f"vtk{vb}", "wsT"], [f"ps{bank}"])
                b = k % 2
                tt("dve", mx[b][:].rearrange("p (a b) -> p a b", b=128), ps[bank][:, 0:TT].rearrange("p (a b) -> p a b", b=128),
                   bsb[:, k * 128:(k + 1) * 128].unsqueeze(1).broadcast_to([128, NTB_, 128]), ALU.add, [f"ps{bank}", "bsb"], [f"mx{b}"])
                tt("dve", yo[:, k, :], mx[b][:], uu[:, k, :], ALU.mult, [f"mx{b}", "uu"], ["yo"])
            stq(ycat[D + HYW:2 * D, :].rearrange("(k p) t -> p k t", p=128)[:, :, tsl], yo[:], ["yo"])
        S.barrier()

    def hyena_phase(l):
        HK = HYW // 128
        S.sb_reset()
        kb0 = S.sb("kb0", [128, HK], F32)
        base_mark = S.sbuf_off
        zp = S.sb("zp", [33, T], F32)
        ld(zp[:], zpos_d, ["zp"])
        w1 = S.sb("w1", [33, 64], F32)
        w2 = S.sb("w2", [64, 64], F32)
        w3 = S.sb("w3", [64, 64], F32)
        w4 = S.sb("w4", [64, 2 * HYW], F32)
        pp = S.sb("pp", [64, 4], F32)
        fb = S.sb("fb", [64, 3], F32)
        ld(w1[:], hy_w1[l], ["w1"])
        ld(w2[:], hy_w2[l], ["w2"])
        ld(w3[:], hy_w3[l], ["w3"])
        ld(w4[:], hy_w4[l], ["w4"])
        ld(pp[:], p64[l], ["pp"])
        ld(kb0[:], pcol[l, :, PO["hbias"]:PO["hbias"] + HK], ["kb0"])
        ts("dve", fb[:], pp[:, 0:3], pp[:, 3:4], None, ALU.mult, None, ["pp"], ["fb"])
        tng = S.sb("tng", [128, NCH], F32)
        ld(tng[:], tneg_d, ["tng"])
        dl = S.sb("dl", [128, HYW], F32)
        ld(dl[:], deltas_d[0, :].partition_broadcast(128), ["dl"])
        hbuf = [S.sb("hh", [64, T], F32) for _ in range(2)]
        arg = [S.sb("arg", [64, 512], F32) for _ in range(2)]
        ar2 = [S.sb("ar2", [64, 512], F32) for _ in range(2)]
        MAGIC = 12582912.0
        INV2PI = 1.0 / (2.0 * math.pi)
        PIS = 3.141592
        CW = min(512, T)
        cnt = 0
        cur, curk, curK = zp, "zp", 33
        for i, (W, wk) in enumerate(((w1, "w1"), (w2, "w2"), (w3, "w3"))):
            ob = hbuf[i % 2]
            for c0 in range(0, T, CW):
                b = cnt % 2
                bank = cnt % 2
                cnt += 1
                mm(ps[bank][0:64, 0:CW], W[0:curK, :], cur[0:curK, c0:c0 + CW], True, True, [wk, curk], [f"ps{bank}"])
                act(arg[b][:, 0:CW], ps[bank][0:64, 0:CW], AF.Identity, [f"ps{bank}", "pp", "fb"], [f"arg{b}"],
                    bias=fb[:, i:i + 1], scale=pp[:, 3:4])
                ts("dve", ar2[b][:, 0:CW], arg[b][:, 0:CW], INV2PI, MAGIC, ALU.mult, ALU.add, [f"arg{b}"], [f"ar2{b}"])
                ts("dve", ar2[b][:, 0:CW], ar2[b][:, 0:CW], -MAGIC, None, ALU.add, None, [f"ar2{b}"], [f"ar2{b}"])
                stt(arg[b][:, 0:CW], ar2[b][:, 0:CW], -2.0 * math.pi, arg[b][:, 0:CW], ALU.mult, ALU.add, [f"ar2{b}", f"arg{b}"], [f"arg{b}"])
                ts("dve", arg[b][:, 0:CW], arg[b][:, 0:CW], PIS, -PIS, ALU.min, ALU.max, [f"arg{b}"], [f"arg{b}"])
                act(ob[:, c0:c0 + CW], arg[b][:, 0:CW], AF.Sin, [f"arg{b}"], [f"hh{i % 2}"])
            cur, curk, curK = ob, f"hh{i % 2}", 64
        h3, h3k = cur, curk
        for k in range(HK):
            mm(ps[2][:, 2 * k:2 * k + 2], w4[:, k * 128:(k + 1) * 128], h3[:, 0:2], True, True, ["w4", h3k], ["ps2"])
        tt("dve", kb0[:], kb0[:], ps[2][:, 0:2 * HK].rearrange("p (k two) -> p k two", two=2)[:, :, 0], ALU.add, ["kb0", "ps2"], ["kb0"])
        winf = [S.sb("winf", [128, HYW], F32) for _ in range(2)]
        kt = [S.sb("kt", [128, 2 * HYW], BF16) for _ in range(2)]
        cnt = 0
        for pc in range(NCH):
            b = pc % 2
            act(winf[b][:], dl[:], AF.Exp, ["dl", "tng"], [f"winf{b}"], scale=tng[:, pc:pc + 1])
            if pc == 0:
                ts("dve", winf[b][:], winf[b][:], m0, None, ALU.mult, None, [f"winf{b}", "cst"], [f"winf{b}"])
            for half in range(2):
                for c0 in range(0, HYW, 512):
                    cw_ = min(512, HYW - c0)
                    bank = 3 + cnt % 4
                    cnt += 1
                    mm(ps[bank][:, 0:cw_], h3[:, pc * 128:(pc + 1) * 128], w4[:, half * HYW + c0:half * HYW + c0 + cw_], True, True,
                       [h3k, "w4"], [f"ps{bank}"])
                    tt("dve", kt[b][:, half * HYW + c0:half * HYW + c0 + cw_], ps[bank][:, 0:cw_], winf[b][:, c0:c0 + cw_], ALU.mult,
                       [f"ps{bank}", f"winf{b}"], [f"kt{b}"])
            stq(k_tok[pc * 128:(pc + 1) * 128, :], kt[b][:], [f"kt{b}"])
        S.barrier()

        S.sbuf_off = base_mark
        hw_ = S.sb("hw", [128, 12 * HK], F32)
        ld(hw_[:], pcol[l, :, PO["hcw0"]:PO["hcw0"] + 12 * HK], ["hw"])
        hin = [S.sb("hin", [128, 2, SEG + 2], BF16) for _ in range(3)]
        hacc = [S.sb("hacc", [128, 2, SEG], F32) for _ in range(3)]
        wbf = [S.sb("wbf", [128, T], BF16) for _ in range(2)]
        x0b = [S.sb("x0b", [128, T], BF16) for _ in range(2)]
        wtk = [S.sb("wtk", [128, 8, 128], BF16) for _ in range(2)]
        for p_ in range(3):
            mset("dve", hin[p_][:, 0, 0:1], 0.0, [], [f"hin{p_}"])
            mset("dve", hin[p_][:, 1, SEG + 1:SEG + 2], 0.0, [], [f"hin{p_}"])
        tcnt = 0
        for j in range(HK):
            b = j % 2
            for p_ in range(3):
                cc = p_ * HK + j
                ki, ka = f"hin{p_}", f"hacc{p_}"
                ld(hin[p_][:, :, 1:SEG + 1], phy[cc * 128:(cc + 1) * 128, :].rearrange("p (s t) -> p s t", s=2), [ki])
                ts("dve", hin[p_][:, 0, SEG + 1:SEG + 2], hin[p_][:, 1, 1:2], cplc[:, 0:1], None, ALU.mult, None, [ki, "cplc"], [ki])
                ts("dve", hin[p_][:, 1, 0:1], hin[p_][:, 0, SEG:SEG + 1], cplc[:, 0:1], None, ALU.mult, None, [ki, "cplc"], [ki])
                ts("dve", hacc[p_][:], hin[p_][:, :, 0:SEG], hw_[:, cc:cc + 1], hw_[:, 9 * HK + cc:9 * HK + cc + 1], ALU.mult, ALU.add,
                   [ki, "hw"], [ka])
                for jj in range(1, 3):
                    stt(hacc[p_][:], hin[p_][:, :, jj:jj + SEG], hw_[:, jj * 3 * HK + cc:jj * 3 * HK + cc + 1], hacc[p_][:],
                        ALU.mult, ALU.add, [ki, "hw", ka], [ka])
            tt("dve", wbf[b][:].rearrange("p (s t) -> p s t", s=2), hacc[2][:], hacc[1][:], ALU.mult, ["hacc2", "hacc1"], [f"wbf{b}"])
            cp("act", x0b[b][:].rearrange("p (s t) -> p s t", s=2), hacc[0][:], ["hacc0"], [f"x0b{b}"])
            stq(w_fm[j * 128:(j + 1) * 128, :], wbf[b][:], [f"wbf{b}"])
            stq(x0_fm[j * 128:(j + 1) * 128, :], x0b[b][:], [f"x0b{b}"])
            for t8 in range(0, NCH, 8):
                n8 = min(8, NCH - t8)
                bank = 2 + tcnt % 2
                kb_ = tcnt % 2
                tcnt += 1
                pst = ps[bank][:].bitcast(BF16)
                for i in range(n8):
                    tr(pst[:, i * 128:(i + 1) * 128], wbf[b][:, (t8 + i) * 128:(t8 + i + 1) * 128], identb[:],
                       [f"wbf{b}", "identb"], [f"ps{bank}"])
                cp("act", wtk[kb_][:, 0:n8, :], pst[:, 0:n8 * 128].rearrange("p (a b) -> p a b", b=128), [f"ps{bank}"], [f"wtk{kb_}"])
                stq(w_tokd[t8 * 128:(t8 + n8) * 128, j * 128:(j + 1) * 128].rearrange("(a p) c -> p a c", p=128), wtk[kb_][:, 0:n8, :], [f"wtk{kb_}"])
        S.barrier()

        S.sbuf_off = base_mark
        CB = 256 if HYW >= 256 else HYW
        NCB = HYW // CB
        seq = [S.sb("seq", [128, NF, CB], BF16) for _ in range(6)]
        fsl = [S.sb("fsl", [128, NF, 256], BF16) for _ in range(2)]
        pq = [S.sb("pq", [128, 2 * CB], F32) for _ in range(6)]
        pqc = [S.sb("pqc", [128, 2 * CB], F32) for _ in range(2)]
        kf_ = [S.sb("kf", [128, CB], F32) for _ in range(6)]
        ya = [S.sb("ya", [128, CB], F32) for _ in range(2)]
        Yt = [S.sb("Y", [128, NF, CB], BF16) for _ in range(4)]
        gsl = S.sb("gsl", [128, NF, 2, TBW], BF16)
        xw = [[S.sb("xw", [128, TBW], BF16) for _ in range(2)] for _ in range(2)]
        yf = [S.sb("yf", [128, TBW], F32) for _ in range(2)]
        yb_ = [S.sb("yb", [128, TBW], BF16) for _ in range(2)]
        fcnt = 0
        ocnt = 0
        for cb in range(NCB):
            c0 = cb * CB
            srcs = [w_tokd[0:SEG, c0:c0 + CB], w_tokd[SEG:2 * SEG, c0:c0 + CB],
                    k_tok[0:SEG, c0:c0 + CB], k_tok[SEG:2 * SEG, c0:c0 + CB],
                    k_tok[0:SEG, HYW + c0:HYW + c0 + CB], k_tok[SEG:2 * SEG, HYW + c0:HYW + c0 + CB]]
            for s_ in range(6):
                ld(seq[s_][:], srcs[s_].rearrange("(tc p) c -> p tc c", p=128), [f"seq{s_}"])
            for j in range(NF):
                fbuf = fcnt % 2
                fcnt += 1
                ld(fsl[fbuf][:], dftf_d[j], [f"fsl{fbuf}"])
                for s_ in range(6):
                    for part in range(2):
                        for tc in range(NF):
                            mm(ps[s_][:, part * CB:(part + 1) * CB], fsl[fbuf][:, tc, part * 128:(part + 1) * 128], seq[s_][:, tc, :],
                               tc == 0, tc == NF - 1, [f"fsl{fbuf}", f"seq{s_}"], [f"ps{s_}"])
                    cp("act" if s_ % 2 else "dve", pq[s_][:], ps[s_][:, 0:2 * CB], [f"ps{s_}"], [f"pq{s_}"])
                    if s_ < 2:
                        act(pqc[s_][:], ps[s_][:, 0:2 * CB], AF.Copy, [f"ps{s_}", "cplc"], [f"pqc{s_}"], scale=cplc[:, 0:1])
                P = lambda s_: pq[s_][:, 0:CB]
                Q = lambda s_: pq[s_][:, CB:2 * CB]
                KF = ["kf0", "kf1", "kf2", "kf3", "kf4", "kf5"]
                tt("dve", kf_[0][:], P(2), P(4), ALU.add, ["pq2", "pq4"], ["kf0"])
                tt("dve", kf_[1][:], Q(4), Q(2), ALU.subtract, ["pq2", "pq4"], ["kf1"])
                stt(kf_[2][:], Q(2), sgn, P(3), ALU.mult, ALU.add, ["pq2", "pq3", "cst"], ["kf2"])
                stt(kf_[3][:], P(2), sgn, Q(3), ALU.mult, ALU.subtract, ["pq2", "pq3", "cst"], ["kf3"])
                stt(kf_[4][:], Q(4), sgn, P(5), ALU.mult, ALU.add, ["pq4", "pq5", "cst"], ["kf4"])
                stt(kf_[5][:], P(4), sgn, Q(5), ALU.mult, ALU.subtract, ["pq4", "pq5", "cst"], ["kf5"])
                Pc = lambda s_: pqc[s_][:, 0:CB]
                Qc = lambda s_: pqc[s_][:, CB:2 * CB]

                def combo(dst, terms, rk):
                    first = True
                    for (sg_, a, b_) in terms:
                        if first:
                            tt("dve", ya[0][:], a, b_, ALU.mult, rk, ["ya0"])
                            if sg_ < 0:
                                ts("dve", ya[0][:], ya[0][:], -1.0, None, ALU.mult, None, ["ya0"], ["ya0"])
                            first = False
                        else:
                            tt("dve", ya[1][:], a, b_, ALU.mult, rk, ["ya1"])
                            tt("dve", ya[0][:], ya[0][:], ya[1][:], ALU.add if sg_ > 0 else ALU.subtract, ["ya0", "ya1"], ["ya0"])
                    cp("act", dst, ya[0][:], ["ya0"], ["Y"])
                rk = ["pq0", "pq1", "pqc0", "pqc1"] + KF
                combo(Yt[0][:, j, :], [(1, kf_[0][:], P(0)), (1, kf_[1][:], Q(0)), (1, kf_[4][:], Pc(1)), (-1, kf_[5][:], Qc(1))], rk)
                combo(Yt[1][:, j, :], [(1, kf_[0][:], Q(0)), (-1, kf_[1][:], P(0)), (1, kf_[4][:], Qc(1)), (1, kf_[5][:], Pc(1))], rk)
                combo(Yt[2][:, j, :], [(1, kf_[0][:], P(1)), (1, kf_[1][:], Q(1)), (1, kf_[2][:], Pc(0)), (1, kf_[3][:], Qc(0))], rk)
                combo(Yt[3][:, j, :], [(1, kf_[0][:], Q(1)), (-1, kf_[1][:], P(1)), (1, kf_[2][:], Qc(0)), (-1, kf_[3][:], Pc(0))], rk)
            for tb in range(NTB):
                ld(gsl[:], dfti_d[tb], ["gsl"])
                for sg_i in range(2):
                    for cc in range(CB // 128):
                        ob_ = ocnt % 2
                        bank = 6 + ocnt % 2
                        ocnt += 1
                        ch0 = c0 + cc * 128
                        tok0 = sg_i * SEG + tb * TBW
                        ld(xw[ob_][0][:], x0_fm[ch0:ch0 + 128, tok0:tok0 + TBW], [f"xw{ob_}0"])
                        ld(xw[ob_][1][:], w_fm[ch0:ch0 + 128, tok0:tok0 + TBW], [f"xw{ob_}1"])
                        for j in range(NF):
                            mm(ps[bank][:, 0:TBW], Yt[2 * sg_i][:, j, cc * 128:(cc + 1) * 128], gsl[:, j, 0, :], j == 0, False, ["Y", "gsl"], [f"ps{bank}"])
                            mm(ps[bank][:, 0:TBW], Yt[2 * sg_i + 1][:, j, cc * 128:(cc + 1) * 128], gsl[:, j, 1, :], False, j == NF - 1, ["Y", "gsl"], [f"ps{bank}"])
                        kch = ch0 // 128
                        stt(yf[ob_][:], xw[ob_][1][:], kb0[:, kch:kch + 1], ps[bank][:, 0:TBW], ALU.mult, ALU.add,
                            [f"xw{ob_}1", "kb0", f"ps{bank}"], [f"yf{ob_}"])
                        tt("dve", yb_[ob_][:], yf[ob_][:], xw[ob_][0][:], ALU.mult, [f"yf{ob_}", f"xw{ob_}0"], [f"yb{ob_}"])
                        stq(ycat[D + ch0:D + ch0 + 128, tok0:tok0 + TBW], yb_[ob_][:], [f"yb{ob_}"])
        S.barrier()

    def ssd_phase(l):
        H2 = 2 * NH
        QH = 4
        S.sb_reset()
        rowp = S.sb("rowp", [128, 4 * NH], F32)
        ld(rowp[:], prow[l, 0:4 * NH].partition_broadcast(128), ["rowp"])
        Abc = S.sb("Abc", [128, H2], F32)
        act(Abc[:], rowp[:, H2:2 * H2], AF.Exp, ["rowp"], ["Abc"])
        ts("dve", Abc[:], Abc[:], -1.0, None, ALU.mult, None, ["Abc"], ["Abc"])
        pcs = S.sb("pcs", [128, 2 * KD], F32)
        ld(pcs[:], pcol[l, :, PO["dskip"]:PO["dskip"] + 2 * KD], ["pcs"])
        neg4 = [S.sb("neg4", [128, QH, 128], F32) for _ in range(2)]
        for d in range(2):
            cp("dve", neg4[d][:], C_("negf" if d == 0 else "negb").unsqueeze(1).broadcast_to([128, QH, 128]), ["cst"], [f"neg4{d}"])
        tri = [C_("triU"), C_("triL")]
        dtr = S.sb("dtr", [128, 128], F32)
        uu = S.sb("su", [128, H2], F32)
        dt = S.sb("dt", [128, H2], F32)
        dtA = S.sb("dtA", [128, H2], F32)
        acs = S.sb("acs", [128, H2], F32)
        cd = S.sb("cd", [128, H2], F32)
        dend = S.sb("dend", [128, H2], F32)
        coef = S.sb("coef", [128, H2], F32)
        xt = S.sb("xt", [128, D], BF16)
        bt = S.sb("bt", [128, GN], BF16)
        xc = [S.sb("xc", [128, NH, 64], BF16) for _ in range(2)]
        xcd = [S.sb("xcd", [128, NH, 64], BF16) for _ in range(2)]
        mark = S.sbuf_off

        def prep(c, dirs):
            ld(dtr[0:H2, :], pdt[:, c * 128:(c + 1) * 128], ["dtr"])
            ld(xt[:], x_tok[c * 128:(c + 1) * 128, :], ["xt"])
            ld(bt[:], b_tok[c * 128:(c + 1) * 128, :], ["bt"])
            tr(ps[6][:, 0:H2], dtr[0:H2, :], C_("ident")[0:H2, 0:H2], ["dtr", "cst"], ["ps6"])
            tt("dve", uu[:], ps[6][:, 0:H2], rowp[:, 0:H2], ALU.add, ["ps6", "rowp"], ["su"])
            act(uu[:], uu[:], AF.Exp, ["su"], ["su"])
            act(dt[:], uu[:], AF.Ln, ["su"], ["dt"], bias=1.0)
            tt("dve", dtA[:], dt[:], Abc[:], ALU.mult, ["dt", "Abc"], ["dtA"])
            mm(ps[6][:, 128:128 + NH], tri[0], dtA[:, 0:NH], True, True, ["cst", "dtA"], ["ps6"])
            mm(ps[6][:, 128 + NH:128 + H2], tri[1], dtA[:, NH:H2], True, True, ["cst", "dtA"], ["ps6"])
            mm(ps[6][:, 256:256 + H2], C_("ones"), dtA[:], True, True, ["cst", "dtA"], ["ps6"])
            cp("dve", acs[:], ps[6][:, 128:128 + H2], ["ps6"], ["acs"])
            act(cd[:], ps[6][:, 256:256 + H2], AF.Exp, ["ps6"], ["cd"])
            tt("dve", dend[:], ps[6][:, 256:256 + H2], acs[:], ALU.subtract, ["ps6", "acs"], ["dend"])
            act(dend[:], dend[:], AF.Exp, ["dend"], ["dend"])
            tt("dve", coef[:], dt[:], dend[:], ALU.mult, ["dt", "dend"], ["coef"])
            xv = xt[:].rearrange("p (h q) -> p h q", q=64)
            for d in dirs:
                tt("pool", xcd[d][:], xv, coef[:, d * NH:(d + 1) * NH].unsqueeze(2).broadcast_to([128, NH, 64]), ALU.mult,
                   ["xt", "coef"], [f"xcd{d}"])
                if len(dirs) == 2:
                    tt("pool", xc[d][:], xv, dt[:, d * NH:(d + 1) * NH].unsqueeze(2).broadcast_to([128, NH, 64]), ALU.mult,
                       ["xt", "dt"], [f"xc{d}"])

        def upd(Hs, hk, d):
            for g in range(NG):
                mm(ps[7][:, 0:HG * 64], bt[:, g * 128:(g + 1) * 128], xcd[d][:, g * HG:(g + 1) * HG, :].rearrange("p h q -> p (h q)"),
                   True, True, ["bt", f"xcd{d}"], ["ps7"])
                hv = Hs[:, g * HG * 64:(g + 1) * HG * 64]
                tt("dve", hv.rearrange("p (h q) -> p h q", q=64), hv.rearrange("p (h q) -> p h q", q=64),
                   cd[:, d * NH + g * HG:d * NH + (g + 1) * HG].unsqueeze(2).broadcast_to([128, HG, 64]), ALU.mult, [hk, "cd"], [hk])
                tt("dve", hv, hv, ps[7][:, 0:HG * 64], ALU.add, [hk, "ps7"], [hk])

        Hb = S.sb("Hb", [128, D], F32)
        Hbb = [S.sb("Hbb", [128, D], BF16) for _ in range(2)]
        mset("dve", Hb[:], 0.0, [], ["Hb"])
        for c in reversed(range(NCH)):
            prep(c, [1])
            b = c % 2
            cp("act", Hbb[b][:], Hb[:], ["Hb"], [f"Hbb{b}"])
            stq(hb_d[c], Hbb[b][:], [f"Hbb{b}"])
            upd(Hb, "Hb", 1)
            if c == SCH:
                ts("dve", Hb[:], Hb[:], cplc[:, 0:1], None, ALU.mult, None, ["Hb", "cplc"], ["Hb"])
        S.barrier()

        S.sbuf_off = mark
        Hf = S.sb("Hf", [128, D], F32)
        Hfb = S.sb("Hfb", [128, D], BF16)
        hbt = S.sb("hbt", [128, D], BF16)
        bcT = S.sb("bcT", [128, 2 * NG, 128], BF16)
        xsz = S.sb("xsz", [128, 2, KD, 128], BF16)
        cbT = S.sb("cbT", [128, NG, 128], BF16)
        prod = [S.sb("prod", [128, QH, 128], F32) for _ in range(4)]
        E2 = [S.sb("E2", [128, QH, 128], BF16) for _ in range(2)]
        E1 = [S.sb("E1", [128, QH, 128], BF16) for _ in range(2)]
        NQ = HG // QH
        Mt = [[[S.sb("M", [128, QH, 128], BF16) for _ in range(NQ)] for _ in range(2)] for _ in range(2)]
        Cs = [[[S.sb("Cs", [128, QH, 128], BF16) for _ in range(NQ)] for _ in range(2)] for _ in range(2)]
        NP = HG // 2
        y1 = S.sb("y1", [128, NP, 128], F32)
        y2 = S.sb("y2", [128, NP, 128], F32)
        sqy = S.sb("sqy", [128, NP, 128], F32)
        rsn = S.sb("rsn", [128, 128], F32)
        SUP = 2 if NCH % 2 == 0 else 1
        yst = S.sb("yst", [128, KD, SUP * 128], BF16)
        mset("dve", Hf[:], 0.0, [], ["Hf"])
        qc = 0
        gc = 0
        for c in range(NCH):
            csl = slice(c * 128, (c + 1) * 128)
            prep(c, [0, 1])
            ld(hbt[:], hb_d[c], ["hbt"])
            cp("act", Hfb[:], Hf[:], ["Hf"], ["Hfb"])
            ld(bcT[:], xcT[D:D + 2 * GN, csl].rearrange("(g p) t -> p g t", p=128), ["bcT"])
            ld(xsz[:, 0], xcT[0:D, csl].rearrange("(k p) t -> p k t", p=128), ["xsz"])
            ld(xsz[:, 1], pz[:, csl].rearrange("(k p) t -> p k t", p=128), ["xsz"])
            for g in range(NG):
                mm(ps[7][:, 0:128], bcT[:, g, :], bcT[:, NG + g, :], True, True, ["bcT"], ["ps7"])
                cp("act", cbT[:, g, :], ps[7][:, 0:128], ["ps7"], ["cbT"])
            for g in range(NG):
                gb = gc % 2
                gc += 1
                for d in range(2):
                    for q in range(NQ):
                        h0 = d * NH + g * HG + q * QH
                        pb = qc % 4
                        ab = qc % 2
                        bA, bB = 2 * ab, 2 * ab + 1
                        qc += 1
                        tt("dve", prod[pb][:], tri[d].unsqueeze(1).broadcast_to([128, QH, 128]),
                           dtA[:, h0:h0 + QH].unsqueeze(2).broadcast_to([128, QH, 128]), ALU.mult, ["cst", "dtA"], [f"prod{pb}"])
                        pf = prod[pb][:].rearrange("p a b -> p (a b)")
                        mm(ps[bA][:, 0:QH * 128], C_("ones"), pf, True, True, ["cst", f"prod{pb}"], [f"ps{bA}"])
                        act(E2[ab][:].rearrange("p a b -> p (a b)"), ps[bA][:, 0:QH * 128], AF.Exp, [f"ps{bA}"], [f"E2{ab}"])
                        tt("pool", Cs[gb][d][q][:], E2[ab][:], bcT[:, NG + g, :].unsqueeze(1).broadcast_to([128, QH, 128]), ALU.mult,
                           [f"E2{ab}", "bcT"], [f"Cs{gb}{d}{q}"])
                        mm(ps[bB][:, 0:QH * 128], C_("ones"), pf, True, False, ["cst", f"prod{pb}"], [f"ps{bB}"])
                        mm(ps[bB][:, 0:QH * 128], C_("ident"), neg4[d][:].rearrange("p a b -> p (a b)"), False, False, ["cst", f"neg4{d}"], [f"ps{bB}"])
                        for i in range(QH):
                            mm(ps[bB][:, i * 128:(i + 1) * 128], prod[pb][:, i, :], C_("negones"), False, i == QH - 1,
                               ["cst", f"prod{pb}"], [f"ps{bB}"])
                        act(E1[ab][:].rearrange("p a b -> p (a b)"), ps[bB][:, 0:QH * 128], AF.Exp, [f"ps{bB}"], [f"E1{ab}"])
                        tt("dve", Mt[gb][d][q][:], E1[ab][:], cbT[:, g, :].unsqueeze(1).broadcast_to([128, QH, 128]), ALU.mult,
                           [f"E1{ab}", "cbT"], [f"M{gb}{d}{q}"])
                yb = 4 + gb
                rk = [f"M{gb}{d}{q}" for d in range(2) for q in range(NQ)] + [f"Cs{gb}{d}{q}" for d in range(2) for q in range(NQ)] + \
                     ["xc0", "xc1", "Hfb", "hbt"]
                for hp in range(NP):
                    for half in range(2):
                        hh = g * HG + hp * 2 + half
                        q, hi = (hp * 2 + half) // QH, (hp * 2 + half) % QH
                        out = ps[yb][half * 64:(half + 1) * 64, hp * 128:(hp + 1) * 128]
                        mm(out, xc[0][:, hh, :], Mt[gb][0][q][:, hi, :], True, False, rk, [f"ps{yb}"])
                        mm(out, Hfb[:, hh * 64:(hh + 1) * 64], Cs[gb][0][q][:, hi, :], False, False, rk, [f"ps{yb}"])
                        mm(out, xc[1][:, hh, :], Mt[gb][1][q][:, hi, :], False, False, rk, [f"ps{yb}"])
                        mm(out, hbt[:, hh * 64:(hh + 1) * 64], Cs[gb][1][q][:, hi, :], False, True, rk, [f"ps{yb}"])
                for hp in range(NP):
                    kch = g * NP + hp
                    stt(y1[:, hp, :], xsz[:, 0, kch, :], pcs[:, kch:kch + 1], ps[yb][:, hp * 128:(hp + 1) * 128], ALU.mult, ALU.add,
                        ["xsz", "pcs", f"ps{yb}"], ["y1"])
                tt("dve", y2[:], y1[:], xsz[:, 1, g * NP:(g + 1) * NP, :], ALU.mult, ["y1", "xsz"], ["y2"])
                act(sqy[:], y2[:], AF.Square, ["y2"], ["sqy"])
                for hp in range(NP):
                    mm(ps[6][:, 384:512], C_("ones"), sqy[:, hp, :], hp == 0, hp == NP - 1, ["cst", "sqy"], ["ps6"])
                ts("dve", rsn[:], ps[6][:, 384:512], 1.0 / (NP * 128), 1e-6, ALU.mult, ALU.add, ["ps6"], ["rsn"])
                act(rsn[:], rsn[:], AF.Sqrt, ["rsn"], ["rsn"])
                recip(rsn[:], rsn[:], ["rsn"], ["rsn"])
                for hp in range(NP):
                    kch = g * NP + hp
                    stt(yst[:, kch, (c % SUP) * 128:(c % SUP + 1) * 128], y2[:, hp, :], pcs[:, KD + kch:KD + kch + 1], rsn[:], ALU.mult, ALU.mult,
                        ["y2", "pcs", "rsn"], ["yst"])
            if c % SUP == SUP - 1:
                t0 = (c - SUP + 1) * 128
                stq(ycat[0:D, :].rearrange("(k p) t -> p k t", p=128)[:, :, t0:t0 + SUP * 128], yst[:], ["yst"])
            upd(Hf, "Hf", 0)
            if c == SCH - 1:
                ts("dve", Hf[:], Hf[:], cplc[:, 0:1], None, ALU.mult, None, ["Hf", "cplc"], ["Hf"])
        S.barrier()

    for l in range(L):
        norm_phase(l, "g1", hT)

        o_z, o_xbc, o_dt, o_hy, o_sg, o_gate = 0, D, D + XBC, D + XBC + 2 * NH, D + XBC + 2 * NH + 3 * HYW, \
            D + XBC + 2 * NH + 3 * HYW + 2 * SGW
        chunks = (chunks_of(o_z, o_xbc, "z") + chunks_of(o_xbc, o_dt, "xbc") + chunks_of(o_dt, o_hy, "dt") +
                  chunks_of(o_hy, o_sg, "hy") + chunks_of(o_sg, o_gate, "sg") + chunks_of(o_gate, NIN, "gate"))

        def win_setup(l=l):
            bg = S.sb("bg", [128, 3 * KD], F32)
            ld(bg[:], pcol[l, :, PO["bgate"]:PO["bgate"] + 3 * KD], ["bg"])
            ob = [S.sb("ob", [128, TT], BF16) for _ in range(3)]
            of = [S.sb("of", [128, TT], F32) for _ in range(2)]
            return bg, ob, of

        def win_handler(tag, c0, m, ti, pss, extra, cnt):
            bg, ob, of = extra
            pa, pk = pss[0]
            tsl = slice(ti * TT, (ti + 1) * TT)
            if tag == "dt":
                b = cnt % 2
                cp("act", of[b][0:m, :], pa, [pk], [f"of{b}"])
                stq(pdt[c0 - o_dt:c0 - o_dt + m, tsl], of[b][0:m, :], [f"of{b}"])
                return
            b = cnt % 3
            if tag == "z":
                act(ob[b][0:m, :], pa, AF.Silu, [pk], [f"ob{b}"])
                dst = pz[c0 - o_z:c0 - o_z + m, tsl]
            elif tag == "xbc":
                cp("dve", ob[b][0:m, :], pa, [pk], [f"ob{b}"])
                dst = pxbc[c0 - o_xbc:c0 - o_xbc + m, tsl]
            elif tag == "hy":
                cp("dve", ob[b][0:m, :], pa, [pk], [f"ob{b}"])
                dst = phy[c0 - o_hy:c0 - o_hy + m, tsl]
            elif tag == "sg":
                act(ob[b][0:m, :], pa, AF.Gelu, [pk], [f"ob{b}"])
                dst = psg[c0 - o_sg:c0 - o_sg + m, tsl]
            else:
                j = (c0 - o_gate) // 128
                act(ob[b][0:m, :], pa, AF.Sigmoid, [pk, "bg"], [f"ob{b}"], bias=bg[0:m, j:j + 1])
                dst = pgate[c0 - o_gate:c0 - o_gate + m, tsl]
            stq(dst, ob[b][0:m, :], [f"ob{b}"])

        gemm([(hT, wb_in[l])], chunks, 512, win_handler, win_setup)
        if "stop_win" in debug and l == 0:
            break

        S.sb_reset()
        XK = XBC // 128
        cw = S.sb("cw", [128, 6 * XK], F32)
        ld(cw[:], pcol[l, :, PO["scw0"]:PO["scw0"] + 6 * XK], ["cw"])
        cin = [S.sb("cin", [128, 2, SEG + 4], BF16) for _ in range(2)]
        cacc = [S.sb("cacc", [128, 2, SEG], F32) for _ in range(2)]
        cout = [S.sb("cout", [128, T], BF16) for _ in range(2)]
        ctk = [S.sb("ctk", [128, 8, 128], BF16) for _ in range(2)]
        for b in range(2):
            mset("dve", cin[b][:, 0, 0:2], 0.0, [], [f"cin{b}"])
            mset("dve", cin[b][:, 1, SEG + 2:SEG + 4], 0.0, [], [f"cin{b}"])
        tcnt = 0
        for c in range(XK):
            b = c % 2
            ld(cin[b][:, :, 2:SEG + 2], pxbc[c * 128:(c + 1) * 128, :].rearrange("p (s t) -> p s t", s=2), [f"cin{b}"])
            ts("dve", cin[b][:, 0, SEG + 2:SEG + 4], cin[b][:, 1, 2:4], cplc[:, 0:1], None, ALU.mult, None, [f"cin{b}", "cplc"], [f"cin{b}"])
            ts("dve", cin[b][:, 1, 0:2], cin[b][:, 0, SEG:SEG + 2], cplc[:, 0:1], None, ALU.mult, None, [f"cin{b}", "cplc"], [f"cin{b}"])
            ts("dve", cacc[b][:], cin[b][:, :, 0:SEG], cw[:, c:c + 1], cw[:, 5 * XK + c:5 * XK + c + 1], ALU.mult, ALU.add,
               [f"cin{b}", "cw"], [f"cacc{b}"])
            for j in range(1, 5):
                stt(cacc[b][:], cin[b][:, :, j:j + SEG], cw[:, j * XK + c:j * XK + c + 1], cacc[b][:], ALU.mult, ALU.add,
                    [f"cin{b}", "cw", f"cacc{b}"], [f"cacc{b}"])
            act(cout[b][:], cacc[b][:].rearrange("p s t -> p (s t)"), AF.Silu, [f"cacc{b}"], [f"cout{b}"])
            stq(xcT[c * 128:(c + 1) * 128, :], cout[b][:], [f"cout{b}"])
            if c < KD + GN // 128:
                for t8 in range(0, NCH, 8):
                    n8 = min(8, NCH - t8)
                    bank = 2 + tcnt % 2
                    kb = tcnt % 2
                    tcnt += 1
                    pst = ps[bank][:].bitcast(BF16)
                    for i in range(n8):
                        tr(pst[:, i * 128:(i + 1) * 128], cout[b][:, (t8 + i) * 128:(t8 + i + 1) * 128], identb[:],
                           [f"cout{b}", "identb"], [f"ps{bank}"])
                    cp("act", ctk[kb][:, 0:n8, :], pst[:, 0:n8 * 128].rearrange("p (a b) -> p a b", b=128), [f"ps{bank}"], [f"ctk{kb}"])
                    if c < KD:
                        dst = x_tok[t8 * 128:(t8 + n8) * 128, c * 128:(c + 1) * 128]
                    else:
                        dst = b_tok[t8 * 128:(t8 + n8) * 128, (c - KD) * 128:(c - KD + 1) * 128]
                    stq(dst.rearrange("(a p) c -> p a c", p=128), ctk[kb][:, 0:n8, :], [f"ctk{kb}"])
        S.barrier()

        ssd_phase(l)
        hyena_phase(l)
        sgu_phase(l)

        def wbr_setup():
            gt = [[S.sb("gt", [128, TT], BF16) for _ in range(3)] for _ in range(2)]
            t3 = [S.sb("t3", [128, TT], F32) for _ in range(3)]
            ob = [S.sb("mo", [128, TT], BF16) for _ in range(2)]
            return gt, t3, ob

        def wbr_handler(tag, c0, m, ti, pss, extra, cnt):
            gt, t3, ob = extra
            b = cnt % 2
            tsl = slice(ti * TT, (ti + 1) * TT)
            for g in range(3):
                ld(gt[b][g][0:m, :], pgate[g * D + c0:g * D + c0 + m, tsl], [f"gt{b}_{g}"])
            for g in range(3):
                tt("dve", t3[g][0:m, :], pss[g][0], gt[b][g][0:m, :], ALU.mult, [pss[g][1], f"gt{b}_{g}"], [f"t3_{g}"])
            tt("dve", t3[0][0:m, :], t3[0][0:m, :], t3[1][0:m, :], ALU.add, ["t3_0", "t3_1"], ["t3_0"])
            tt("dve", ob[b][0:m, :], t3[0][0:m, :], t3[2][0:m, :], ALU.add, ["t3_0", "t3_2"], [f"mo{b}"])
            stq(mergedT[c0:c0 + m, tsl], ob[b][0:m, :], [f"mo{b}"])

        gemm([(ycat[0:D, :], wrows(wb_br[l], 0, D)), (ycat[D:D + HYW, :], wrows(wb_br[l], D, D + HYW)),
              (ycat[D + HYW:2 * D, :], wrows(wb_br[l], D + HYW, 2 * D))], chunks_of(0, D, "m"), 256, wbr_handler, wbr_setup)

        def res_setup():
            xr = [S.sb("xr", [128, TT], F32) for _ in range(3)]
            return xr

        def res_handler(tag, c0, m, ti, pss, extra, cnt):
            xr = extra
            b = cnt % 3
            tsl = slice(ti * TT, (ti + 1) * TT)
            ld(xr[b][0:m, :], xT[c0:c0 + m, tsl], [f"xr{b}"])
            tt("dve", xr[b][0:m, :], pss[0][0], xr[b][0:m, :], ALU.add, [pss[0][1], f"xr{b}"], [f"xr{b}"])
            stq(xT[c0:c0 + m, tsl], xr[b][0:m, :], [f"xr{b}"])

        gemm([(mergedT, wb_out[l])], chunks_of(0, D, "r"), 512, res_handler, res_setup)

        norm_phase(l, "g2", hT)

        def up_setup():
            return [S.sb("uo", [128, TT], BF16) for _ in range(3)]

        def up_handler(tag, c0, m, ti, pss, extra, cnt):
            b = cnt % 3
            eng = "act" if cnt % 2 else "dve"
            cp(eng, extra[b][0:m, :], pss[0][0], [pss[0][1]], [f"uo{b}"])
            stq(upT[c0:c0 + m, ti * TT:(ti + 1) * TT], extra[b][0:m, :], [f"uo{b}"])

        gemm([(hT, wb_up[l])], chunks_of(0, 2 * DFF, "u"), 512, up_handler, up_setup)

        S.sb_reset()
        FK = DFF // 128
        fw = S.sb("fw", [128, 8 * FK], F32)
        ld(fw[:], pcol[l, :, PO["fcw0"]:PO["fcw0"] + 8 * FK], ["fw"])
        fin = [[S.sb("fin", [128, 2, SEG + 2], BF16) for _ in range(2)] for _ in range(2)]
        facc = [[S.sb("facc", [128, 2, SEG], F32) for _ in range(2)] for _ in range(2)]
        fsg = [S.sb("fsg", [128, 2, SEG], F32) for _ in range(2)]
        fo = [S.sb("fo", [128, T], BF16) for _ in range(2)]
        for b in range(2):
            for h in range(2):
                mset("dve", fin[b][h][:, 0, 0:1], 0.0, [], [f"fin{b}{h}"])
                mset("dve", fin[b][h][:, 1, SEG + 1:SEG + 2], 0.0, [], [f"fin{b}{h}"])
        for c in range(FK):
            b = c % 2
            for h in range(2):
                cc = c + h * FK
                k_in, k_acc = f"fin{b}{h}", f"facc{b}{h}"
                ld(fin[b][h][:, :, 1:SEG + 1], upT[cc * 128:(cc + 1) * 128, :].rearrange("p (s t) -> p s t", s=2), [k_in])
                ts("dve", fin[b][h][:, 0, SEG + 1:SEG + 2], fin[b][h][:, 1, 1:2], cplc[:, 0:1], None, ALU.mult, None, [k_in, "cplc"], [k_in])
                ts("dve", fin[b][h][:, 1, 0:1], fin[b][h][:, 0, SEG:SEG + 1], cplc[:, 0:1], None, ALU.mult, None, [k_in, "cplc"], [k_in])
                ts("dve", facc[b][h][:], fin[b][h][:, :, 0:SEG], fw[:, cc:cc + 1], fw[:, 6 * FK + cc:6 * FK + cc + 1], ALU.mult, ALU.add,
                   [k_in, "fw"], [k_acc])
                for j in range(1, 3):
                    stt(facc[b][h][:], fin[b][h][:, :, j:j + SEG], fw[:, j * 2 * FK + cc:j * 2 * FK + cc + 1], facc[b][h][:],
                        ALU.mult, ALU.add, [k_in, "fw", k_acc], [k_acc])
            act(fsg[b][:], facc[b][0][:], AF.Silu, [f"facc{b}0"], [f"fsg{b}"])
            tt("dve", fo[b][:].rearrange("p (s t) -> p s t", s=2), fsg[b][:], facc[b][1][:], ALU.mult, [f"fsg{b}", f"facc{b}1"], [f"fo{b}"])
            stq(actT[c * 128:(c + 1) * 128, :], fo[b][:], [f"fo{b}"])
        S.barrier()

        gemm([(actT, wb_down[l])], chunks_of(0, D, "r"), 256, res_handler, res_setup)

    if "stop_win" not in debug:
        norm_phase(0, "gf", None, final=True)
    S.emit()
    return nc


def _col(v):
    v = np.asarray(v, np.float32)
    return v.reshape(-1, 128).T


def prep_inputs(cfg, inp, n_prompt=8, n_sample=2):
    D, SEG, T, L, NH = cfg["D"], cfg["SEG"], cfg["T"], cfg["L"], cfg["NH"]
    PO = cfg["pcol_off"]
    f = lambda k: np.asarray(inp[k], np.float32)
    pcol = np.zeros((L, 128, cfg["NCOL"]), np.float32)
    prow = np.zeros((L, cfg["NROW"]), np.float32)
    p64 = np.zeros((L, 64, 4), np.float32)

    def put(l, name, v):
        c = _col(v)
        pcol[l, :, PO[name]:PO[name] + c.shape[1]] = c

    for l in range(L):
        put(l, "g1", f("norm1_g")[l])
        put(l, "g2", f("norm2_g")[l])
        put(l, "bgate", f("b_gate")[l])
        for j in range(5):
            put(l, f"scw{j}", f("ssm_conv_w")[l, j])
        put(l, "scb", f("ssm_conv_b")[l])
        put(l, "dskip", np.repeat(f("ssm_d")[l], 64))
        put(l, "ng", f("ssm_norm_g")[l])
        for j in range(3):
            put(l, f"hcw{j}", f("hy_conv_w")[l, j])
        put(l, "hcb", f("hy_conv_b")[l])
        put(l, "hbias", f("hy_bias")[l])
        put(l, "lng", f("sg_ln_g")[l])
        put(l, "lnb", f("sg_ln_b")[l])
        for j in range(3):
            put(l, f"fcw{j}", f("ffn_conv_w")[l, j])
        put(l, "fcb", f("ffn_conv_b")[l])
        put(l, "gf", f("normf_g"))
        prow[l, 0:2 * NH] = f("ssm_dt_bias")[l].reshape(-1)
        prow[l, 2 * NH:4 * NH] = f("ssm_a_log")[l].reshape(-1)
        prow[l, 4 * NH:] = f("sg_bs")[l].reshape(-1)
        p64[l, :, 0] = f("hy_b1")[l]
        p64[l, :, 1] = f("hy_b2")[l]
        p64[l, :, 2] = f("hy_b3")[l]
        p64[l, :, 3] = f("hy_freq")[l]
    hc = host_consts(cfg)
    shared = dict(w_in=f("w_in"), w_br=f("w_br"), w_out=f("w_out"), w_up=f("w_up"), w_down=f("w_down"),
                  hy_w1=f("hy_w1"), hy_w2=f("hy_w2"), hy_w3=f("hy_w3"), hy_w4=f("hy_w4"), p64=p64,
                  sg_ws=f("sg_ws"), pcol=pcol, prow=prow, cst=hc["cst"], deltas=hc["deltas"],
                  dftf=hc["dftf"], dfti=hc["dfti"])
    xp, xs = f("x_prompt"), f("x_sample")
    zp_s, tn_s = host_zpos(2 * SEG, T)
    zp_p, tn_p = host_zpos(SEG, T)
    plan = []
    for i in range(n_sample):
        plan.append(("s", i))
    rest = 8 - n_sample
    npair = n_prompt - rest
    pi = 0
    for c in range(rest):
        if c < npair:
            plan.append(("pp", pi, pi + 1))
            pi += 2
        else:
            plan.append(("p", pi))
            pi += 1
    maps = []
    for pl in plan:
        m = dict(shared)
        if pl[0] == "s":
            m["x_in"] = np.ascontiguousarray(xs[pl[1]])
            m["cpl"] = np.ones((128, 1), np.float32)
            m["zposT"], m["tneg"] = zp_s, tn_s
        else:
            x = np.zeros((T, D), np.float32)
            x[0:SEG] = xp[pl[1]]
            if pl[0] == "pp":
                x[SEG:] = xp[pl[2]]
            m["x_in"] = x
            m["cpl"] = np.zeros((128, 1), np.float32)
            m["zposT"], m["tneg"] = zp_p, tn_p
        maps.append(m)
    return maps, plan


def assemble(cfg, plan, results, n_prompt=8, n_sample=2, key="y_out"):
    D, SEG, T = cfg["D"], cfg["SEG"], cfg["T"]
    yp = np.zeros((n_prompt, SEG, D), np.float32)
    ys = np.zeros((n_sample, T, D), np.float32)
    for pl, r in zip(plan, results):
        y = np.asarray(r[key], np.float32)
        if pl[0] == "s":
            ys[pl[1]] = y
        else:
            yp[pl[1]] = y[0:SEG]
            if pl[0] == "pp":
                yp[pl[2]] = y[SEG:]
    return yp, ys


def kernel(**inputs):
    cfg = make_cfg()
    nc = build(cfg)
    maps, plan = prep_inputs(cfg, inputs)
    res = run_bass_kernel_spmd(nc, maps, core_ids=list(range(8)))
    yp, ys = assemble(cfg, plan, res.results)
    return (yp, ys)
```

```python
import math
import contextlib
import numpy as np
import ml_dtypes
import concourse.bass as bass
import concourse.mybir as mybir
from concourse.bass_utils import run_bass_kernel_spmd

F32 = mybir.dt.float32
BF16 = mybir.dt.bfloat16
AF = mybir.ActivationFunctionType
ALU = mybir.AluOpType

ENGS = ("pe", "act", "dve", "pool", "sp")
NDMASEM = 12
NEGBIG = -30000.0


class Op:
    __slots__ = ("eng", "fn", "dma", "waits", "inc", "dsem", "dval", "idx", "prewait", "dns")

    def __init__(self, eng, fn, dma):
        self.eng = eng
        self.fn = fn
        self.dma = dma
        self.waits = []
        self.inc = False
        self.dsem = None
        self.dval = 0
        self.prewait = None
        self.idx = 0
        self.dns = eng


class Sched:
    def __init__(self, nc):
        self.nc = nc
        self.ops = {e: [] for e in ENGS}
        self.last_w = {}
        self.readers = {}
        self.dma_rr = {e: 0 for e in ENGS}
        self.dma_cum = {}
        self.sbuf_off = 16640
        self.sbuf_base = 16640
        self.nbar = 0
        self.maxbar = 10 ** 9
        self.nalloc = 0

    def sb(self, name, shape, dtype):
        esz = 2 if dtype == BF16 else 4
        n = 1
        for s in shape[1:]:
            n *= s
        nbytes = (n * esz + 63) // 64 * 64
        off = self.sbuf_off
        self.sbuf_off += nbytes
        assert self.sbuf_off <= 229312, ("SBUF overflow", name, self.sbuf_off)
        self.nalloc += 1
        return self.nc.alloc_sbuf_tensor_at(f"{name}_{self.nalloc}", list(shape), dtype, offset=off)

    def sb_reset(self):
        self.sbuf_off = self.sbuf_base

    def _dep(self, op, dep):
        if dep is None or dep is op:
            return
        if dep.dma:
            op.waits.append((("d", dep.dns, dep.dsem), dep.dval))
        else:
            if dep.eng == op.eng and not op.dma and dep.eng == "pe":
                return
            dep.inc = True
            op.waits.append((("e", dep.eng), dep))

    def add(self, eng, fn, reads=(), writes=(), dma=False, ns=None):
        op = Op(eng, fn, dma)
        if ns:
            op.dns = eng + ":" + ns
        if self.nbar >= self.maxbar:
            return op
        for k in reads:
            self._dep(op, self.last_w.get(k))
            if k.startswith("ps"):
                for r in self.readers.get(k, ()):
                    if r.eng != eng:
                        self._dep(op, r)
        for k in writes:
            self._dep(op, self.last_w.get(k))
            for r in self.readers.get(k, ()):
                self._dep(op, r)
        for k in reads:
            self.readers.setdefault(k, []).append(op)
        for k in writes:
            self.last_w[k] = op
            self.readers[k] = []
        if dma:
            slot = self.dma_rr.get(op.dns, 0) % NDMASEM
            self.dma_rr[op.dns] = self.dma_rr.get(op.dns, 0) + 1
            key = (op.dns, slot)
            prev = self.dma_cum.get(key, 0)
            if prev:
                op.prewait = (("d", op.dns, slot), prev)
            op.dsem = slot
            op.dval = prev + 16
            self.dma_cum[key] = op.dval
        self.ops[eng].append(op)
        return op

    def barrier(self):
        if self.nbar >= self.maxbar:
            return
        self.nbar += 1
        nb = self.nbar
        op = Op("sp", ("bar_signal", nb), False)
        for e in ENGS:
            if e == "sp":
                continue
            for o in reversed(self.ops[e]):
                if not o.dma and not isinstance(o.fn, tuple):
                    o.inc = True
                    op.waits.append((("e", e), o))
                    break
        op.waits.extend([(("d", e, s), v) for (e, s), v in self.dma_cum.items()])
        self.ops["sp"].append(op)
        for e in ENGS:
            if e != "sp":
                self.ops[e].append(Op(e, ("bar_wait", nb), False))
        self.last_w = {}
        self.readers = {}

    def emit(self):
        nc = self.nc
        with contextlib.ExitStack() as st:
            esem = {e: st.enter_context(nc.semaphore(f"s_{e}")) for e in ENGS}
            dsem = {}
            for (dns_, s_) in sorted(self.dma_cum.keys()):
                dsem[(dns_, s_)] = st.enter_context(nc.semaphore(f"d_{dns_.replace(':', '_')}_{s_}"))
            bsem = st.enter_context(nc.semaphore("s_bar"))
            for e in ENGS:
                c = 0
                for o in self.ops[e]:
                    if o.inc and not o.dma:
                        c += 1
                        o.idx = c
            block = st.enter_context(nc.Block())

            def run(e, eng):
                seen = {}
                for o in self.ops[e]:
                    waits = list(o.waits)
                    if o.prewait:
                        waits.append(o.prewait)
                    best = {}
                    for k, v in waits:
                        if not isinstance(v, int):
                            v = v.idx
                        if v > best.get(k, 0):
                            best[k] = v
                    for k, v in best.items():
                        if seen.get(k, 0) >= v:
                            continue
                        seen[k] = v
                        if k[0] == "e":
                            eng.wait_ge(esem[k[1]], v)
                        else:
                            eng.wait_ge(dsem[(k[1], k[2])], v)
                    if isinstance(o.fn, tuple):
                        if o.fn[0] == "bar_signal":
                            eng.sem_inc(bsem, 1)
                        else:
                            eng.wait_ge(bsem, o.fn[1])
                        continue
                    ins = o.fn(eng)
                    if o.dma:
                        ins.then_inc(dsem[(o.dns, o.dsem)], 16)
                    elif o.inc:
                        ins.then_inc(esem[e], 1)

            block.tensor(lambda eng: run("pe", eng))
            block.scalar(lambda eng: run("act", eng))
            block.vector(lambda eng: run("dve", eng))
            block.gpsimd(lambda eng: run("pool", eng))
            block.sync(lambda eng: run("sp", eng))


def make_cfg(D=4096, SEG=2048, L=2, NG=8, DFF=11008):
    c = dict(D=D, SEG=SEG, T=2 * SEG, L=L, NG=NG, DFF=DFF)
    c["NH"] = D // 64
    c["HG"] = c["NH"] // NG
    c["GN"] = NG * 128
    c["XBC"] = D + 2 * c["GN"]
    c["HYW"] = D // 2
    c["SGW"] = D // 2
    c["SGG"] = c["SGW"] // 128
    c["NIN"] = D + c["XBC"] + 2 * c["NH"] + 3 * c["HYW"] + 2 * c["SGW"] + 3 * D
    off = {}
    n = 0
    for name, w in [("g1", D), ("g2", D), ("bgate", 3 * D)] + [(f"scw{j}", c["XBC"]) for j in range(5)] + \
            [("scb", c["XBC"]), ("dskip", D), ("ng", D)] + [(f"hcw{j}", 3 * c["HYW"]) for j in range(3)] + \
            [("hcb", 3 * c["HYW"]), ("hbias", c["HYW"]), ("lng", c["SGW"]), ("lnb", c["SGW"])] + \
            [(f"fcw{j}", 2 * DFF) for j in range(3)] + [("fcb", 2 * DFF), ("gf", D)]:
        off[name] = n
        n += w // 128
    c["pcol_off"] = off
    c["NCOL"] = n
    c["NROW"] = 4 * c["NH"] + c["SGG"] * 128
    return c


CST_NAMES = ["ident", "triU", "triL", "negf", "negb", "ones", "negones"]


def host_consts(cfg):
    SEG = cfg["SEG"]
    k = np.arange(128)[:, None]
    l = np.arange(128)[None, :]
    blocks = {
        "ident": (k == l), "triU": (k <= l), "triL": (k >= l),
        "negf": np.where(l >= k, 0.0, NEGBIG), "negb": np.where(l <= k, 0.0, NEGBIG),
        "ones": np.ones((128, 128)), "negones": -np.ones((128, 128)),
    }
    cst = np.concatenate([np.asarray(blocks[n], np.float32) for n in CST_NAMES] +
                         [np.where(np.arange(128) % 2 == 0, 1.0, -1.0).astype(np.float32)[:, None],
                          (np.arange(128) > 0).astype(np.float32)[:, None]], axis=1)
    N = 2 * SEG
    tau = np.arange(SEG, dtype=np.float64)
    f = np.arange(SEG, dtype=np.float64)
    ang = 2.0 * np.pi * np.outer(tau, f + 0.5) / N
    C = np.cos(ang)
    Sn = np.sin(ang)
    NF = SEG // 128
    fw = np.zeros((NF, 128, NF, 256), np.float32)
    Cr = C.reshape(NF, 128, NF, 128)
    Sr = Sn.reshape(NF, 128, NF, 128)
    fw[:, :, :, 0:128] = Cr.transpose(2, 1, 0, 3)
    fw[:, :, :, 128:256] = Sr.transpose(2, 1, 0, 3)
    TBW = min(512, SEG)
    NTB = SEG // TBW
    iv = np.zeros((NTB, 128, NF, 2, TBW), np.float32)
    Ct = (2.0 / N) * C.T.reshape(NF, 128, NTB, TBW)
    St = (2.0 / N) * Sn.T.reshape(NF, 128, NTB, TBW)
    iv[:, :, :, 0, :] = Ct.transpose(2, 1, 0, 3)
    iv[:, :, :, 1, :] = St.transpose(2, 1, 0, 3)
    HYW = cfg["HYW"]
    min_decay = math.log(1e-2) / 1.5
    max_decay = math.log(1e-2) / 0.3
    deltas = np.abs(np.linspace(min_decay, max_decay, HYW, dtype=np.float32)).astype(np.float32)[None, :]
    return dict(cst=cst, dftf=fw.astype(ml_dtypes.bfloat16), dfti=iv.astype(ml_dtypes.bfloat16), deltas=deltas)


def host_zpos(n, TF):
    t = np.linspace(0.0, 1.0, n, dtype=np.float32)[:, None]
    bands = 16
    wpos = (np.float32(2.0 * math.pi / n) * np.arange(n, dtype=np.float32))[:, None]
    fr = np.linspace(1e-4, bands - 1, bands, dtype=np.float32)[None, :]
    z = np.concatenate([t, np.cos(fr * wpos), -np.sin(fr * wpos)], axis=-1).astype(np.float32)
    zp = np.zeros((TF, 33), np.float32)
    zp[:n] = z
    tt = np.zeros((TF,), np.float32)
    tt[:n] = t[:, 0]
    tneg = (-tt).reshape(TF // 128, 128).T.copy()
    return zp.T.copy(), tneg


def build(cfg, debug=()):
    D, SEG, T, L, NG, DFF = (cfg[k] for k in ("D", "SEG", "T", "L", "NG", "DFF"))
    NH, HG, GN, XBC, HYW, SGW, SGG, NIN = (cfg[k] for k in ("NH", "HG", "GN", "XBC", "HYW", "SGW", "SGG", "NIN"))
    NCOL, NROW, PO = cfg["NCOL"], cfg["NROW"], cfg["pcol_off"]
    KD = D // 128
    NCH = T // 128
    SCH = SEG // 128
    TT = min(512, SEG)
    NTT = T // TT
    NF = SEG // 128
    TBW = min(512, SEG)
    NTB = SEG // TBW
    NCST = 128 * len(CST_NAMES) + 2

    nc = bass.Bass("TRN2", target_bir_lowering=False)
    S = Sched(nc)
    for d_ in debug:
        if d_.startswith("maxbar="):
            S.maxbar = int(d_[7:])

    def din(name, shape, dt=F32):
        return nc.dram_tensor(name, list(shape), dt, kind="ExternalInput").ap()

    def dscr(name, shape, dt=BF16):
        kind = "ExternalOutput" if name in debug else "Internal"
        return nc.dram_tensor(name, list(shape), dt, kind=kind).ap()

    x_in = din("x_in", [T, D])
    w_in = din("w_in", [L, D, NIN])
    w_br = din("w_br", [L, 2 * D, D])
    w_out = din("w_out", [L, D, D])
    w_up = din("w_up", [L, D, 2 * DFF])
    w_down = din("w_down", [L, DFF, D])
    hy_w1 = din("hy_w1", [L, 33, 64])
    hy_w2 = din("hy_w2", [L, 64, 64])
    hy_w3 = din("hy_w3", [L, 64, 64])
    hy_w4 = din("hy_w4", [L, 64, 2 * HYW])
    p64 = din("p64", [L, 64, 4])
    sg_ws = din("sg_ws", [L, SGG, 128, 128])
    pcol = din("pcol", [L, 128, NCOL])
    prow = din("prow", [L, NROW])
    cst_d = din("cst", [128, NCST])
    cpl_d = din("cpl", [128, 1])
    zpos_d = din("zposT", [33, T])
    tneg_d = din("tneg", [128, T // 128])
    deltas_d = din("deltas", [1, HYW])
    dftf_d = din("dftf", [NF, 128, NF, 256], BF16)
    dfti_d = din("dfti", [NTB, 128, NF, 2, TBW], BF16)
    y_out = nc.dram_tensor("y_out", [T, D], F32, kind="ExternalOutput").ap()

    def bigw(name, K_, N_):
        out = []
        for l_ in range(L):
            maxrows = max(128, (200 * 2 ** 20 // (N_ * 2)) // 128 * 128)
            pieces = []
            r0 = 0
            while r0 < K_:
                r1 = min(K_, r0 + maxrows)
                pieces.append((r0, r1, dscr(f"{name}_{l_}_{r0}", [r1 - r0, N_])))
                r0 = r1
            out.append(pieces)
        return out

    wb_in = bigw("wb_in", D, NIN)
    wb_br = bigw("wb_br", 2 * D, D)
    wb_out = bigw("wb_out", D, D)
    wb_up = bigw("wb_up", D, 2 * DFF)
    wb_down = bigw("wb_down", DFF, D)

    def wrows(pieces, a, b):
        out = []
        for (r0, r1, ap) in pieces:
            lo, hi = max(a, r0), min(b, r1)
            if lo < hi:
                out.append((lo - a, hi - a, ap[lo - r0:hi - r0, :]))
        return out
    xT = dscr("xT", [D, T], F32)
    hT = dscr("hT", [D, T])
    pz = dscr("pz", [D, T])
    pxbc = dscr("pxbc", [XBC, T])
    pdt = dscr("pdt", [2 * NH, T], F32)
    phy = dscr("phy", [3 * HYW, T])
    psg = dscr("psg", [2 * SGW, T])
    pgate = dscr("pgate", [3 * D, T])
    xcT = dscr("xcT", [XBC, T])
    x_tok = dscr("x_tok", [T, D])
    b_tok = dscr("b_tok", [T, GN])
    hb_d = dscr("hb_d", [NCH, 128, D])
    ycat = dscr("ycat", [2 * D, T])
    k_tok = dscr("k_tok", [T, 2 * HYW])
    w_tokd = dscr("w_tokd", [T, HYW])
    w_fm = dscr("w_fm", [HYW, T])
    x0_fm = dscr("x0_fm", [HYW, T])
    mergedT = dscr("mergedT", [D, T])
    upT = dscr("upT", [2 * DFF, T])
    actT = dscr("actT", [DFF, T])

    ps = [nc.alloc_psum_tensor(f"psb{i}", [128, 512], F32) for i in range(8)]

    def mm(out, lhsT, rhs, start, stop, r, w):
        S.add("pe", lambda e: e.matmul(out, lhsT=lhsT, rhs=rhs, start=start, stop=stop), r, w)

    def tr(out, in_, ident, r, w):
        S.add("pe", lambda e: e.transpose(out=out, in_=in_, identity=ident), r, w)

    def act(out, in_, func, r, w, bias=None, scale=None):
        kw = {}
        if bias is not None:
            kw["bias"] = bias
        if scale is not None:
            kw["scale"] = scale
        S.add("act", lambda e: e.activation(out=out, in_=in_, func=func, **kw), r, w)

    def tt(eng, out, in0, in1, op, r, w):
        S.add(eng, lambda e: e.tensor_tensor(out=out, in0=in0, in1=in1, op=op), r, w)

    def ts(eng, out, in0, s1, s2, op0, op1, r, w):
        if op1 is None:
            S.add(eng, lambda e: e.tensor_scalar(out=out, in0=in0, scalar1=s1, scalar2=None, op0=op0), r, w)
        else:
            S.add(eng, lambda e: e.tensor_scalar(out=out, in0=in0, scalar1=s1, scalar2=s2, op0=op0, op1=op1), r, w)

    def stt(out, in0, scalar, in1, op0, op1, r, w):
        S.add("dve", lambda e: e.scalar_tensor_tensor(out=out, in0=in0, scalar=scalar, in1=in1, op0=op0, op1=op1), r, w)

    def cp(eng, out, in_, r, w):
        if eng == "act":
            S.add("act", lambda e: e.copy(out=out, in_=in_), r, w)
        else:
            S.add(eng, lambda e: e.tensor_copy(out=out, in_=in_), r, w)

    def recip(out, in_, r, w):
        S.add("dve", lambda e: e.reciprocal(out=out, in_=in_), r, w)

    def mset(eng, ap, v, r, w):
        S.add(eng, lambda e: e.memset(ap, v), r, w)

    def ld(out, in_, w, r=()):
        S.add("sp", lambda e: e.dma_start(out=out, in_=in_), r, w, dma=True)

    def stq(out, in_, r, w=()):
        S.add("pool", lambda e: e.dma_start(out=out, in_=in_), r, w, dma=True)

    cst = S.sb("cst", [128, NCST], F32)
    cplc = S.sb("cplc", [128, 1], F32)
    identb = S.sb("identb", [128, 128], BF16)
    onesb = S.sb("onesb", [128, 128], BF16)
    ld(cst[:], cst_d, ["cst"])
    ld(cplc[:], cpl_d, ["cplc"])

    def C_(name):
        i = CST_NAMES.index(name)
        return cst[:, i * 128:(i + 1) * 128]
    sgn = cst[:, NCST - 2:NCST - 1]
    m0 = cst[:, NCST - 1:NCST]
    cp("dve", identb[:], C_("ident"), ["cst"], ["identb"])
    cp("dve", onesb[:], C_("ones"), ["cst"], ["onesb"])
    S.sbuf_base = S.sbuf_off
    CK = ["cst", "cplc", "identb", "onesb"]

    def rearm():
        pass

    cast_groups = {}
    cast_keys = {}
    WSRC = {"in": (w_in, wb_in, D, NIN), "br": (w_br, wb_br, 2 * D, D), "out": (w_out, wb_out, D, D),
            "up": (w_up, wb_up, D, 2 * DFF), "down": (w_down, wb_down, DFF, D)}
    for wn, (src, dst, K_, N_) in WSRC.items():
        for l in range(L):
            fns, keys = [], []
            for (p0, p1, pap) in dst[l]:
                for r0 in range(p0, p1, 128):
                    for c0 in range(0, N_, 8192):
                        c1 = min(N_, c0 + 8192)
                        key = f"cw_{wn}_{l}_{r0}_{c0}"
                        keys.append(key)
                        fns.append((key, lambda e, src=src, pap=pap, l=l, r0=r0, p0=p0, c0=c0, c1=c1:
                                    e.dma_start(out=pap[r0 - p0:r0 - p0 + 128, c0:c1], in_=src[l, r0:r0 + 128, c0:c1])))
            cast_groups[(wn, l)] = fns
            cast_keys[(wn, l)] = keys
    cast_order = []
    for l in range(L):
        for wn in ("in", "br", "out", "up", "down"):
            cast_order.append((wn, l))
    cast_pending = []
    for g_ in cast_order:
        for kf in cast_groups[g_]:
            cast_pending.append((g_, kf))
    cast_pos = [0]
    drip_cnt = [0]

    def cast_one():
        if cast_pos[0] < len(cast_pending):
            _, (key, fn) = cast_pending[cast_pos[0]]
            cast_pos[0] += 1
            S.add("pool", fn, (), [key], dma=True, ns="c")

    def drip(every=1):
        drip_cnt[0] += 1
        if drip_cnt[0] % every == 0:
            cast_one()

    def cast_until(g_):
        idx = cast_order.index(g_)
        while cast_pos[0] < len(cast_pending) and cast_order.index(cast_pending[cast_pos[0]][0]) <= idx:
            cast_one()

    cast_until(("in", 0))

    S.sb_reset()
    xin_t = [S.sb("xin", [128, D], F32) for _ in range(2)]
    xst = S.sb("xst", [128, KD, TT], F32)
    for tt_i in range(NTT):
        for tb in range(TT // 128):
            tok0 = tt_i * TT + tb * 128
            b = (tt_i * (TT // 128) + tb) % 2
            ld(xin_t[b][:], x_in[tok0:tok0 + 128, :], [f"xin{b}"])
            for k4 in range(0, KD, 4):
                bank = (k4 // 4) % 4
                nk = min(4, KD - k4)
                for k in range(k4, k4 + nk):
                    tr(ps[bank][:, (k - k4) * 128:(k - k4 + 1) * 128], xin_t[b][:, k * 128:(k + 1) * 128], C_("ident"),
                       [f"xin{b}", "cst"], [f"ps{bank}"])
                eng = "act" if (k4 // 4) % 2 else "dve"
                cp(eng, xst[:, k4:k4 + nk, tb * 128:(tb + 1) * 128],
                   ps[bank][:, 0:nk * 128].rearrange("p (a b) -> p a b", b=128), [f"ps{bank}"], ["xst"])
        stq(xT.rearrange("(k p) t -> p k t", p=128)[:, :, tt_i * TT:(tt_i + 1) * TT], xst[:], ["xst"])
    S.barrier()

    def norm_phase(l, gname, dst, final=False):
        S.sb_reset()
        pc = S.sb("pc", [128, KD], F32)
        ld(pc[:], pcol[l, :, PO[gname]:PO[gname] + KD], ["pc"])
        xt = [S.sb("xt", [128, KD, TT], F32) for _ in range(2)]
        sq = [S.sb("sq", [128, TT], F32) for _ in range(2)]
        rs = S.sb("rs", [128, TT], F32)
        if not final:
            ho = S.sb("ho", [128, KD, TT], BF16)
        else:
            hf = [S.sb("hf", [128, TT], F32) for _ in range(2)]
            yt = [S.sb("yt", [128, D], F32) for _ in range(2)]
        xTv = xT.rearrange("(k p) t -> p k t", p=128)
        for ti in range(NTT):
            b = ti % 2
            ld(xt[b][:], xTv[:, :, ti * TT:(ti + 1) * TT], [f"xt{b}"])
            for k in range(KD):
                act(sq[k % 2][:], xt[b][:, k, :], AF.Square, [f"xt{b}"], [f"sq{k % 2}"])
                mm(ps[0][:, 0:TT], C_("ones"), sq[k % 2][:], k == 0, k == KD - 1, [f"sq{k % 2}", "cst"], ["ps0"])
            ts("dve", rs[:], ps[0][:, 0:TT], 1.0 / D, 1e-6, ALU.mult, ALU.add, ["ps0"], ["rs"])
            act(rs[:], rs[:], AF.Sqrt, ["rs"], ["rs"])
            recip(rs[:], rs[:], ["rs"], ["rs"])
            if not final:
                for k in range(KD):
                    stt(ho[:, k, :], xt[b][:, k, :], pc[:, k:k + 1], rs[:], ALU.mult, ALU.mult, [f"xt{b}", "pc", "rs"], ["ho"])
                stq(dst.rearrange("(k p) t -> p k t", p=128)[:, :, ti * TT:(ti + 1) * TT], ho[:], ["ho"])
            else:
                for tb in range(TT // 128):
                    yb = (ti * (TT // 128) + tb) % 2
                    for k in range(KD):
                        hb_ = k % 2
                        bank = 1 + (k // 4) % 4
                        if tb == 0 or True:
                            stt(hf[hb_][:, 0:128], xt[b][:, k, tb * 128:(tb + 1) * 128], pc[:, k:k + 1],
                                rs[:, tb * 128:(tb + 1) * 128], ALU.mult, ALU.mult, [f"xt{b}", "pc", "rs"], [f"hf{hb_}"])
                        tr(ps[bank][:, (k % 4) * 128:(k % 4 + 1) * 128], hf[hb_][:, 0:128], C_("ident"),
                           [f"hf{hb_}", "cst"], [f"ps{bank}"])
                        if k % 4 == 3 or k == KD - 1:
                            k4 = k - (k % 4)
                            nk = k - k4 + 1
                            cp("act", yt[yb][:, k4 * 128:(k4 + nk) * 128], ps[bank][:, 0:nk * 128], [f"ps{bank}"], [f"yt{yb}"])
                    tok0 = ti * TT + tb * 128
                    stq(y_out[tok0:tok0 + 128, :], yt[yb][:], [f"yt{yb}"])
        S.barrier()

    def gemm(groups, chunks, SW, handler, setup=None, wkey=None):
        S.sb_reset()
        cast_until(wkey)
        wk_ = cast_keys[wkey]
        extra = setup() if setup else None
        KCs = [a.shape[0] // 128 for a, _ in groups]
        ng = len(groups)
        at = [S.sb("gact", [128, kc, TT], BF16) for kc in KCs]
        wsl = [[S.sb("gw", [128, kc, SW], BF16) for kc in KCs] for _ in range(2)]
        slabs = []
        cur = []
        for ch in chunks:
            if cur and (ch[0] + ch[1] - cur[0][0] > SW or ch[0] != cur[-1][0] + cur[-1][1]):
                slabs.append(cur)
                cur = []
            cur.append(ch)
        if cur:
            slabs.append(cur)
        nb = 6 // ng
        cnt = 0
        si = 0
        for ti in range(NTT):
            for g, (a, _) in enumerate(groups):
                ld(at[g][:], a.rearrange("(k p) t -> p k t", p=128)[:, :, ti * TT:(ti + 1) * TT], [f"gact{g}"])
            for slab in slabs:
                sb_ = si % 2
                si += 1
                s0 = slab[0][0]
                s1 = slab[-1][0] + slab[-1][1]
                for g, (_, wd) in enumerate(groups):
                    for pi_, (r0, r1, pap) in enumerate(wd):
                        ld(wsl[sb_][g][:, r0 // 128:r1 // 128, 0:s1 - s0], pap.rearrange("(k p) n -> p k n", p=128)[:, :, s0:s1],
                           [f"gw{sb_}_{g}_{pi_}"], wk_)
                for (c0, m, tag) in slab:
                    bset = cnt % nb
                    cnt += 1
                    pss = []
                    for g in range(ng):
                        bank = bset * ng + g
                        for kc in range(KCs[g]):
                            pi_ = [i_ for i_, (r0, r1, _) in enumerate(groups[g][1]) if r0 <= kc * 128 < r1][0]
                            mm(ps[bank][0:m, 0:TT], wsl[sb_][g][:, kc, c0 - s0:c0 - s0 + m], at[g][:, kc, :],
                               kc == 0, kc == KCs[g] - 1, [f"gw{sb_}_{g}_{pi_}", f"gact{g}"], [f"ps{bank}"])
                        pss.append((ps[bank][0:m, 0:TT], f"ps{bank}"))
                    handler(tag, c0, m, ti, pss, extra, cnt)
                    drip(16)
        S.barrier()

    def chunks_of(c0, c1, tag):
        out = []
        c = c0
        while c < c1:
            m = min(128, c1 - c)
            out.append((c, m, tag))
            c += m
        return out

    def sgu_phase(l):
        S.sb_reset()
        SK = SGW // 128
        lg = S.sb("lg", [128, 2 * SK], F32)
        ld(lg[:], pcol[l, :, PO["lng"]:PO["lng"] + 2 * SK], ["lg"])
        bsb = S.sb("bsb", [128, SGG * 128], F32)
        ld(bsb[:], prow[l, 4 * NH:4 * NH + SGG * 128].partition_broadcast(128), ["bsb"])
        wsf = S.sb("wsf", [128, SGG, 128], F32)
        ld(wsf[:], sg_ws[l].rearrange("g t s -> t g s"), ["wsf"])
        wsT = S.sb("wsT", [128, SGG, 128], BF16)
        for g0 in range(0, SGG, 4):
            n4 = min(4, SGG - g0)
            for g in range(g0, g0 + n4):
                tr(ps[0][:, (g - g0) * 128:(g - g0 + 1) * 128], wsf[:, g, :], C_("ident"), ["wsf", "cst"], ["ps0"])
            cp("dve", wsT[:, g0:g0 + n4, :], ps[0][:, 0:n4 * 128].rearrange("p (a b) -> p a b", b=128), ["ps0"], ["wsT"])
        vv = S.sb("vv", [128, SK, TT], BF16)
        uu = S.sb("uu", [128, SK, TT], BF16)
        vn = S.sb("vn", [128, SK, TT], BF16)
        yo = S.sb("yo", [128, SK, TT], BF16)
        sq = [S.sb("sq", [128, TT], F32) for _ in range(2)]
        mean = S.sb("mean", [128, TT], F32)
        msq = S.sb("msq", [128, TT], F32)
        rstd = S.sb("rstd", [128, TT], F32)
        d1 = [S.sb("d1", [128, TT], F32) for _ in range(2)]
        vtk = [S.sb("vtk", [128, 128], BF16) for _ in range(4)]
        mx = [S.sb("mx", [128, TT], F32) for _ in range(2)]
        NTB_ = TT // 128
        cnt = 0
        for ti in range(NTT):
            tsl = slice(ti * TT, (ti + 1) * TT)
            ld(uu[:], psg[0:SGW, :].rearrange("(k p) t -> p k t", p=128)[:, :, tsl], ["uu"])
            ld(vv[:], psg[SGW:2 * SGW, :].rearrange("(k p) t -> p k t", p=128)[:, :, tsl], ["vv"])
            for k in range(SK):
                mm(ps[0][:, 0:TT], onesb[:], vv[:, k, :], k == 0, k == SK - 1, ["onesb", "vv"], ["ps0"])
            for k in range(SK):
                act(sq[k % 2][:], vv[:, k, :], AF.Square, ["vv"], [f"sq{k % 2}"])
                mm(ps[1][:, 0:TT], C_("ones"), sq[k % 2][:], k == 0, k == SK - 1, ["cst", f"sq{k % 2}"], ["ps1"])
            ts("dve", mean[:], ps[0][:, 0:TT], 1.0 / SGW, None, ALU.mult, None, ["ps0"], ["mean"])
            tt("dve", msq[:], mean[:], mean[:], ALU.mult, ["mean"], ["msq"])
            stt(rstd[:], ps[1][:, 0:TT], 1.0 / SGW, msq[:], ALU.mult, ALU.subtract, ["ps1", "msq"], ["rstd"])
            ts("dve", rstd[:], rstd[:], 1e-5, None, ALU.add, None, ["rstd"], ["rstd"])
            act(rstd[:], rstd[:], AF.Sqrt, ["rstd"], ["rstd"])
            recip(rstd[:], rstd[:], ["rstd"], ["rstd"])
            for k in range(SK):
                b = k % 2
                tt("dve", d1[b][:], vv[:, k, :], mean[:], ALU.subtract, ["vv", "mean"], [f"d1{b}"])
                tt("dve", d1[b][:], d1[b][:], rstd[:], ALU.mult, [f"d1{b}", "rstd"], [f"d1{b}"])
                ts("dve", vn[:, k, :], d1[b][:], lg[:, k:k + 1], lg[:, SK + k:SK + k + 1], ALU.mult, ALU.add, [f"d1{b}", "lg"], ["vn"])
            for k in range(SK):
                bank = 2 + k % 2
                pbank = 4 + k % 2
                pst = ps[pbank][:].bitcast(BF16)
                for tb in range(NTB_):
                    vb = cnt % 4
                    cnt += 1
                    tr(pst[:, tb * 128:(tb + 1) * 128], vn[:, k, tb * 128:(tb + 1) * 128], identb[:], ["vn", "identb"], [f"ps{pbank}"])
                    cp("act", vtk[vb][:], pst[:, tb * 128:(tb + 1) * 128], [f"ps{pbank}"], [f"vtk{vb}"])
                    mm(ps[bank][:, tb * 128:(tb + 1) * 128], vtk[vb][:], wsT[:, k, :], True, True, [f"vtk{vb}", "wsT"], [f"ps{bank}"])
                b = k % 2
                tt("dve", mx[b][:].rearrange("p (a b) -> p a b", b=128), ps[bank][:, 0:TT].rearrange("p (a b) -> p a b", b=128),
                   bsb[:, k * 128:(k + 1) * 128].unsqueeze(1).broadcast_to([128, NTB_, 128]), ALU.add, [f"ps{bank}", "bsb"], [f"mx{b}"])
                tt("dve", yo[:, k, :], mx[b][:], uu[:, k, :], ALU.mult, [f"mx{b}", "uu"], ["yo"])
            stq(ycat[D + HYW:2 * D, :].rearrange("(k p) t -> p k t", p=128)[:, :, tsl], yo[:], ["yo"])
        S.barrier()

    def hyena_phase(l):
        HK = HYW // 128
        S.sb_reset()
        kb0 = S.sb("kb0", [128, HK], F32)
        base_mark = S.sbuf_off
        zp = S.sb("zp", [33, T], F32)
        ld(zp[:], zpos_d, ["zp"])
        w1 = S.sb("w1", [33, 64], F32)
        w2 = S.sb("w2", [64, 64], F32)
        w3 = S.sb("w3", [64, 64], F32)
        w4 = S.sb("w4", [64, 2 * HYW], F32)
        pp = S.sb("pp", [64, 4], F32)
        fb = S.sb("fb", [64, 3], F32)
        ld(w1[:], hy_w1[l], ["w1"])
        ld(w2[:], hy_w2[l], ["w2"])
        ld(w3[:], hy_w3[l], ["w3"])
        ld(w4[:], hy_w4[l], ["w4"])
        ld(pp[:], p64[l], ["pp"])
        ld(kb0[:], pcol[l, :, PO["hbias"]:PO["hbias"] + HK], ["kb0"])
        ts("dve", fb[:], pp[:, 0:3], pp[:, 3:4], None, ALU.mult, None, ["pp"], ["fb"])
        tng = S.sb("tng", [128, NCH], F32)
        ld(tng[:], tneg_d, ["tng"])
        dl = S.sb("dl", [128, HYW], F32)
        ld(dl[:], deltas_d[0, :].partition_broadcast(128), ["dl"])
        hbuf = [S.sb("hh", [64, T], F32) for _ in range(2)]
        arg = [S.sb("arg", [64, 512], F32) for _ in range(2)]
        ar2 = [S.sb("ar2", [64, 512], F32) for _ in range(2)]
        MAGIC = 12582912.0
        INV2PI = 1.0 / (2.0 * math.pi)
        PIS = 3.141592
        CW = min(512, T)
        cnt = 0
        cur, curk, curK = zp, "zp", 33
        for i, (W, wk) in enumerate(((w1, "w1"), (w2, "w2"), (w3, "w3"))):
            ob = hbuf[i % 2]
            for c0 in range(0, T, CW):
                b = cnt % 2
                bank = cnt % 2
                cnt += 1
                mm(ps[bank][0:64, 0:CW], W[0:curK, :], cur[0:curK, c0:c0 + CW], True, True, [wk, curk], [f"ps{bank}"])
                act(arg[b][:, 0:CW], ps[bank][0:64, 0:CW], AF.Identity, [f"ps{bank}", "pp", "fb"], [f"arg{b}"],
                    bias=fb[:, i:i + 1], scale=pp[:, 3:4])
                ts("dve", ar2[b][:, 0:CW], arg[b][:, 0:CW], INV2PI, MAGIC, ALU.mult, ALU.add, [f"arg{b}"], [f"ar2{b}"])
                ts("dve", ar2[b][:, 0:CW], ar2[b][:, 0:CW], -MAGIC, None, ALU.add, None, [f"ar2{b}"], [f"ar2{b}"])
                stt(arg[b][:, 0:CW], ar2[b][:, 0:CW], -2.0 * math.pi, arg[b][:, 0:CW], ALU.mult, ALU.add, [f"ar2{b}", f"arg{b}"], [f"arg{b}"])
                ts("dve", arg[b][:, 0:CW], arg[b][:, 0:CW], PIS, -PIS, ALU.min, ALU.max, [f"arg{b}"], [f"arg{b}"])
                act(ob[:, c0:c0 + CW], arg[b][:, 0:CW], AF.Sin, [f"arg{b}"], [f"hh{i % 2}"])
            cur, curk, curK = ob, f"hh{i % 2}", 64
        h3, h3k = cur, curk
        for k in range(HK):
            mm(ps[2][:, 2 * k:2 * k + 2], w4[:, k * 128:(k + 1) * 128], h3[:, 0:2], True, True, ["w4", h3k], ["ps2"])
        tt("dve", kb0[:], kb0[:], ps[2][:, 0:2 * HK].rearrange("p (k two) -> p k two", two=2)[:, :, 0], ALU.add, ["kb0", "ps2"], ["kb0"])
        winf = [S.sb("winf", [128, HYW], F32) for _ in range(2)]
        kt = [S.sb("kt", [128, 2 * HYW], BF16) for _ in range(2)]
        cnt = 0
        for pc in range(NCH):
            b = pc % 2
            act(winf[b][:], dl[:], AF.Exp, ["dl", "tng"], [f"winf{b}"], scale=tng[:, pc:pc + 1])
            if pc == 0:
                ts("dve", winf[b][:], winf[b][:], m0, None, ALU.mult, None, [f"winf{b}", "cst"], [f"winf{b}"])
            for half in range(2):
                for c0 in range(0, HYW, 512):
                    cw_ = min(512, HYW - c0)
                    bank = 3 + cnt % 4
                    cnt += 1
                    mm(ps[bank][:, 0:cw_], h3[:, pc * 128:(pc + 1) * 128], w4[:, half * HYW + c0:half * HYW + c0 + cw_], True, True,
                       [h3k, "w4"], [f"ps{bank}"])
                    tt("dve", kt[b][:, half * HYW + c0:half * HYW + c0 + cw_], ps[bank][:, 0:cw_], winf[b][:, c0:c0 + cw_], ALU.mult,
                       [f"ps{bank}", f"winf{b}"], [f"kt{b}"])
            stq(k_tok[pc * 128:(pc + 1) * 128, :], kt[b][:], [f"kt{b}"])
        S.barrier()

        S.sbuf_off = base_mark
        hw_ = S.sb("hw", [128, 12 * HK], F32)
        ld(hw_[:], pcol[l, :, PO["hcw0"]:PO["hcw0"] + 12 * HK], ["hw"])
        hin = [S.sb("hin", [128, 2, SEG + 2], BF16) for _ in range(3)]
        hacc = [S.sb("hacc", [128, 2, SEG], F32) for _ in range(3)]
        wbf = [S.sb("wbf", [128, T], BF16) for _ in range(2)]
        x0b = [S.sb("x0b", [128, T], BF16) for _ in range(2)]
        wtk = [S.sb("wtk", [128, 8, 128], BF16) for _ in range(2)]
        for p_ in range(3):
            mset("dve", hin[p_][:, 0, 0:1], 0.0, [], [f"hin{p_}"])
            mset("dve", hin[p_][:, 1, SEG + 1:SEG + 2], 0.0, [], [f"hin{p_}"])
        tcnt = 0
        for j in range(HK):
            b = j % 2
            for p_ in range(3):
                cc = p_ * HK + j
                ki, ka = f"hin{p_}", f"hacc{p_}"
                ld(hin[p_][:, :, 1:SEG + 1], phy[cc * 128:(cc + 1) * 128, :].rearrange("p (s t) -> p s t", s=2), [ki])
                ts("dve", hin[p_][:, 0, SEG + 1:SEG + 2], hin[p_][:, 1, 1:2], cplc[:, 0:1], None, ALU.mult, None, [ki, "cplc"], [ki])
                ts("dve", hin[p_][:, 1, 0:1], hin[p_][:, 0, SEG:SEG + 1], cplc[:, 0:1], None, ALU.mult, None, [ki, "cplc"], [ki])
                ts("dve", hacc[p_][:], hin[p_][:, :, 0:SEG], hw_[:, cc:cc + 1], hw_[:, 9 * HK + cc:9 * HK + cc + 1], ALU.mult, ALU.add,
                   [ki, "hw"], [ka])
                for jj in range(1, 3):
                    stt(hacc[p_][:], hin[p_][:, :, jj:jj + SEG], hw_[:, jj * 3 * HK + cc:jj * 3 * HK + cc + 1], hacc[p_][:],
                        ALU.mult, ALU.add, [ki, "hw", ka], [ka])
            tt("dve", wbf[b][:].rearrange("p (s t) -> p s t", s=2), hacc[2][:], hacc[1][:], ALU.mult, ["hacc2", "hacc1"], [f"wbf{b}"])
            cp("act", x0b[b][:].rearrange("p (s t) -> p s t", s=2), hacc[0][:], ["hacc0"], [f"x0b{b}"])
            stq(w_fm[j * 128:(j + 1) * 128, :], wbf[b][:], [f"wbf{b}"])
            stq(x0_fm[j * 128:(j + 1) * 128, :], x0b[b][:], [f"x0b{b}"])
            for t8 in range(0, NCH, 8):
                n8 = min(8, NCH - t8)
                bank = 2 + tcnt % 2
                kb_ = tcnt % 2
                tcnt += 1
                pst = ps[bank][:].bitcast(BF16)
                for i in range(n8):
                    tr(pst[:, i * 128:(i + 1) * 128], wbf[b][:, (t8 + i) * 128:(t8 + i + 1) * 128], identb[:],
                       [f"wbf{b}", "identb"], [f"ps{bank}"])
                cp("act", wtk[kb_][:, 0:n8, :], pst[:, 0:n8 * 128].rearrange("p (a b) -> p a b", b=128), [f"ps{bank}"], [f"wtk{kb_}"])
                stq(w_tokd[t8 * 128:(t8 + n8) * 128, j * 128:(j + 1) * 128].rearrange("(a p) c -> p a c", p=128), wtk[kb_][:, 0:n8, :], [f"wtk{kb_}"])
        S.barrier()

        S.sbuf_off = base_mark
        CB = 256 if HYW >= 256 else HYW
        NCB = HYW // CB
        seq = [S.sb("seq", [128, NF, CB], BF16) for _ in range(6)]
        fsl = [S.sb("fsl", [128, NF, 256], BF16) for _ in range(2)]
        pq = [S.sb("pq", [128, 2 * CB], F32) for _ in range(6)]
        pqc = [S.sb("pqc", [128, 2 * CB], F32) for _ in range(2)]
        kf_ = [S.sb("kf", [128, CB], F32) for _ in range(6)]
        ya = [S.sb("ya", [128, CB], F32) for _ in range(2)]
        Yt = [S.sb("Y", [128, NF, CB], BF16) for _ in range(4)]
        gsl = S.sb("gsl", [128, NF, 2, TBW], BF16)
        xw = [[S.sb("xw", [128, TBW], BF16) for _ in range(2)] for _ in range(2)]
        yf = [S.sb("yf", [128, TBW], F32) for _ in range(2)]
        yb_ = [S.sb("yb", [128, TBW], BF16) for _ in range(2)]
        fcnt = 0
        ocnt = 0
        for cb in range(NCB):
            c0 = cb * CB
            srcs = [w_tokd[0:SEG, c0:c0 + CB], w_tokd[SEG:2 * SEG, c0:c0 + CB],
                    k_tok[0:SEG, c0:c0 + CB], k_tok[SEG:2 * SEG, c0:c0 + CB],
                    k_tok[0:SEG, HYW + c0:HYW + c0 + CB], k_tok[SEG:2 * SEG, HYW + c0:HYW + c0 + CB]]
            for s_ in range(6):
                ld(seq[s_][:], srcs[s_].rearrange("(tc p) c -> p tc c", p=128), [f"seq{s_}"])
            for j in range(NF):
                fbuf = fcnt % 2
                fcnt += 1
                drip(1)
                ld(fsl[fbuf][:], dftf_d[j], [f"fsl{fbuf}"])
                for s_ in range(6):
                    for part in range(2):
                        for tc in range(NF):
                            mm(ps[s_][:, part * CB:(part + 1) * CB], fsl[fbuf][:, tc, part * 128:(part + 1) * 128], seq[s_][:, tc, :],
                               tc == 0, tc == NF - 1, [f"fsl{fbuf}", f"seq{s_}"], [f"ps{s_}"])
                    cp("act" if s_ % 2 else "dve", pq[s_][:], ps[s_][:, 0:2 * CB], [f"ps{s_}"], [f"pq{s_}"])
                    if s_ < 2:
                        act(pqc[s_][:], ps[s_][:, 0:2 * CB], AF.Copy, [f"ps{s_}", "cplc"], [f"pqc{s_}"], scale=cplc[:, 0:1])
                P = lambda s_: pq[s_][:, 0:CB]
                Q = lambda s_: pq[s_][:, CB:2 * CB]
                KF = ["kf0", "kf1", "kf2", "kf3", "kf4", "kf5"]
                tt("dve", kf_[0][:], P(2), P(4), ALU.add, ["pq2", "pq4"], ["kf0"])
                tt("dve", kf_[1][:], Q(4), Q(2), ALU.subtract, ["pq2", "pq4"], ["kf1"])
                stt(kf_[2][:], Q(2), sgn, P(3), ALU.mult, ALU.add, ["pq2", "pq3", "cst"], ["kf2"])
                stt(kf_[3][:], P(2), sgn, Q(3), ALU.mult, ALU.subtract, ["pq2", "pq3", "cst"], ["kf3"])
                stt(kf_[4][:], Q(4), sgn, P(5), ALU.mult, ALU.add, ["pq4", "pq5", "cst"], ["kf4"])
                stt(kf_[5][:], P(4), sgn, Q(5), ALU.mult, ALU.subtract, ["pq4", "pq5", "cst"], ["kf5"])
                Pc = lambda s_: pqc[s_][:, 0:CB]
                Qc = lambda s_: pqc[s_][:, CB:2 * CB]

                def combo(dst, terms, rk):
                    first = True
                    for (sg_, a, b_) in terms:
                        if first:
                            tt("dve", ya[0][:], a, b_, ALU.mult, rk, ["ya0"])
                            if sg_ < 0:
                                ts("dve", ya[0][:], ya[0][:], -1.0, None, ALU.mult, None, ["ya0"], ["ya0"])
                            first = False
                        else:
                            tt("dve", ya[1][:], a, b_, ALU.mult, rk, ["ya1"])
                            tt("dve", ya[0][:], ya[0][:], ya[1][:], ALU.add if sg_ > 0 else ALU.subtract, ["ya0", "ya1"], ["ya0"])
                    cp("act", dst, ya[0][:], ["ya0"], ["Y"])
                rk = ["pq0", "pq1", "pqc0", "pqc1"] + KF
                combo(Yt[0][:, j, :], [(1, kf_[0][:], P(0)), (1, kf_[1][:], Q(0)), (1, kf_[4][:], Pc(1)), (-1, kf_[5][:], Qc(1))], rk)
                combo(Yt[1][:, j, :], [(1, kf_[0][:], Q(0)), (-1, kf_[1][:], P(0)), (1, kf_[4][:], Qc(1)), (1, kf_[5][:], Pc(1))], rk)
                combo(Yt[2][:, j, :], [(1, kf_[0][:], P(1)), (1, kf_[1][:], Q(1)), (1, kf_[2][:], Pc(0)), (1, kf_[3][:], Qc(0))], rk)
                combo(Yt[3][:, j, :], [(1, kf_[0][:], Q(1)), (-1, kf_[1][:], P(1)), (1, kf_[2][:], Qc(0)), (-1, kf_[3][:], Pc(0))], rk)
            for tb in range(NTB):
                ld(gsl[:], dfti_d[tb], ["gsl"])
                for sg_i in range(2):
                    for cc in range(CB // 128):
                        ob_ = ocnt % 2
                        bank = 6 + ocnt % 2
                        ocnt += 1
                        ch0 = c0 + cc * 128
                        tok0 = sg_i * SEG + tb * TBW
                        ld(xw[ob_][0][:], x0_fm[ch0:ch0 + 128, tok0:tok0 + TBW], [f"xw{ob_}0"])
                        ld(xw[ob_][1][:], w_fm[ch0:ch0 + 128, tok0:tok0 + TBW], [f"xw{ob_}1"])
                        for j in range(NF):
                            mm(ps[bank][:, 0:TBW], Yt[2 * sg_i][:, j, cc * 128:(cc + 1) * 128], gsl[:, j, 0, :], j == 0, False, ["Y", "gsl"], [f"ps{bank}"])
                            mm(ps[bank][:, 0:TBW], Yt[2 * sg_i + 1][:, j, cc * 128:(cc + 1) * 128], gsl[:, j, 1, :], False, j == NF - 1, ["Y", "gsl"], [f"ps{bank}"])
                        kch = ch0 // 128
                        stt(yf[ob_][:], xw[ob_][1][:], kb0[:, kch:kch + 1], ps[bank][:, 0:TBW], ALU.mult, ALU.add,
                            [f"xw{ob_}1", "kb0", f"ps{bank}"], [f"yf{ob_}"])
                        tt("dve", yb_[ob_][:], yf[ob_][:], xw[ob_][0][:], ALU.mult, [f"yf{ob_}", f"xw{ob_}0"], [f"yb{ob_}"])
                        stq(ycat[D + ch0:D + ch0 + 128, tok0:tok0 + TBW], yb_[ob_][:], [f"yb{ob_}"])
        S.barrier()

    def ssd_phase(l):
        H2 = 2 * NH
        QH = 4
        S.sb_reset()
        rowp = S.sb("rowp", [128, 4 * NH], F32)
        ld(rowp[:], prow[l, 0:4 * NH].partition_broadcast(128), ["rowp"])
        Abc = S.sb("Abc", [128, H2], F32)
        act(Abc[:], rowp[:, H2:2 * H2], AF.Exp, ["rowp"], ["Abc"])
        ts("dve", Abc[:], Abc[:], -1.0, None, ALU.mult, None, ["Abc"], ["Abc"])
        pcs = S.sb("pcs", [128, 2 * KD], F32)
        ld(pcs[:], pcol[l, :, PO["dskip"]:PO["dskip"] + 2 * KD], ["pcs"])
        neg4 = [S.sb("neg4", [128, QH, 128], BF16) for _ in range(2)]
        trib = [S.sb("trib", [128, 128], BF16) for _ in range(2)]
        negob = S.sb("negob", [128, 128], BF16)
        for d in range(2):
            cp("dve", neg4[d][:], C_("negf" if d == 0 else "negb").unsqueeze(1).broadcast_to([128, QH, 128]), ["cst"], [f"neg4{d}"])
            cp("dve", trib[d][:], C_("triU" if d == 0 else "triL"), ["cst"], [f"trib{d}"])
        cp("dve", negob[:], C_("negones"), ["cst"], ["negob"])
        tri = [C_("triU"), C_("triL")]
        dth = S.sb("dth", [128, H2], BF16)
        dthf = S.sb("dthf", [128, H2], F32)
        dtl = S.sb("dtl", [128, H2], BF16)
        dtr = S.sb("dtr", [128, 128], F32)
        uu = S.sb("su", [128, H2], F32)
        dt = S.sb("dt", [128, H2], F32)
        dtA = S.sb("dtA", [128, H2], F32)
        acs = S.sb("acs", [128, H2], F32)
        cd = S.sb("cd", [128, H2], F32)
        dend = S.sb("dend", [128, H2], F32)
        coef = S.sb("coef", [128, H2], F32)
        xt = S.sb("xt", [128, D], BF16)
        bt = S.sb("bt", [128, GN], BF16)
        xc = [S.sb("xc", [128, NH, 64], BF16) for _ in range(2)]
        xcd = [S.sb("xcd", [128, NH, 64], BF16) for _ in range(2)]
        mark = S.sbuf_off

        def prep(c, dirs):
            ld(dtr[0:H2, :], pdt[:, c * 128:(c + 1) * 128], ["dtr"])
            ld(xt[:], x_tok[c * 128:(c + 1) * 128, :], ["xt"])
            ld(bt[:], b_tok[c * 128:(c + 1) * 128, :], ["bt"])
            tr(ps[6][:, 0:H2], dtr[0:H2, :], C_("ident")[0:H2, 0:H2], ["dtr", "cst"], ["ps6"])
            tt("dve", uu[:], ps[6][:, 0:H2], rowp[:, 0:H2], ALU.add, ["ps6", "rowp"], ["su"])
            act(uu[:], uu[:], AF.Exp, ["su"], ["su"])
            act(dt[:], uu[:], AF.Ln, ["su"], ["dt"], bias=1.0)
            tt("dve", dtA[:], dt[:], Abc[:], ALU.mult, ["dt", "Abc"], ["dtA"])
            cp("dve", dth[:], dtA[:], ["dtA"], ["dth"])
            cp("dve", dthf[:], dth[:], ["dth"], ["dthf"])
            tt("dve", dtl[:], dtA[:], dthf[:], ALU.subtract, ["dtA", "dthf"], ["dtl"])
            mm(ps[6][:, 128:128 + NH], tri[0], dtA[:, 0:NH], True, True, ["cst", "dtA"], ["ps6"])
            mm(ps[6][:, 128 + NH:128 + H2], tri[1], dtA[:, NH:H2], True, True, ["cst", "dtA"], ["ps6"])
            mm(ps[6][:, 256:256 + H2], C_("ones"), dtA[:], True, True, ["cst", "dtA"], ["ps6"])
            cp("dve", acs[:], ps[6][:, 128:128 + H2], ["ps6"], ["acs"])
            act(cd[:], ps[6][:, 256:256 + H2], AF.Exp, ["ps6"], ["cd"])
            tt("dve", dend[:], ps[6][:, 256:256 + H2], acs[:], ALU.subtract, ["ps6", "acs"], ["dend"])
            act(dend[:], dend[:], AF.Exp, ["dend"], ["dend"])
            tt("dve", coef[:], dt[:], dend[:], ALU.mult, ["dt", "dend"], ["coef"])
            xv = xt[:].rearrange("p (h q) -> p h q", q=64)
            for d in dirs:
                tt("pool", xcd[d][:], xv, coef[:, d * NH:(d + 1) * NH].unsqueeze(2).broadcast_to([128, NH, 64]), ALU.mult,
                   ["xt", "coef"], [f"xcd{d}"])
                if len(dirs) == 2:
                    tt("pool", xc[d][:], xv, dt[:, d * NH:(d + 1) * NH].unsqueeze(2).broadcast_to([128, NH, 64]), ALU.mult,
                       ["xt", "dt"], [f"xc{d}"])

        def upd(Hs, hk, d):
            for g in range(NG):
                mm(ps[7][:, 0:HG * 64], bt[:, g * 128:(g + 1) * 128], xcd[d][:, g * HG:(g + 1) * HG, :].rearrange("p h q -> p (h q)"),
                   True, True, ["bt", f"xcd{d}"], ["ps7"])
                hv = Hs[:, g * HG * 64:(g + 1) * HG * 64]
                tt("dve", hv.rearrange("p (h q) -> p h q", q=64), hv.rearrange("p (h q) -> p h q", q=64),
                   cd[:, d * NH + g * HG:d * NH + (g + 1) * HG].unsqueeze(2).broadcast_to([128, HG, 64]), ALU.mult, [hk, "cd"], [hk])
                tt("dve", hv, hv, ps[7][:, 0:HG * 64], ALU.add, [hk, "ps7"], [hk])

        Hb = S.sb("Hb", [128, D], F32)
        Hbb = [S.sb("Hbb", [128, D], BF16) for _ in range(2)]
        mset("dve", Hb[:], 0.0, [], ["Hb"])
        for c in reversed(range(NCH)):
            prep(c, [1])
            b = c % 2
            cp("act", Hbb[b][:], Hb[:], ["Hb"], [f"Hbb{b}"])
            stq(hb_d[c], Hbb[b][:], [f"Hbb{b}"])
            upd(Hb, "Hb", 1)
            if c == SCH:
                ts("dve", Hb[:], Hb[:], cplc[:, 0:1], None, ALU.mult, None, ["Hb", "cplc"], ["Hb"])
        S.barrier()

        S.sbuf_off = mark
        Hf = S.sb("Hf", [128, D], F32)
        Hfb = S.sb("Hfb", [128, D], BF16)
        hbt = S.sb("hbt", [128, D], BF16)
        bcT = S.sb("bcT", [128, 2 * NG, 128], BF16)
        xsz = S.sb("xsz", [128, 2, KD, 128], BF16)
        cbT = S.sb("cbT", [128, NG, 128], BF16)
        prod = [S.sb("prod", [128, QH, 128], BF16) for _ in range(4)]
        prol = [S.sb("prol", [128, QH, 128], BF16) for _ in range(4)]
        E2 = [S.sb("E2", [128, QH, 128], BF16) for _ in range(2)]
        E1 = [S.sb("E1", [128, QH, 128], BF16) for _ in range(2)]
        NQ = HG // QH
        Mt = [[[S.sb("M", [128, QH, 128], BF16) for _ in range(NQ)] for _ in range(2)] for _ in range(2)]
        Cs = [[[S.sb("Cs", [128, QH, 128], BF16) for _ in range(NQ)] for _ in range(2)] for _ in range(2)]
        NP = HG // 2
        y1 = S.sb("y1", [128, NP, 128], F32)
        y2 = S.sb("y2", [128, NP, 128], F32)
        sqy = S.sb("sqy", [128, NP, 128], F32)
        rsn = S.sb("rsn", [128, 128], F32)
        SUP = 2 if NCH % 2 == 0 else 1
        yst = S.sb("yst", [128, KD, SUP * 128], BF16)
        mset("dve", Hf[:], 0.0, [], ["Hf"])
        qc = 0
        gc = 0
        for c in range(NCH):
            csl = slice(c * 128, (c + 1) * 128)
            prep(c, [0, 1])
            ld(hbt[:], hb_d[c], ["hbt"])
            cp("act", Hfb[:], Hf[:], ["Hf"], ["Hfb"])
            ld(bcT[:], xcT[D:D + 2 * GN, csl].rearrange("(g p) t -> p g t", p=128), ["bcT"])
            ld(xsz[:, 0], xcT[0:D, csl].rearrange("(k p) t -> p k t", p=128), ["xsz"])
            ld(xsz[:, 1], pz[:, csl].rearrange("(k p) t -> p k t", p=128), ["xsz"])
            for g in range(NG):
                mm(ps[7][:, 0:128], bcT[:, g, :], bcT[:, NG + g, :], True, True, ["bcT"], ["ps7"])
                cp("act", cbT[:, g, :], ps[7][:, 0:128], ["ps7"], ["cbT"])
            for g in range(NG):
                gb = gc % 2
                gc += 1
                drip(1)
                for d in range(2):
                    for q in range(NQ):
                        h0 = d * NH + g * HG + q * QH
                        pb = qc % 4
                        ab = qc % 2
                        bA, bB = 2 * ab, 2 * ab + 1
                        qc += 1
                        tt("dve", prod[pb][:], trib[d][:].unsqueeze(1).broadcast_to([128, QH, 128]),
                           dth[:, h0:h0 + QH].unsqueeze(2).broadcast_to([128, QH, 128]), ALU.mult, [f"trib{d}", "dth"], [f"prod{pb}"])
                        tt("dve", prol[pb][:], trib[d][:].unsqueeze(1).broadcast_to([128, QH, 128]),
                           dtl[:, h0:h0 + QH].unsqueeze(2).broadcast_to([128, QH, 128]), ALU.mult, [f"trib{d}", "dtl"], [f"prol{pb}"])
                        pf = prod[pb][:].rearrange("p a b -> p (a b)")
                        pl_ = prol[pb][:].rearrange("p a b -> p (a b)")
                        mm(ps[bA][:, 0:QH * 128], onesb[:], pf, True, False, ["onesb", f"prod{pb}"], [f"ps{bA}"])
                        mm(ps[bA][:, 0:QH * 128], onesb[:], pl_, False, True, ["onesb", f"prol{pb}"], [f"ps{bA}"])
                        act(E2[ab][:].rearrange("p a b -> p (a b)"), ps[bA][:, 0:QH * 128], AF.Exp, [f"ps{bA}"], [f"E2{ab}"])
                        tt("pool", Cs[gb][d][q][:], E2[ab][:], bcT[:, NG + g, :].unsqueeze(1).broadcast_to([128, QH, 128]), ALU.mult,
                           [f"E2{ab}", "bcT"], [f"Cs{gb}{d}{q}"])
                        mm(ps[bB][:, 0:QH * 128], onesb[:], pf, True, False, ["onesb", f"prod{pb}"], [f"ps{bB}"])
                        mm(ps[bB][:, 0:QH * 128], onesb[:], pl_, False, False, ["onesb", f"prol{pb}"], [f"ps{bB}"])
                        mm(ps[bB][:, 0:QH * 128], identb[:], neg4[d][:].rearrange("p a b -> p (a b)"), False, False, ["identb", f"neg4{d}"], [f"ps{bB}"])
                        for i in range(QH):
                            mm(ps[bB][:, i * 128:(i + 1) * 128], prod[pb][:, i, :], negob[:], False, False,
                               ["negob", f"prod{pb}"], [f"ps{bB}"])
                            mm(ps[bB][:, i * 128:(i + 1) * 128], prol[pb][:, i, :], negob[:], False, i == QH - 1,
                               ["negob", f"prol{pb}"], [f"ps{bB}"])
                        act(E1[ab][:].rearrange("p a b -> p (a b)"), ps[bB][:, 0:QH * 128], AF.Exp, [f"ps{bB}"], [f"E1{ab}"])
                        tt("dve", Mt[gb][d][q][:], E1[ab][:], cbT[:, g, :].unsqueeze(1).broadcast_to([128, QH, 128]), ALU.mult,
                           [f"E1{ab}", "cbT"], [f"M{gb}{d}{q}"])
                yb = 4 + gb
                rk = [f"M{gb}{d}{q}" for d in range(2) for q in range(NQ)] + [f"Cs{gb}{d}{q}" for d in range(2) for q in range(NQ)] + \
                     ["xc0", "xc1", "Hfb", "hbt"]
                for hp in range(NP):
                    for half in range(2):
                        hh = g * HG + hp * 2 + half
                        q, hi = (hp * 2 + half) // QH, (hp * 2 + half) % QH
                        out = ps[yb][half * 64:(half + 1) * 64, hp * 128:(hp + 1) * 128]
                        mm(out, xc[0][:, hh, :], Mt[gb][0][q][:, hi, :], True, False, rk, [f"ps{yb}"])
                        mm(out, Hfb[:, hh * 64:(hh + 1) * 64], Cs[gb][0][q][:, hi, :], False, False, rk, [f"ps{yb}"])
                        mm(out, xc[1][:, hh, :], Mt[gb][1][q][:, hi, :], False, False, rk, [f"ps{yb}"])
                        mm(out, hbt[:, hh * 64:(hh + 1) * 64], Cs[gb][1][q][:, hi, :], False, True, rk, [f"ps{yb}"])
                for hp in range(NP):
                    kch = g * NP + hp
                    stt(y1[:, hp, :], xsz[:, 0, kch, :], pcs[:, kch:kch + 1], ps[yb][:, hp * 128:(hp + 1) * 128], ALU.mult, ALU.add,
                        ["xsz", "pcs", f"ps{yb}"], ["y1"])
                tt("dve", y2[:], y1[:], xsz[:, 1, g * NP:(g + 1) * NP, :], ALU.mult, ["y1", "xsz"], ["y2"])
                act(sqy[:], y2[:], AF.Square, ["y2"], ["sqy"])
                for hp in range(NP):
                    mm(ps[6][:, 384:512], C_("ones"), sqy[:, hp, :], hp == 0, hp == NP - 1, ["cst", "sqy"], ["ps6"])
                ts("dve", rsn[:], ps[6][:, 384:512], 1.0 / (NP * 128), 1e-6, ALU.mult, ALU.add, ["ps6"], ["rsn"])
                act(rsn[:], rsn[:], AF.Sqrt, ["rsn"], ["rsn"])
                recip(rsn[:], rsn[:], ["rsn"], ["rsn"])
                for hp in range(NP):
                    kch = g * NP + hp
                    stt(yst[:, kch, (c % SUP) * 128:(c % SUP + 1) * 128], y2[:, hp, :], pcs[:, KD + kch:KD + kch + 1], rsn[:], ALU.mult, ALU.mult,
                        ["y2", "pcs", "rsn"], ["yst"])
            if c % SUP == SUP - 1:
                t0 = (c - SUP + 1) * 128
                stq(ycat[0:D, :].rearrange("(k p) t -> p k t", p=128)[:, :, t0:t0 + SUP * 128], yst[:], ["yst"])
            upd(Hf, "Hf", 0)
            if c == SCH - 1:
                ts("dve", Hf[:], Hf[:], cplc[:, 0:1], None, ALU.mult, None, ["Hf", "cplc"], ["Hf"])
        S.barrier()

    for l in range(L):
        norm_phase(l, "g1", hT)

        o_z, o_xbc, o_dt, o_hy, o_sg, o_gate = 0, D, D + XBC, D + XBC + 2 * NH, D + XBC + 2 * NH + 3 * HYW, \
            D + XBC + 2 * NH + 3 * HYW + 2 * SGW
        chunks = (chunks_of(o_z, o_xbc, "z") + chunks_of(o_xbc, o_dt, "xbc") + chunks_of(o_dt, o_hy, "dt") +
                  chunks_of(o_hy, o_sg, "hy") + chunks_of(o_sg, o_gate, "sg") + chunks_of(o_gate, NIN, "gate"))

        def win_setup(l=l):
            bg = S.sb("bg", [128, 3 * KD], F32)
            ld(bg[:], pcol[l, :, PO["bgate"]:PO["bgate"] + 3 * KD], ["bg"])
            ob = [S.sb("ob", [128, TT], BF16) for _ in range(3)]
            of = [S.sb("of", [128, TT], F32) for _ in range(2)]
            return bg, ob, of

        def win_handler(tag, c0, m, ti, pss, extra, cnt):
            bg, ob, of = extra
            pa, pk = pss[0]
            tsl = slice(ti * TT, (ti + 1) * TT)
            if tag == "dt":
                b = cnt % 2
                cp("act", of[b][0:m, :], pa, [pk], [f"of{b}"])
                stq(pdt[c0 - o_dt:c0 - o_dt + m, tsl], of[b][0:m, :], [f"of{b}"])
                return
            b = cnt % 3
            if tag == "z":
                act(ob[b][0:m, :], pa, AF.Silu, [pk], [f"ob{b}"])
                dst = pz[c0 - o_z:c0 - o_z + m, tsl]
            elif tag == "xbc":
                cp("dve", ob[b][0:m, :], pa, [pk], [f"ob{b}"])
                dst = pxbc[c0 - o_xbc:c0 - o_xbc + m, tsl]
            elif tag == "hy":
                cp("dve", ob[b][0:m, :], pa, [pk], [f"ob{b}"])
                dst = phy[c0 - o_hy:c0 - o_hy + m, tsl]
            elif tag == "sg":
                act(ob[b][0:m, :], pa, AF.Gelu, [pk], [f"ob{b}"])
                dst = psg[c0 - o_sg:c0 - o_sg + m, tsl]
            else:
                j = (c0 - o_gate) // 128
                act(ob[b][0:m, :], pa, AF.Sigmoid, [pk, "bg"], [f"ob{b}"], bias=bg[0:m, j:j + 1])
                dst = pgate[c0 - o_gate:c0 - o_gate + m, tsl]
            stq(dst, ob[b][0:m, :], [f"ob{b}"])

        gemm([(hT, wb_in[l])], chunks, 512, win_handler, win_setup, wkey=("in", l))
        if "stop_win" in debug and l == 0:
            break

        S.sb_reset()
        XK = XBC // 128
        cw = S.sb("cw", [128, 6 * XK], F32)
        ld(cw[:], pcol[l, :, PO["scw0"]:PO["scw0"] + 6 * XK], ["cw"])
        cin = [S.sb("cin", [128, 2, SEG + 4], BF16) for _ in range(2)]
        cacc = [S.sb("cacc", [128, 2, SEG], F32) for _ in range(2)]
        cout = [S.sb("cout", [128, T], BF16) for _ in range(2)]
        ctk = [S.sb("ctk", [128, 8, 128], BF16) for _ in range(2)]
        for b in range(2):
            mset("dve", cin[b][:, 0, 0:2], 0.0, [], [f"cin{b}"])
            mset("dve", cin[b][:, 1, SEG + 2:SEG + 4], 0.0, [], [f"cin{b}"])
        tcnt = 0
        for c in range(XK):
            b = c % 2
            ld(cin[b][:, :, 2:SEG + 2], pxbc[c * 128:(c + 1) * 128, :].rearrange("p (s t) -> p s t", s=2), [f"cin{b}"])
            ts("dve", cin[b][:, 0, SEG + 2:SEG + 4], cin[b][:, 1, 2:4], cplc[:, 0:1], None, ALU.mult, None, [f"cin{b}", "cplc"], [f"cin{b}"])
            ts("dve", cin[b][:, 1, 0:2], cin[b][:, 0, SEG:SEG + 2], cplc[:, 0:1], None, ALU.mult, None, [f"cin{b}", "cplc"], [f"cin{b}"])
            ts("dve", cacc[b][:], cin[b][:, :, 0:SEG], cw[:, c:c + 1], cw[:, 5 * XK + c:5 * XK + c + 1], ALU.mult, ALU.add,
               [f"cin{b}", "cw"], [f"cacc{b}"])
            for j in range(1, 5):
                stt(cacc[b][:], cin[b][:, :, j:j + SEG], cw[:, j * XK + c:j * XK + c + 1], cacc[b][:], ALU.mult, ALU.add,
                    [f"cin{b}", "cw", f"cacc{b}"], [f"cacc{b}"])
            act(cout[b][:], cacc[b][:].rearrange("p s t -> p (s t)"), AF.Silu, [f"cacc{b}"], [f"cout{b}"])
            stq(xcT[c * 128:(c + 1) * 128, :], cout[b][:], [f"cout{b}"])
            if c < KD + GN // 128:
                for t8 in range(0, NCH, 8):
                    n8 = min(8, NCH - t8)
                    bank = 2 + tcnt % 2
                    kb = tcnt % 2
                    tcnt += 1
                    pst = ps[bank][:].bitcast(BF16)
                    for i in range(n8):
                        tr(pst[:, i * 128:(i + 1) * 128], cout[b][:, (t8 + i) * 128:(t8 + i + 1) * 128], identb[:],
                           [f"cout{b}", "identb"], [f"ps{bank}"])
                    cp("act", ctk[kb][:, 0:n8, :], pst[:, 0:n8 * 128].rearrange("p (a b) -> p a b", b=128), [f"ps{bank}"], [f"ctk{kb}"])
                    if c < KD:
                        dst = x_tok[t8 * 128:(t8 + n8) * 128, c * 128:(c + 1) * 128]
                    else:
                        dst = b_tok[t8 * 128:(t8 + n8) * 128, (c - KD) * 128:(c - KD + 1) * 128]
                    stq(dst.rearrange("(a p) c -> p a c", p=128), ctk[kb][:, 0:n8, :], [f"ctk{kb}"])
        S.barrier()

        ssd_phase(l)
        hyena_phase(l)
        sgu_phase(l)

        def wbr_setup():
            gt = [[S.sb("gt", [128, TT], BF16) for _ in range(3)] for _ in range(2)]
            t3 = [S.sb("t3", [128, TT], F32) for _ in range(3)]
            ob = [S.sb("mo", [128, TT], BF16) for _ in range(2)]
            return gt, t3, ob

        def wbr_handler(tag, c0, m, ti, pss, extra, cnt):
            gt, t3, ob = extra
            b = cnt % 2
            tsl = slice(ti * TT, (ti + 1) * TT)
            for g in range(3):
                ld(gt[b][g][0:m, :], pgate[g * D + c0:g * D + c0 + m, tsl], [f"gt{b}_{g}"])
            for g in range(3):
                tt("dve", t3[g][0:m, :], pss[g][0], gt[b][g][0:m, :], ALU.mult, [pss[g][1], f"gt{b}_{g}"], [f"t3_{g}"])
            tt("dve", t3[0][0:m, :], t3[0][0:m, :], t3[1][0:m, :], ALU.add, ["t3_0", "t3_1"], ["t3_0"])
            tt("dve", ob[b][0:m, :], t3[0][0:m, :], t3[2][0:m, :], ALU.add, ["t3_0", "t3_2"], [f"mo{b}"])
            stq(mergedT[c0:c0 + m, tsl], ob[b][0:m, :], [f"mo{b}"])

        gemm([(ycat[0:D, :], wrows(wb_br[l], 0, D)), (ycat[D:D + HYW, :], wrows(wb_br[l], D, D + HYW)),
              (ycat[D + HYW:2 * D, :], wrows(wb_br[l], D + HYW, 2 * D))], chunks_of(0, D, "m"), 256, wbr_handler, wbr_setup, wkey=("br", l))

        def res_setup():
            xr = [S.sb("xr", [128, TT], F32) for _ in range(3)]
            return xr

        def res_handler(tag, c0, m, ti, pss, extra, cnt):
            xr = extra
            b = cnt % 3
            tsl = slice(ti * TT, (ti + 1) * TT)
            ld(xr[b][0:m, :], xT[c0:c0 + m, tsl], [f"xr{b}"])
            tt("dve", xr[b][0:m, :], pss[0][0], xr[b][0:m, :], ALU.add, [pss[0][1], f"xr{b}"], [f"xr{b}"])
            stq(xT[c0:c0 + m, tsl], xr[b][0:m, :], [f"xr{b}"])

        gemm([(mergedT, wb_out[l])], chunks_of(0, D, "r"), 512, res_handler, res_setup, wkey=("out", l))

        norm_phase(l, "g2", hT)

        def up_setup():
            return [S.sb("uo", [128, TT], BF16) for _ in range(3)]

        def up_handler(tag, c0, m, ti, pss, extra, cnt):
            b = cnt % 3
            eng = "act" if cnt % 2 else "dve"
            cp(eng, extra[b][0:m, :], pss[0][0], [pss[0][1]], [f"uo{b}"])
            stq(upT[c0:c0 + m, ti * TT:(ti + 1) * TT], extra[b][0:m, :], [f"uo{b}"])

        gemm([(hT, wb_up[l])], chunks_of(0, 2 * DFF, "u"), 512, up_handler, up_setup, wkey=("up", l))

        S.sb_reset()
        FK = DFF // 128
        fw = S.sb("fw", [128, 8 * FK], F32)
        ld(fw[:], pcol[l, :, PO["fcw0"]:PO["fcw0"] + 8 * FK], ["fw"])
        fin = [[S.sb("fin", [128, 2, SEG + 2], BF16) for _ in range(2)] for _ in range(2)]
        dg = [[[S.sb("dg", [128, 128], BF16) for _ in range(3)] for _ in range(2)] for _ in range(2)]
        fsg = [S.sb("fsg", [128, TT], F32) for _ in range(2)]
        fo = [S.sb("fo", [128, T], BF16) for _ in range(2)]
        for b in range(2):
            for h in range(2):
                mset("dve", fin[b][h][:, 0, 0:1], 0.0, [], [f"fin{b}{h}"])
                mset("dve", fin[b][h][:, 1, SEG + 1:SEG + 2], 0.0, [], [f"fin{b}{h}"])
        fcnt_ = 0
        for c in range(FK):
            b = c % 2
            for h in range(2):
                cc = c + h * FK
                k_in = f"fin{b}{h}"
                ld(fin[b][h][:, :, 1:SEG + 1], upT[cc * 128:(cc + 1) * 128, :].rearrange("p (s t) -> p s t", s=2), [k_in])
                ts("dve", fin[b][h][:, 0, SEG + 1:SEG + 2], fin[b][h][:, 1, 1:2], cplc[:, 0:1], None, ALU.mult, None, [k_in, "cplc"], [k_in])
                ts("dve", fin[b][h][:, 1, 0:1], fin[b][h][:, 0, SEG:SEG + 1], cplc[:, 0:1], None, ALU.mult, None, [k_in, "cplc"], [k_in])
                for j in range(3):
                    ts("dve", dg[b][h][j][:], C_("ident"), fw[:, j * 2 * FK + cc:j * 2 * FK + cc + 1], None, ALU.mult, None,
                       ["cst", "fw"], [f"dg{b}{h}{j}"])
            for s_ in range(2):
                for t0 in range(0, SEG, TT):
                    fb_ = fcnt_ % 2
                    fcnt_ += 1
                    bg_, bv_ = fb_, 2 + fb_
                    for h, bank in ((0, bg_), (1, bv_)):
                        for j in range(3):
                            mm(ps[bank][:, 0:TT], dg[b][h][j][:], fin[b][h][:, s_, t0 + j:t0 + j + TT], j == 0, j == 2,
                               [f"dg{b}{h}{j}", f"fin{b}{h}"], [f"ps{bank}"])
                    act(fsg[fb_][:], ps[bg_][:, 0:TT], AF.Silu, [f"ps{bg_}", "fw"], [f"fsg{fb_}"], bias=fw[:, 6 * FK + c:6 * FK + c + 1])
                    stt(fo[b][:, s_ * SEG + t0:s_ * SEG + t0 + TT], ps[bv_][:, 0:TT], fw[:, 6 * FK + FK + c:6 * FK + FK + c + 1], fsg[fb_][:],
                        ALU.add, ALU.mult, [f"ps{bv_}", "fw", f"fsg{fb_}"], [f"fo{b}"])
            stq(actT[c * 128:(c + 1) * 128, :], fo[b][:], [f"fo{b}"])
            drip(2)
        S.barrier()

        gemm([(actT, wb_down[l])], chunks_of(0, D, "r"), 256, res_handler, res_setup, wkey=("down", l))

    if "stop_win" not in debug:
        norm_phase(0, "gf", None, final=True)
    S.emit()
    return nc


def _col(v):
    v = np.asarray(v, np.float32)
    return v.reshape(-1, 128).T


def prep_inputs(cfg, inp, n_prompt=8, n_sample=2):
    D, SEG, T, L, NH = cfg["D"], cfg["SEG"], cfg["T"], cfg["L"], cfg["NH"]
    PO = cfg["pcol_off"]
    f = lambda k: np.asarray(inp[k], np.float32)
    pcol = np.zeros((L, 128, cfg["NCOL"]), np.float32)
    prow = np.zeros((L, cfg["NROW"]), np.float32)
    p64 = np.zeros((L, 64, 4), np.float32)

    def put(l, name, v):
        c = _col(v)
        pcol[l, :, PO[name]:PO[name] + c.shape[1]] = c

    for l in range(L):
        put(l, "g1", f("norm1_g")[l])
        put(l, "g2", f("norm2_g")[l])
        put(l, "bgate", f("b_gate")[l])
        for j in range(5):
            put(l, f"scw{j}", f("ssm_conv_w")[l, j])
        put(l, "scb", f("ssm_conv_b")[l])
        put(l, "dskip", np.repeat(f("ssm_d")[l], 64))
        put(l, "ng", f("ssm_norm_g")[l])
        for j in range(3):
            put(l, f"hcw{j}", f("hy_conv_w")[l, j])
        put(l, "hcb", f("hy_conv_b")[l])
        put(l, "hbias", f("hy_bias")[l])
        put(l, "lng", f("sg_ln_g")[l])
        put(l, "lnb", f("sg_ln_b")[l])
        for j in range(3):
            put(l, f"fcw{j}", f("ffn_conv_w")[l, j])
        put(l, "fcb", f("ffn_conv_b")[l])
        put(l, "gf", f("normf_g"))
        prow[l, 0:2 * NH] = f("ssm_dt_bias")[l].reshape(-1)
        prow[l, 2 * NH:4 * NH] = f("ssm_a_log")[l].reshape(-1)
        prow[l, 4 * NH:] = f("sg_bs")[l].reshape(-1)
        p64[l, :, 0] = f("hy_b1")[l]
        p64[l, :, 1] = f("hy_b2")[l]
        p64[l, :, 2] = f("hy_b3")[l]
        p64[l, :, 3] = f("hy_freq")[l]
    hc = host_consts(cfg)
    shared = dict(w_in=f("w_in"), w_br=f("w_br"), w_out=f("w_out"), w_up=f("w_up"), w_down=f("w_down"),
                  hy_w1=f("hy_w1"), hy_w2=f("hy_w2"), hy_w3=f("hy_w3"), hy_w4=f("hy_w4"), p64=p64,
                  sg_ws=f("sg_ws"), pcol=pcol, prow=prow, cst=hc["cst"], deltas=hc["deltas"],
                  dftf=hc["dftf"], dfti=hc["dfti"])
    xp, xs = f("x_prompt"), f("x_sample")
    zp_s, tn_s = host_zpos(2 * SEG, T)
    zp_p, tn_p = host_zpos(SEG, T)
    plan = []
    for i in range(n_sample):
        plan.append(("s", i))
    rest = 8 - n_sample
    npair = n_prompt - rest
    pi = 0
    for c in range(rest):
        if c < npair:
            plan.append(("pp", pi, pi + 1))
            pi += 2
        else:
            plan.append(("p", pi))
            pi += 1
    maps = []
    for pl in plan:
        m = dict(shared)
        if pl[0] == "s":
            m["x_in"] = np.ascontiguousarray(xs[pl[1]])
            m["cpl"] = np.ones((128, 1), np.float32)
            m["zposT"], m["tneg"] = zp_s, tn_s
        else:
            x = np.zeros((T, D), np.float32)
            x[0:SEG] = xp[pl[1]]
            if pl[0] == "pp":
                x[SEG:] = xp[pl[2]]
            m["x_in"] = x
            m["cpl"] = np.zeros((128, 1), np.float32)
            m["zposT"], m["tneg"] = zp_p, tn_p
        maps.append(m)
    return maps, plan


def assemble(cfg, plan, results, n_prompt=8, n_sample=2, key="y_out"):
    D, SEG, T = cfg["D"], cfg["SEG"], cfg["T"]
    yp = np.zeros((n_prompt, SEG, D), np.float32)
    ys = np.zeros((n_sample, T, D), np.float32)
    for pl, r in zip(plan, results):
        y = np.asarray(r[key], np.float32)
        if pl[0] == "s":
            ys[pl[1]] = y
        else:
            yp[pl[1]] = y[0:SEG]
            if pl[0] == "pp":
                yp[pl[2]] = y[SEG:]
    return yp, ys


def kernel(**inputs):
    cfg = make_cfg()
    nc = build(cfg)
    maps, plan = prep_inputs(cfg, inputs)
    res = run_bass_kernel_spmd(nc, maps, core_ids=list(range(8)))
    yp, ys = assemble(cfg, plan, res.results)
    return (yp, ys)
```

```python
import math
import contextlib
import numpy as np
import ml_dtypes
import concourse.bass as bass
import concourse.mybir as mybir
from concourse.bass_utils import run_bass_kernel_spmd

F32 = mybir.dt.float32
BF16 = mybir.dt.bfloat16
AF = mybir.ActivationFunctionType
ALU = mybir.AluOpType

ENGS = ("pe", "act", "dve", "pool", "sp")
NDMASEM = 12
NEGBIG = -30000.0


class Op:
    __slots__ = ("eng", "fn", "dma", "waits", "inc", "dsem", "dval", "idx", "prewait", "dns")

    def __init__(self, eng, fn, dma):
        self.eng = eng
        self.fn = fn
        self.dma = dma
        self.waits = []
        self.inc = False
        self.dsem = None
        self.dval = 0
        self.prewait = None
        self.idx = 0
        self.dns = eng


class Sched:
    def __init__(self, nc):
        self.nc = nc
        self.ops = {e: [] for e in ENGS}
        self.last_w = {}
        self.readers = {}
        self.dma_rr = {e: 0 for e in ENGS}
        self.dma_cum = {}
        self.sbuf_off = 16640
        self.sbuf_base = 16640
        self.nbar = 0
        self.maxbar = 10 ** 9
        self.nalloc = 0

    def sb(self, name, shape, dtype):
        esz = 2 if dtype == BF16 else 4
        n = 1
        for s in shape[1:]:
            n *= s
        nbytes = (n * esz + 63) // 64 * 64
        off = self.sbuf_off
        self.sbuf_off += nbytes
        assert self.sbuf_off <= 229312, ("SBUF overflow", name, self.sbuf_off)
        self.nalloc += 1
        return self.nc.alloc_sbuf_tensor_at(f"{name}_{self.nalloc}", list(shape), dtype, offset=off)

    def sb_reset(self):
        self.sbuf_off = self.sbuf_base

    def _dep(self, op, dep):
        if dep is None or dep is op:
            return
        if dep.dma:
            op.waits.append((("d", dep.dns, dep.dsem), dep.dval))
        else:
            if dep.eng == op.eng and not op.dma and dep.eng == "pe":
                return
            dep.inc = True
            op.waits.append((("e", dep.eng), dep))

    def add(self, eng, fn, reads=(), writes=(), dma=False, ns=None):
        op = Op(eng, fn, dma)
        if ns:
            op.dns = eng + ":" + ns
        if self.nbar >= self.maxbar:
            return op
        for k in reads:
            self._dep(op, self.last_w.get(k))
            if k.startswith("ps"):
                for r in self.readers.get(k, ()):
                    if r.eng != eng:
                        self._dep(op, r)
        for k in writes:
            self._dep(op, self.last_w.get(k))
            for r in self.readers.get(k, ()):
                self._dep(op, r)
        for k in reads:
            self.readers.setdefault(k, []).append(op)
        for k in writes:
            self.last_w[k] = op
            self.readers[k] = []
        if dma:
            slot = self.dma_rr.get(op.dns, 0) % NDMASEM
            self.dma_rr[op.dns] = self.dma_rr.get(op.dns, 0) + 1
            key = (op.dns, slot)
            prev = self.dma_cum.get(key, 0)
            if prev:
                op.prewait = (("d", op.dns, slot), prev)
            op.dsem = slot
            op.dval = prev + 16
            self.dma_cum[key] = op.dval
        self.ops[eng].append(op)
        return op

    def barrier(self):
        if self.nbar >= self.maxbar:
            return
        self.nbar += 1
        nb = self.nbar
        op = Op("sp", ("bar_signal", nb), False)
        for e in ENGS:
            if e == "sp":
                continue
            for o in reversed(self.ops[e]):
                if not o.dma and not isinstance(o.fn, tuple):
                    o.inc = True
                    op.waits.append((("e", e), o))
                    break
        op.waits.extend([(("d", e, s), v) for (e, s), v in self.dma_cum.items()])
        self.ops["sp"].append(op)
        for e in ENGS:
            if e != "sp":
                self.ops[e].append(Op(e, ("bar_wait", nb), False))
        self.last_w = {}
        self.readers = {}

    def emit(self):
        nc = self.nc
        with contextlib.ExitStack() as st:
            esem = {e: st.enter_context(nc.semaphore(f"s_{e}")) for e in ENGS}
            dsem = {}
            for (dns_, s_) in sorted(self.dma_cum.keys()):
                dsem[(dns_, s_)] = st.enter_context(nc.semaphore(f"d_{dns_.replace(':', '_')}_{s_}"))
            bsem = st.enter_context(nc.semaphore("s_bar"))
            for e in ENGS:
                c = 0
                for o in self.ops[e]:
                    if o.inc and not o.dma:
                        c += 1
                        o.idx = c
            block = st.enter_context(nc.Block())

            def run(e, eng):
                seen = {}
                for o in self.ops[e]:
                    waits = list(o.waits)
                    if o.prewait:
                        waits.append(o.prewait)
                    best = {}
                    for k, v in waits:
                        if not isinstance(v, int):
                            v = v.idx
                        if v > best.get(k, 0):
                            best[k] = v
                    for k, v in best.items():
                        if seen.get(k, 0) >= v:
                            continue
                        seen[k] = v
                        if k[0] == "e":
                            eng.wait_ge(esem[k[1]], v)
                        else:
                            eng.wait_ge(dsem[(k[1], k[2])], v)
                    if isinstance(o.fn, tuple):
                        if o.fn[0] == "bar_signal":
                            eng.sem_inc(bsem, 1)
                        else:
                            eng.wait_ge(bsem, o.fn[1])
                        continue
                    ins = o.fn(eng)
                    if o.dma:
                        ins.then_inc(dsem[(o.dns, o.dsem)], 16)
                    elif o.inc:
                        ins.then_inc(esem[e], 1)

            block.tensor(lambda eng: run("pe", eng))
            block.scalar(lambda eng: run("act", eng))
            block.vector(lambda eng: run("dve", eng))
            block.gpsimd(lambda eng: run("pool", eng))
            block.sync(lambda eng: run("sp", eng))


def make_cfg(D=4096, SEG=2048, L=2, NG=8, DFF=11008):
    c = dict(D=D, SEG=SEG, T=2 * SEG, L=L, NG=NG, DFF=DFF)
    c["NH"] = D // 64
    c["HG"] = c["NH"] // NG
    c["GN"] = NG * 128
    c["XBC"] = D + 2 * c["GN"]
    c["HYW"] = D // 2
    c["SGW"] = D // 2
    c["SGG"] = c["SGW"] // 128
    c["NIN"] = D + c["XBC"] + 2 * c["NH"] + 3 * c["HYW"] + 2 * c["SGW"] + 3 * D
    off = {}
    n = 0
    for name, w in [("g1", D), ("g2", D), ("bgate", 3 * D)] + [(f"scw{j}", c["XBC"]) for j in range(5)] + \
            [("scb", c["XBC"]), ("dskip", D), ("ng", D)] + [(f"hcw{j}", 3 * c["HYW"]) for j in range(3)] + \
            [("hcb", 3 * c["HYW"]), ("hbias", c["HYW"]), ("lng", c["SGW"]), ("lnb", c["SGW"])] + \
            [(f"fcw{j}", 2 * DFF) for j in range(3)] + [("fcb", 2 * DFF), ("gf", D)]:
        off[name] = n
        n += w // 128
    c["pcol_off"] = off
    c["NCOL"] = n
    c["NROW"] = 4 * c["NH"] + c["SGG"] * 128
    return c


CST_NAMES = ["ident", "triU", "triL", "negf", "negb", "ones", "negones"]


def host_consts(cfg):
    SEG = cfg["SEG"]
    k = np.arange(128)[:, None]
    l = np.arange(128)[None, :]
    blocks = {
        "ident": (k == l), "triU": (k <= l), "triL": (k >= l),
        "negf": np.where(l >= k, 0.0, NEGBIG), "negb": np.where(l <= k, 0.0, NEGBIG),
        "ones": np.ones((128, 128)), "negones": -np.ones((128, 128)),
    }
    cst = np.concatenate([np.asarray(blocks[n], np.float32) for n in CST_NAMES] +
                         [np.where(np.arange(128) % 2 == 0, 1.0, -1.0).astype(np.float32)[:, None],
                          (np.arange(128) > 0).astype(np.float32)[:, None]], axis=1)
    N = 2 * SEG
    tau = np.arange(SEG, dtype=np.float64)
    f = np.arange(SEG, dtype=np.float64)
    ang = 2.0 * np.pi * np.outer(tau, f + 0.5) / N
    C = np.cos(ang)
    Sn = np.sin(ang)
    NF = SEG // 128
    fw = np.zeros((NF, 128, NF, 256), np.float32)
    Cr = C.reshape(NF, 128, NF, 128)
    Sr = Sn.reshape(NF, 128, NF, 128)
    fw[:, :, :, 0:128] = Cr.transpose(2, 1, 0, 3)
    fw[:, :, :, 128:256] = Sr.transpose(2, 1, 0, 3)
    TBW = min(512, SEG)
    NTB = SEG // TBW
    iv = np.zeros((NTB, 128, NF, 2, TBW), np.float32)
    Ct = (2.0 / N) * C.T.reshape(NF, 128, NTB, TBW)
    St = (2.0 / N) * Sn.T.reshape(NF, 128, NTB, TBW)
    iv[:, :, :, 0, :] = Ct.transpose(2, 1, 0, 3)
    iv[:, :, :, 1, :] = St.transpose(2, 1, 0, 3)
    HYW = cfg["HYW"]
    min_decay = math.log(1e-2) / 1.5
    max_decay = math.log(1e-2) / 0.3
    deltas = np.abs(np.linspace(min_decay, max_decay, HYW, dtype=np.float32)).astype(np.float32)[None, :]
    return dict(cst=cst, dftf=fw.astype(ml_dtypes.bfloat16), dfti=iv.astype(ml_dtypes.bfloat16), deltas=deltas)


def host_zpos(n, TF):
    t = np.linspace(0.0, 1.0, n, dtype=np.float32)[:, None]
    bands = 16
    wpos = (np.float32(2.0 * math.pi / n) * np.arange(n, dtype=np.float32))[:, None]
    fr = np.linspace(1e-4, bands - 1, bands, dtype=np.float32)[None, :]
    z = np.concatenate([t, np.cos(fr * wpos), -np.sin(fr * wpos)], axis=-1).astype(np.float32)
    zp = np.zeros((TF, 33), np.float32)
    zp[:n] = z
    tt = np.zeros((TF,), np.float32)
    tt[:n] = t[:, 0]
    tneg = (-tt).reshape(TF // 128, 128).T.copy()
    return zp.T.copy(), tneg


def build(cfg, debug=()):
    D, SEG, T, L, NG, DFF = (cfg[k] for k in ("D", "SEG", "T", "L", "NG", "DFF"))
    NH, HG, GN, XBC, HYW, SGW, SGG, NIN = (cfg[k] for k in ("NH", "HG", "GN", "XBC", "HYW", "SGW", "SGG", "NIN"))
    NCOL, NROW, PO = cfg["NCOL"], cfg["NROW"], cfg["pcol_off"]
    KD = D // 128
    NCH = T // 128
    SCH = SEG // 128
    TT = min(512, SEG)
    NTT = T // TT
    NF = SEG // 128
    TBW = min(512, SEG)
    NTB = SEG // TBW
    NCST = 128 * len(CST_NAMES) + 2

    nc = bass.Bass("TRN2", target_bir_lowering=False)
    S = Sched(nc)
    for d_ in debug:
        if d_.startswith("maxbar="):
            S.maxbar = int(d_[7:])

    def din(name, shape, dt=F32):
        return nc.dram_tensor(name, list(shape), dt, kind="ExternalInput").ap()

    def dscr(name, shape, dt=BF16):
        kind = "ExternalOutput" if name in debug else "Internal"
        return nc.dram_tensor(name, list(shape), dt, kind=kind).ap()

    x_in = din("x_in", [T, D])
    w_in = din("w_in", [L, D, NIN])
    w_br = din("w_br", [L, 2 * D, D])
    w_out = din("w_out", [L, D, D])
    w_up = din("w_up", [L, D, 2 * DFF])
    w_down = din("w_down", [L, DFF, D])
    hy_w1 = din("hy_w1", [L, 33, 64])
    hy_w2 = din("hy_w2", [L, 64, 64])
    hy_w3 = din("hy_w3", [L, 64, 64])
    hy_w4 = din("hy_w4", [L, 64, 2 * HYW])
    p64 = din("p64", [L, 64, 4])
    sg_ws = din("sg_ws", [L, SGG, 128, 128])
    pcol = din("pcol", [L, 128, NCOL])
    prow = din("prow", [L, NROW])
    cst_d = din("cst", [128, NCST])
    cpl_d = din("cpl", [128, 1])
    zpos_d = din("zposT", [33, T])
    tneg_d = din("tneg", [128, T // 128])
    deltas_d = din("deltas", [1, HYW])
    dftf_d = din("dftf", [NF, 128, NF, 256], BF16)
    dfti_d = din("dfti", [NTB, 128, NF, 2, TBW], BF16)
    y_out = nc.dram_tensor("y_out", [T, D], F32, kind="ExternalOutput").ap()

    def bigw(name, K_, N_):
        out = []
        for l_ in range(L):
            maxrows = max(128, (200 * 2 ** 20 // (N_ * 2)) // 128 * 128)
            pieces = []
            r0 = 0
            while r0 < K_:
                r1 = min(K_, r0 + maxrows)
                pieces.append((r0, r1, dscr(f"{name}_{l_}_{r0}", [r1 - r0, N_])))
                r0 = r1
            out.append(pieces)
        return out

    wb_in = bigw("wb_in", D, NIN)
    wb_br = bigw("wb_br", 2 * D, D)
    wb_out = bigw("wb_out", D, D)
    wb_up = bigw("wb_up", D, 2 * DFF)
    wb_down = bigw("wb_down", DFF, D)

    def wrows(pieces, a, b):
        out = []
        for (r0, r1, ap) in pieces:
            lo, hi = max(a, r0), min(b, r1)
            if lo < hi:
                out.append((lo - a, hi - a, ap[lo - r0:hi - r0, :]))
        return out
    xT = dscr("xT", [D, T], F32)
    hT = dscr("hT", [D, T])
    pz = dscr("pz", [D, T])
    pxbc = dscr("pxbc", [XBC, T])
    pdt = dscr("pdt", [2 * NH, T], F32)
    phy = dscr("phy", [3 * HYW, T])
    psg = dscr("psg", [2 * SGW, T])
    pgate = dscr("pgate", [3 * D, T])
    xcT = dscr("xcT", [XBC, T])
    x_tok = dscr("x_tok", [T, D])
    b_tok = dscr("b_tok", [T, GN])
    hb_d = dscr("hb_d", [NCH, 128, D])
    ycat = dscr("ycat", [2 * D, T])
    k_tok = dscr("k_tok", [T, 2 * HYW])
    w_tokd = dscr("w_tokd", [T, HYW])
    w_fm = dscr("w_fm", [HYW, T])
    x0_fm = dscr("x0_fm", [HYW, T])
    mergedT = dscr("mergedT", [D, T])
    upT = dscr("upT", [2 * DFF, T])
    actT = dscr("actT", [DFF, T])

    ps = [nc.alloc_psum_tensor(f"psb{i}", [128, 512], F32) for i in range(8)]

    def mm(out, lhsT, rhs, start, stop, r, w):
        S.add("pe", lambda e: e.matmul(out, lhsT=lhsT, rhs=rhs, start=start, stop=stop), r, w)

    def tr(out, in_, ident, r, w):
        S.add("pe", lambda e: e.transpose(out=out, in_=in_, identity=ident), r, w)

    def act(out, in_, func, r, w, bias=None, scale=None):
        kw = {}
        if bias is not None:
            kw["bias"] = bias
        if scale is not None:
            kw["scale"] = scale
        S.add("act", lambda e: e.activation(out=out, in_=in_, func=func, **kw), r, w)

    def tt(eng, out, in0, in1, op, r, w):
        S.add(eng, lambda e: e.tensor_tensor(out=out, in0=in0, in1=in1, op=op), r, w)

    def ts(eng, out, in0, s1, s2, op0, op1, r, w):
        if op1 is None:
            S.add(eng, lambda e: e.tensor_scalar(out=out, in0=in0, scalar1=s1, scalar2=None, op0=op0), r, w)
        else:
            S.add(eng, lambda e: e.tensor_scalar(out=out, in0=in0, scalar1=s1, scalar2=s2, op0=op0, op1=op1), r, w)

    def stt(out, in0, scalar, in1, op0, op1, r, w):
        S.add("dve", lambda e: e.scalar_tensor_tensor(out=out, in0=in0, scalar=scalar, in1=in1, op0=op0, op1=op1), r, w)

    def cp(eng, out, in_, r, w):
        if eng == "act":
            S.add("act", lambda e: e.copy(out=out, in_=in_), r, w)
        else:
            S.add(eng, lambda e: e.tensor_copy(out=out, in_=in_), r, w)

    def recip(out, in_, r, w):
        S.add("dve", lambda e: e.reciprocal(out=out, in_=in_), r, w)

    def mset(eng, ap, v, r, w):
        S.add(eng, lambda e: e.memset(ap, v), r, w)

    def ld(out, in_, w, r=()):
        S.add("sp", lambda e: e.dma_start(out=out, in_=in_), r, w, dma=True)

    def stq(out, in_, r, w=()):
        S.add("pool", lambda e: e.dma_start(out=out, in_=in_), r, w, dma=True)

    cst = S.sb("cst", [128, NCST], F32)
    cplc = S.sb("cplc", [128, 1], F32)
    identb = S.sb("identb", [128, 128], BF16)
    onesb = S.sb("onesb", [128, 128], BF16)
    ld(cst[:], cst_d, ["cst"])
    ld(cplc[:], cpl_d, ["cplc"])

    def C_(name):
        i = CST_NAMES.index(name)
        return cst[:, i * 128:(i + 1) * 128]
    sgn = cst[:, NCST - 2:NCST - 1]
    m0 = cst[:, NCST - 1:NCST]
    cp("dve", identb[:], C_("ident"), ["cst"], ["identb"])
    cp("dve", onesb[:], C_("ones"), ["cst"], ["onesb"])
    S.sbuf_base = S.sbuf_off
    CK = ["cst", "cplc", "identb", "onesb"]

    def rearm():
        pass

    cast_groups = {}
    cast_keys = {}
    WSRC = {"in": (w_in, wb_in, D, NIN), "br": (w_br, wb_br, 2 * D, D), "out": (w_out, wb_out, D, D),
            "up": (w_up, wb_up, D, 2 * DFF), "down": (w_down, wb_down, DFF, D)}
    for wn, (src, dst, K_, N_) in WSRC.items():
        for l in range(L):
            fns, keys = [], []
            for (p0, p1, pap) in dst[l]:
                for r0 in range(p0, p1, 128):
                    for c0 in range(0, N_, 8192):
                        c1 = min(N_, c0 + 8192)
                        key = f"cw_{wn}_{l}_{r0}_{c0}"
                        keys.append(key)
                        fns.append((key, lambda e, src=src, pap=pap, l=l, r0=r0, p0=p0, c0=c0, c1=c1:
                                    e.dma_start(out=pap[r0 - p0:r0 - p0 + 128, c0:c1], in_=src[l, r0:r0 + 128, c0:c1])))
            cast_groups[(wn, l)] = fns
            cast_keys[(wn, l)] = keys
    cast_order = []
    for l in range(L):
        for wn in ("in", "br", "out", "up", "down"):
            cast_order.append((wn, l))
    cast_pending = []
    for g_ in cast_order:
        for kf in cast_groups[g_]:
            cast_pending.append((g_, kf))
    cast_pos = [0]
    drip_cnt = [0]

    def cast_one():
        if cast_pos[0] < len(cast_pending):
            _, (key, fn) = cast_pending[cast_pos[0]]
            cast_pos[0] += 1
            S.add("pool", fn, (), [key], dma=True, ns="c")

    def drip(every=1):
        drip_cnt[0] += 1
        if drip_cnt[0] % every == 0:
            cast_one()

    def cast_until(g_):
        idx = cast_order.index(g_)
        while cast_pos[0] < len(cast_pending) and cast_order.index(cast_pending[cast_pos[0]][0]) <= idx:
            cast_one()

    cast_until(("in", 0))

    S.sb_reset()
    xin_t = [S.sb("xin", [128, D], F32) for _ in range(2)]
    xst = S.sb("xst", [128, KD, TT], F32)
    for tt_i in range(NTT):
        for tb in range(TT // 128):
            tok0 = tt_i * TT + tb * 128
            b = (tt_i * (TT // 128) + tb) % 2
            ld(xin_t[b][:], x_in[tok0:tok0 + 128, :], [f"xin{b}"])
            for k4 in range(0, KD, 4):
                bank = (k4 // 4) % 4
                nk = min(4, KD - k4)
                for k in range(k4, k4 + nk):
                    tr(ps[bank][:, (k - k4) * 128:(k - k4 + 1) * 128], xin_t[b][:, k * 128:(k + 1) * 128], C_("ident"),
                       [f"xin{b}", "cst"], [f"ps{bank}"])
                eng = "act" if (k4 // 4) % 2 else "dve"
                cp(eng, xst[:, k4:k4 + nk, tb * 128:(tb + 1) * 128],
                   ps[bank][:, 0:nk * 128].rearrange("p (a b) -> p a b", b=128), [f"ps{bank}"], ["xst"])
        stq(xT.rearrange("(k p) t -> p k t", p=128)[:, :, tt_i * TT:(tt_i + 1) * TT], xst[:], ["xst"])
    S.barrier()

    def norm_phase(l, gname, dst, final=False):
        S.sb_reset()
        pc = S.sb("pc", [128, KD], F32)
        ld(pc[:], pcol[l, :, PO[gname]:PO[gname] + KD], ["pc"])
        xt = [S.sb("xt", [128, KD, TT], F32) for _ in range(2)]
        sq = [S.sb("sq", [128, TT], F32) for _ in range(2)]
        rs = S.sb("rs", [128, TT], F32)
        if not final:
            ho = S.sb("ho", [128, KD, TT], BF16)
        else:
            hf = [S.sb("hf", [128, TT], F32) for _ in range(2)]
            yt = [S.sb("yt", [128, D], F32) for _ in range(2)]
        xTv = xT.rearrange("(k p) t -> p k t", p=128)
        for ti in range(NTT):
            b = ti % 2
            ld(xt[b][:], xTv[:, :, ti * TT:(ti + 1) * TT], [f"xt{b}"])
            for k in range(KD):
                act(sq[k % 2][:], xt[b][:, k, :], AF.Square, [f"xt{b}"], [f"sq{k % 2}"])
                mm(ps[0][:, 0:TT], C_("ones"), sq[k % 2][:], k == 0, k == KD - 1, [f"sq{k % 2}", "cst"], ["ps0"])
            ts("dve", rs[:], ps[0][:, 0:TT], 1.0 / D, 1e-6, ALU.mult, ALU.add, ["ps0"], ["rs"])
            act(rs[:], rs[:], AF.Sqrt, ["rs"], ["rs"])
            recip(rs[:], rs[:], ["rs"], ["rs"])
            if not final:
                for k in range(KD):
                    stt(ho[:, k, :], xt[b][:, k, :], pc[:, k:k + 1], rs[:], ALU.mult, ALU.mult, [f"xt{b}", "pc", "rs"], ["ho"])
                stq(dst.rearrange("(k p) t -> p k t", p=128)[:, :, ti * TT:(ti + 1) * TT], ho[:], ["ho"])
            else:
                for tb in range(TT // 128):
                    yb = (ti * (TT // 128) + tb) % 2
                    for k in range(KD):
                        hb_ = k % 2
                        bank = 1 + (k // 4) % 4
                        if tb == 0 or True:
                            stt(hf[hb_][:, 0:128], xt[b][:, k, tb * 128:(tb + 1) * 128], pc[:, k:k + 1],
                                rs[:, tb * 128:(tb + 1) * 128], ALU.mult, ALU.mult, [f"xt{b}", "pc", "rs"], [f"hf{hb_}"])
                        tr(ps[bank][:, (k % 4) * 128:(k % 4 + 1) * 128], hf[hb_][:, 0:128], C_("ident"),
                           [f"hf{hb_}", "cst"], [f"ps{bank}"])
                        if k % 4 == 3 or k == KD - 1:
                            k4 = k - (k % 4)
                            nk = k - k4 + 1
                            cp("act", yt[yb][:, k4 * 128:(k4 + nk) * 128], ps[bank][:, 0:nk * 128], [f"ps{bank}"], [f"yt{yb}"])
                    tok0 = ti * TT + tb * 128
                    stq(y_out[tok0:tok0 + 128, :], yt[yb][:], [f"yt{yb}"])
        S.barrier()

    def gemm(groups, chunks, SW, handler, setup=None, wkey=None):
        S.sb_reset()
        cast_until(wkey)
        wk_ = cast_keys[wkey]
        extra = setup() if setup else None
        KCs = [a.shape[0] // 128 for a, _ in groups]
        ng = len(groups)
        at = [S.sb("gact", [128, kc, TT], BF16) for kc in KCs]
        wsl = [[S.sb("gw", [128, kc, SW], BF16) for kc in KCs] for _ in range(2)]
        slabs = []
        cur = []
        for ch in chunks:
            if cur and (ch[0] + ch[1] - cur[0][0] > SW or ch[0] != cur[-1][0] + cur[-1][1]):
                slabs.append(cur)
                cur = []
            cur.append(ch)
        if cur:
            slabs.append(cur)
        nb = 6 // ng
        cnt = 0
        si = 0
        for ti in range(NTT):
            for g, (a, _) in enumerate(groups):
                ld(at[g][:], a.rearrange("(k p) t -> p k t", p=128)[:, :, ti * TT:(ti + 1) * TT], [f"gact{g}"])
            for slab in slabs:
                sb_ = si % 2
                si += 1
                s0 = slab[0][0]
                s1 = slab[-1][0] + slab[-1][1]
                for g, (_, wd) in enumerate(groups):
                    for pi_, (r0, r1, pap) in enumerate(wd):
                        ld(wsl[sb_][g][:, r0 // 128:r1 // 128, 0:s1 - s0], pap.rearrange("(k p) n -> p k n", p=128)[:, :, s0:s1],
                           [f"gw{sb_}_{g}_{pi_}"], wk_)
                for (c0, m, tag) in slab:
                    bset = cnt % nb
                    cnt += 1
                    pss = []
                    for g in range(ng):
                        bank = bset * ng + g
                        for kc in range(KCs[g]):
                            pi_ = [i_ for i_, (r0, r1, _) in enumerate(groups[g][1]) if r0 <= kc * 128 < r1][0]
                            mm(ps[bank][0:m, 0:TT], wsl[sb_][g][:, kc, c0 - s0:c0 - s0 + m], at[g][:, kc, :],
                               kc == 0, kc == KCs[g] - 1, [f"gw{sb_}_{g}_{pi_}", f"gact{g}"], [f"ps{bank}"])
                        pss.append((ps[bank][0:m, 0:TT], f"ps{bank}"))
                    handler(tag, c0, m, ti, pss, extra, cnt)
        S.barrier()

    def chunks_of(c0, c1, tag):
        out = []
        c = c0
        while c < c1:
            m = min(128, c1 - c)
            out.append((c, m, tag))
            c += m
        return out

    def sgu_phase(l):
        S.sb_reset()
        SK = SGW // 128
        lg = S.sb("lg", [128, 2 * SK], F32)
        ld(lg[:], pcol[l, :, PO["lng"]:PO["lng"] + 2 * SK], ["lg"])
        bsb = S.sb("bsb", [128, SGG * 128], F32)
        ld(bsb[:], prow[l, 4 * NH:4 * NH + SGG * 128].partition_broadcast(128), ["bsb"])
        wsf = S.sb("wsf", [128, SGG, 128], F32)
        ld(wsf[:], sg_ws[l].rearrange("g t s -> t g s"), ["wsf"])
        wsT = S.sb("wsT", [128, SGG, 128], BF16)
        for g0 in range(0, SGG, 4):
            n4 = min(4, SGG - g0)
            for g in range(g0, g0 + n4):
                tr(ps[0][:, (g - g0) * 128:(g - g0 + 1) * 128], wsf[:, g, :], C_("ident"), ["wsf", "cst"], ["ps0"])
            cp("dve", wsT[:, g0:g0 + n4, :], ps[0][:, 0:n4 * 128].rearrange("p (a b) -> p a b", b=128), ["ps0"], ["wsT"])
        vv = S.sb("vv", [128, SK, TT], BF16)
        uu = S.sb("uu", [128, SK, TT], BF16)
        vn = S.sb("vn", [128, SK, TT], BF16)
        yo = S.sb("yo", [128, SK, TT], BF16)
        sq = [S.sb("sq", [128, TT], F32) for _ in range(2)]
        mean = S.sb("mean", [128, TT], F32)
        msq = S.sb("msq", [128, TT], F32)
        rstd = S.sb("rstd", [128, TT], F32)
        d1 = [S.sb("d1", [128, TT], F32) for _ in range(2)]
        vtk = [S.sb("vtk", [128, 128], BF16) for _ in range(4)]
        mx = [S.sb("mx", [128, TT], F32) for _ in range(2)]
        NTB_ = TT // 128
        cnt = 0
        for ti in range(NTT):
            tsl = slice(ti * TT, (ti + 1) * TT)
            ld(uu[:], psg[0:SGW, :].rearrange("(k p) t -> p k t", p=128)[:, :, tsl], ["uu"])
            ld(vv[:], psg[SGW:2 * SGW, :].rearrange("(k p) t -> p k t", p=128)[:, :, tsl], ["vv"])
            for k in range(SK):
                mm(ps[0][:, 0:TT], onesb[:], vv[:, k, :], k == 0, k == SK - 1, ["onesb", "vv"], ["ps0"])
            for k in range(SK):
                act(sq[k % 2][:], vv[:, k, :], AF.Square, ["vv"], [f"sq{k % 2}"])
                mm(ps[1][:, 0:TT], C_("ones"), sq[k % 2][:], k == 0, k == SK - 1, ["cst", f"sq{k % 2}"], ["ps1"])
            ts("dve", mean[:], ps[0][:, 0:TT], 1.0 / SGW, None, ALU.mult, None, ["ps0"], ["mean"])
            tt("dve", msq[:], mean[:], mean[:], ALU.mult, ["mean"], ["msq"])
            stt(rstd[:], ps[1][:, 0:TT], 1.0 / SGW, msq[:], ALU.mult, ALU.subtract, ["ps1", "msq"], ["rstd"])
            ts("dve", rstd[:], rstd[:], 1e-5, None, ALU.add, None, ["rstd"], ["rstd"])
            act(rstd[:], rstd[:], AF.Sqrt, ["rstd"], ["rstd"])
            recip(rstd[:], rstd[:], ["rstd"], ["rstd"])
            for k in range(SK):
                b = k % 2
                tt("dve", d1[b][:], vv[:, k, :], mean[:], ALU.subtract, ["vv", "mean"], [f"d1{b}"])
                tt("dve", d1[b][:], d1[b][:], rstd[:], ALU.mult, [f"d1{b}", "rstd"], [f"d1{b}"])
                ts("dve", vn[:, k, :], d1[b][:], lg[:, k:k + 1], lg[:, SK + k:SK + k + 1], ALU.mult, ALU.add, [f"d1{b}", "lg"], ["vn"])
            for k in range(SK):
                bank = 2 + k % 2
                pbank = 4 + k % 2
                pst = ps[pbank][:].bitcast(BF16)
                for tb in range(NTB_):
                    vb = cnt % 4
                    cnt += 1
                    tr(pst[:, tb * 128:(tb + 1) * 128], vn[:, k, tb * 128:(tb + 1) * 128], identb[:], ["vn", "identb"], [f"ps{pbank}"])
                    cp("act", vtk[vb][:], pst[:, tb * 128:(tb + 1) * 128], [f"ps{pbank}"], [f"vtk{vb}"])
                    mm(ps[bank][:, tb * 128:(tb + 1) * 128], vtk[vb][:], wsT[:, k, :], True, True, [f"vtk{vb}", "wsT"], [f"ps{bank}"])
                b = k % 2
                tt("dve", mx[b][:].rearrange("p (a b) -> p a b", b=128), ps[bank][:, 0:TT].rearrange("p (a b) -> p a b", b=128),
                   bsb[:, k * 128:(k + 1) * 128].unsqueeze(1).broadcast_to([128, NTB_, 128]), ALU.add, [f"ps{bank}", "bsb"], [f"mx{b}"])
                tt("dve", yo[:, k, :], mx[b][:], uu[:, k, :], ALU.mult, [f"mx{b}", "uu"], ["yo"])
            stq(ycat[D + HYW:2 * D, :].rearrange("(k p) t -> p k t", p=128)[:, :, tsl], yo[:], ["yo"])
        S.barrier()

    def hyena_phase(l):
        HK = HYW // 128
        S.sb_reset()
        kb0 = S.sb("kb0", [128, HK], F32)
        base_mark = S.sbuf_off
        zp = S.sb("zp", [33, T], F32)
        ld(zp[:], zpos_d, ["zp"])
        w1 = S.sb("w1", [33, 64], F32)
        w2 = S.sb("w2", [64, 64], F32)
        w3 = S.sb("w3", [64, 64], F32)
        w4 = S.sb("w4", [64, 2 * HYW], F32)
        pp = S.sb("pp", [64, 4], F32)
        fb = S.sb("fb", [64, 3], F32)
        ld(w1[:], hy_w1[l], ["w1"])
        ld(w2[:], hy_w2[l], ["w2"])
        ld(w3[:], hy_w3[l], ["w3"])
        ld(w4[:], hy_w4[l], ["w4"])
        ld(pp[:], p64[l], ["pp"])
        ld(kb0[:], pcol[l, :, PO["hbias"]:PO["hbias"] + HK], ["kb0"])
        ts("dve", fb[:], pp[:, 0:3], pp[:, 3:4], None, ALU.mult, None, ["pp"], ["fb"])
        tng = S.sb("tng", [128, NCH], F32)
        ld(tng[:], tneg_d, ["tng"])
        dl = S.sb("dl", [128, HYW], F32)
        ld(dl[:], deltas_d[0, :].partition_broadcast(128), ["dl"])
        hbuf = [S.sb("hh", [64, T], F32) for _ in range(2)]
        arg = [S.sb("arg", [64, 512], F32) for _ in range(2)]
        ar2 = [S.sb("ar2", [64, 512], F32) for _ in range(2)]
        MAGIC = 12582912.0
        INV2PI = 1.0 / (2.0 * math.pi)
        PIS = 3.141592
        CW = min(512, T)
        cnt = 0
        cur, curk, curK = zp, "zp", 33
        for i, (W, wk) in enumerate(((w1, "w1"), (w2, "w2"), (w3, "w3"))):
            ob = hbuf[i % 2]
            for c0 in range(0, T, CW):
                b = cnt % 2
                bank = cnt % 2
                cnt += 1
                mm(ps[bank][0:64, 0:CW], W[0:curK, :], cur[0:curK, c0:c0 + CW], True, True, [wk, curk], [f"ps{bank}"])
                act(arg[b][:, 0:CW], ps[bank][0:64, 0:CW], AF.Identity, [f"ps{bank}", "pp", "fb"], [f"arg{b}"],
                    bias=fb[:, i:i + 1], scale=pp[:, 3:4])
                ts("dve", ar2[b][:, 0:CW], arg[b][:, 0:CW], INV2PI, MAGIC, ALU.mult, ALU.add, [f"arg{b}"], [f"ar2{b}"])
                ts("dve", ar2[b][:, 0:CW], ar2[b][:, 0:CW], -MAGIC, None, ALU.add, None, [f"ar2{b}"], [f"ar2{b}"])
                stt(arg[b][:, 0:CW], ar2[b][:, 0:CW], -2.0 * math.pi, arg[b][:, 0:CW], ALU.mult, ALU.add, [f"ar2{b}", f"arg{b}"], [f"arg{b}"])
                ts("dve", arg[b][:, 0:CW], arg[b][:, 0:CW], PIS, -PIS, ALU.min, ALU.max, [f"arg{b}"], [f"arg{b}"])
                act(ob[:, c0:c0 + CW], arg[b][:, 0:CW], AF.Sin, [f"arg{b}"], [f"hh{i % 2}"])
            cur, curk, curK = ob, f"hh{i % 2}", 64
        h3, h3k = cur, curk
        for k in range(HK):
            mm(ps[2][:, 2 * k:2 * k + 2], w4[:, k * 128:(k + 1) * 128], h3[:, 0:2], True, True, ["w4", h3k], ["ps2"])
        tt("dve", kb0[:], kb0[:], ps[2][:, 0:2 * HK].rearrange("p (k two) -> p k two", two=2)[:, :, 0], ALU.add, ["kb0", "ps2"], ["kb0"])
        winf = [S.sb("winf", [128, HYW], F32) for _ in range(2)]
        kt = [S.sb("kt", [128, 2 * HYW], BF16) for _ in range(2)]
        cnt = 0
        for pc in range(NCH):
            b = pc % 2
            act(winf[b][:], dl[:], AF.Exp, ["dl", "tng"], [f"winf{b}"], scale=tng[:, pc:pc + 1])
            if pc == 0:
                ts("dve", winf[b][:], winf[b][:], m0, None, ALU.mult, None, [f"winf{b}", "cst"], [f"winf{b}"])
            for half in range(2):
                for c0 in range(0, HYW, 512):
                    cw_ = min(512, HYW - c0)
                    bank = 3 + cnt % 4
                    cnt += 1
                    mm(ps[bank][:, 0:cw_], h3[:, pc * 128:(pc + 1) * 128], w4[:, half * HYW + c0:half * HYW + c0 + cw_], True, True,
                       [h3k, "w4"], [f"ps{bank}"])
                    tt("dve", kt[b][:, half * HYW + c0:half * HYW + c0 + cw_], ps[bank][:, 0:cw_], winf[b][:, c0:c0 + cw_], ALU.mult,
                       [f"ps{bank}", f"winf{b}"], [f"kt{b}"])
            stq(k_tok[pc * 128:(pc + 1) * 128, :], kt[b][:], [f"kt{b}"])
        S.barrier()

        S.sbuf_off = base_mark
        hw_ = S.sb("hw", [128, 12 * HK], F32)
        ld(hw_[:], pcol[l, :, PO["hcw0"]:PO["hcw0"] + 12 * HK], ["hw"])
        hin = [S.sb("hin", [128, 2, SEG + 2], BF16) for _ in range(3)]
        hacc = [S.sb("hacc", [128, 2, SEG], F32) for _ in range(3)]
        wbf = [S.sb("wbf", [128, T], BF16) for _ in range(2)]
        x0b = [S.sb("x0b", [128, T], BF16) for _ in range(2)]
        wtk = [S.sb("wtk", [128, 8, 128], BF16) for _ in range(2)]
        for p_ in range(3):
            mset("dve", hin[p_][:, 0, 0:1], 0.0, [], [f"hin{p_}"])
            mset("dve", hin[p_][:, 1, SEG + 1:SEG + 2], 0.0, [], [f"hin{p_}"])
        tcnt = 0
        for j in range(HK):
            b = j % 2
            for p_ in range(3):
                cc = p_ * HK + j
                ki, ka = f"hin{p_}", f"hacc{p_}"
                ld(hin[p_][:, :, 1:SEG + 1], phy[cc * 128:(cc + 1) * 128, :].rearrange("p (s t) -> p s t", s=2), [ki])
                ts("dve", hin[p_][:, 0, SEG + 1:SEG + 2], hin[p_][:, 1, 1:2], cplc[:, 0:1], None, ALU.mult, None, [ki, "cplc"], [ki])
                ts("dve", hin[p_][:, 1, 0:1], hin[p_][:, 0, SEG:SEG + 1], cplc[:, 0:1], None, ALU.mult, None, [ki, "cplc"], [ki])
                ts("dve", hacc[p_][:], hin[p_][:, :, 0:SEG], hw_[:, cc:cc + 1], hw_[:, 9 * HK + cc:9 * HK + cc + 1], ALU.mult, ALU.add,
                   [ki, "hw"], [ka])
                for jj in range(1, 3):
                    stt(hacc[p_][:], hin[p_][:, :, jj:jj + SEG], hw_[:, jj * 3 * HK + cc:jj * 3 * HK + cc + 1], hacc[p_][:],
                        ALU.mult, ALU.add, [ki, "hw", ka], [ka])
            tt("dve", wbf[b][:].rearrange("p (s t) -> p s t", s=2), hacc[2][:], hacc[1][:], ALU.mult, ["hacc2", "hacc1"], [f"wbf{b}"])
            cp("act", x0b[b][:].rearrange("p (s t) -> p s t", s=2), hacc[0][:], ["hacc0"], [f"x0b{b}"])
            stq(w_fm[j * 128:(j + 1) * 128, :], wbf[b][:], [f"wbf{b}"])
            stq(x0_fm[j * 128:(j + 1) * 128, :], x0b[b][:], [f"x0b{b}"])
            for t8 in range(0, NCH, 8):
                n8 = min(8, NCH - t8)
                bank = 2 + tcnt % 2
                kb_ = tcnt % 2
                tcnt += 1
                pst = ps[bank][:].bitcast(BF16)
                for i in range(n8):
                    tr(pst[:, i * 128:(i + 1) * 128], wbf[b][:, (t8 + i) * 128:(t8 + i + 1) * 128], identb[:],
                       [f"wbf{b}", "identb"], [f"ps{bank}"])
                cp("act", wtk[kb_][:, 0:n8, :], pst[:, 0:n8 * 128].rearrange("p (a b) -> p a b", b=128), [f"ps{bank}"], [f"wtk{kb_}"])
                stq(w_tokd[t8 * 128:(t8 + n8) * 128, j * 128:(j + 1) * 128].rearrange("(a p) c -> p a c", p=128), wtk[kb_][:, 0:n8, :], [f"wtk{kb_}"])
        S.barrier()

        S.sbuf_off = base_mark
        CB = 256 if HYW >= 256 else HYW
        NCB = HYW // CB
        seq = [S.sb("seq", [128, NF, CB], BF16) for _ in range(6)]
        fsl = [S.sb("fsl", [128, NF, 256], BF16) for _ in range(2)]
        pq = [S.sb("pq", [128, 2 * CB], F32) for _ in range(6)]
        pqc = [S.sb("pqc", [128, 2 * CB], F32) for _ in range(2)]
        kf_ = [S.sb("kf", [128, CB], F32) for _ in range(6)]
        ya = [S.sb("ya", [128, CB], F32) for _ in range(2)]
        Yt = [S.sb("Y", [128, NF, CB], BF16) for _ in range(4)]
        gsl = S.sb("gsl", [128, NF, 2, TBW], BF16)
        xw = [[S.sb("xw", [128, TBW], BF16) for _ in range(2)] for _ in range(2)]
        yf = [S.sb("yf", [128, TBW], F32) for _ in range(2)]
        yb_ = [S.sb("yb", [128, TBW], BF16) for _ in range(2)]
        fcnt = 0
        ocnt = 0
        for cb in range(NCB):
            c0 = cb * CB
            srcs = [w_tokd[0:SEG, c0:c0 + CB], w_tokd[SEG:2 * SEG, c0:c0 + CB],
                    k_tok[0:SEG, c0:c0 + CB], k_tok[SEG:2 * SEG, c0:c0 + CB],
                    k_tok[0:SEG, HYW + c0:HYW + c0 + CB], k_tok[SEG:2 * SEG, HYW + c0:HYW + c0 + CB]]
            for s_ in range(6):
                ld(seq[s_][:], srcs[s_].rearrange("(tc p) c -> p tc c", p=128), [f"seq{s_}"])
            for j in range(NF):
                fbuf = fcnt % 2
                fcnt += 1
                drip(1)
                ld(fsl[fbuf][:], dftf_d[j], [f"fsl{fbuf}"])
                for s_ in range(6):
                    for part in range(2):
                        for tc in range(NF):
                            mm(ps[s_][:, part * CB:(part + 1) * CB], fsl[fbuf][:, tc, part * 128:(part + 1) * 128], seq[s_][:, tc, :],
                               tc == 0, tc == NF - 1, [f"fsl{fbuf}", f"seq{s_}"], [f"ps{s_}"])
                    cp("act" if s_ % 2 else "dve", pq[s_][:], ps[s_][:, 0:2 * CB], [f"ps{s_}"], [f"pq{s_}"])
                    if s_ < 2:
                        act(pqc[s_][:], ps[s_][:, 0:2 * CB], AF.Copy, [f"ps{s_}", "cplc"], [f"pqc{s_}"], scale=cplc[:, 0:1])
                P = lambda s_: pq[s_][:, 0:CB]
                Q = lambda s_: pq[s_][:, CB:2 * CB]
                KF = ["kf0", "kf1", "kf2", "kf3", "kf4", "kf5"]
                tt("dve", kf_[0][:], P(2), P(4), ALU.add, ["pq2", "pq4"], ["kf0"])
                tt("dve", kf_[1][:], Q(4), Q(2), ALU.subtract, ["pq2", "pq4"], ["kf1"])
                stt(kf_[2][:], Q(2), sgn, P(3), ALU.mult, ALU.add, ["pq2", "pq3", "cst"], ["kf2"])
                stt(kf_[3][:], P(2), sgn, Q(3), ALU.mult, ALU.subtract, ["pq2", "pq3", "cst"], ["kf3"])
                stt(kf_[4][:], Q(4), sgn, P(5), ALU.mult, ALU.add, ["pq4", "pq5", "cst"], ["kf4"])
                stt(kf_[5][:], P(4), sgn, Q(5), ALU.mult, ALU.subtract, ["pq4", "pq5", "cst"], ["kf5"])
                Pc = lambda s_: pqc[s_][:, 0:CB]
                Qc = lambda s_: pqc[s_][:, CB:2 * CB]

                def combo(dst, terms, rk):
                    first = True
                    for (sg_, a, b_) in terms:
                        if first:
                            tt("dve", ya[0][:], a, b_, ALU.mult, rk, ["ya0"])
                            if sg_ < 0:
                                ts("dve", ya[0][:], ya[0][:], -1.0, None, ALU.mult, None, ["ya0"], ["ya0"])
                            first = False
                        else:
                            tt("dve", ya[1][:], a, b_, ALU.mult, rk, ["ya1"])
                            tt("dve", ya[0][:], ya[0][:], ya[1][:], ALU.add if sg_ > 0 else ALU.subtract, ["ya0", "ya1"], ["ya0"])
                    cp("act", dst, ya[0][:], ["ya0"], ["Y"])
                rk = ["pq0", "pq1", "pqc0", "pqc1"] + KF
                combo(Yt[0][:, j, :], [(1, kf_[0][:], P(0)), (1, kf_[1][:], Q(0)), (1, kf_[4][:], Pc(1)), (-1, kf_[5][:], Qc(1))], rk)
                combo(Yt[1][:, j, :], [(1, kf_[0][:], Q(0)), (-1, kf_[1][:], P(0)), (1, kf_[4][:], Qc(1)), (1, kf_[5][:], Pc(1))], rk)
                combo(Yt[2][:, j, :], [(1, kf_[0][:], P(1)), (1, kf_[1][:], Q(1)), (1, kf_[2][:], Pc(0)), (1, kf_[3][:], Qc(0))], rk)
                combo(Yt[3][:, j, :], [(1, kf_[0][:], Q(1)), (-1, kf_[1][:], P(1)), (1, kf_[2][:], Qc(0)), (-1, kf_[3][:], Pc(0))], rk)
            for tb in range(NTB):
                ld(gsl[:], dfti_d[tb], ["gsl"])
                for sg_i in range(2):
                    for cc in range(CB // 128):
                        ob_ = ocnt % 2
                        bank = 6 + ocnt % 2
                        ocnt += 1
                        ch0 = c0 + cc * 128
                        tok0 = sg_i * SEG + tb * TBW
                        ld(xw[ob_][0][:], x0_fm[ch0:ch0 + 128, tok0:tok0 + TBW], [f"xw{ob_}0"])
                        ld(xw[ob_][1][:], w_fm[ch0:ch0 + 128, tok0:tok0 + TBW], [f"xw{ob_}1"])
                        for j in range(NF):
                            mm(ps[bank][:, 0:TBW], Yt[2 * sg_i][:, j, cc * 128:(cc + 1) * 128], gsl[:, j, 0, :], j == 0, False, ["Y", "gsl"], [f"ps{bank}"])
                            mm(ps[bank][:, 0:TBW], Yt[2 * sg_i + 1][:, j, cc * 128:(cc + 1) * 128], gsl[:, j, 1, :], False, j == NF - 1, ["Y", "gsl"], [f"ps{bank}"])
                        kch = ch0 // 128
                        stt(yf[ob_][:], xw[ob_][1][:], kb0[:, kch:kch + 1], ps[bank][:, 0:TBW], ALU.mult, ALU.add,
                            [f"xw{ob_}1", "kb0", f"ps{bank}"], [f"yf{ob_}"])
                        tt("dve", yb_[ob_][:], yf[ob_][:], xw[ob_][0][:], ALU.mult, [f"yf{ob_}", f"xw{ob_}0"], [f"yb{ob_}"])
                        stq(ycat[D + ch0:D + ch0 + 128, tok0:tok0 + TBW], yb_[ob_][:], [f"yb{ob_}"])
        S.barrier()

    def ssd_phase(l):
        H2 = 2 * NH
        QH = 4
        S.sb_reset()
        rowp = S.sb("rowp", [128, 4 * NH], F32)
        ld(rowp[:], prow[l, 0:4 * NH].partition_broadcast(128), ["rowp"])
        Abc = S.sb("Abc", [128, H2], F32)
        act(Abc[:], rowp[:, H2:2 * H2], AF.Exp, ["rowp"], ["Abc"])
        ts("dve", Abc[:], Abc[:], -1.0, None, ALU.mult, None, ["Abc"], ["Abc"])
        pcs = S.sb("pcs", [128, 2 * KD], F32)
        ld(pcs[:], pcol[l, :, PO["dskip"]:PO["dskip"] + 2 * KD], ["pcs"])
        neg4 = [S.sb("neg4", [128, QH, 128], BF16) for _ in range(2)]
        trib = [S.sb("trib", [128, 128], BF16) for _ in range(2)]
        negob = S.sb("negob", [128, 128], BF16)
        for d in range(2):
            cp("dve", neg4[d][:], C_("negf" if d == 0 else "negb").unsqueeze(1).broadcast_to([128, QH, 128]), ["cst"], [f"neg4{d}"])
            cp("dve", trib[d][:], C_("triU" if d == 0 else "triL"), ["cst"], [f"trib{d}"])
        cp("dve", negob[:], C_("negones"), ["cst"], ["negob"])
        tri = [C_("triU"), C_("triL")]
        dth = S.sb("dth", [128, H2], BF16)
        dthf = S.sb("dthf", [128, H2], F32)
        dtl = S.sb("dtl", [128, H2], BF16)
        dtr = S.sb("dtr", [128, 128], F32)
        uu = S.sb("su", [128, H2], F32)
        dt = S.sb("dt", [128, H2], F32)
        dtA = S.sb("dtA", [128, H2], F32)
        acs = S.sb("acs", [128, H2], F32)
        cd = S.sb("cd", [128, H2], F32)
        dend = S.sb("dend", [128, H2], F32)
        coef = S.sb("coef", [128, H2], F32)
        xt = S.sb("xt", [128, D], BF16)
        bt = S.sb("bt", [128, GN], BF16)
        xc = [S.sb("xc", [128, NH, 64], BF16) for _ in range(2)]
        xcd = [S.sb("xcd", [128, NH, 64], BF16) for _ in range(2)]
        mark = S.sbuf_off

        def prep(c, dirs):
            ld(dtr[0:H2, :], pdt[:, c * 128:(c + 1) * 128], ["dtr"])
            ld(xt[:], x_tok[c * 128:(c + 1) * 128, :], ["xt"])
            ld(bt[:], b_tok[c * 128:(c + 1) * 128, :], ["bt"])
            tr(ps[6][:, 0:H2], dtr[0:H2, :], C_("ident")[0:H2, 0:H2], ["dtr", "cst"], ["ps6"])
            tt("dve", uu[:], ps[6][:, 0:H2], rowp[:, 0:H2], ALU.add, ["ps6", "rowp"], ["su"])
            act(uu[:], uu[:], AF.Exp, ["su"], ["su"])
            act(dt[:], uu[:], AF.Ln, ["su"], ["dt"], bias=1.0)
            tt("dve", dtA[:], dt[:], Abc[:], ALU.mult, ["dt", "Abc"], ["dtA"])
            cp("dve", dth[:], dtA[:], ["dtA"], ["dth"])
            cp("dve", dthf[:], dth[:], ["dth"], ["dthf"])
            tt("dve", dtl[:], dtA[:], dthf[:], ALU.subtract, ["dtA", "dthf"], ["dtl"])
            mm(ps[6][:, 128:128 + NH], tri[0], dtA[:, 0:NH], True, True, ["cst", "dtA"], ["ps6"])
            mm(ps[6][:, 128 + NH:128 + H2], tri[1], dtA[:, NH:H2], True, True, ["cst", "dtA"], ["ps6"])
            mm(ps[6][:, 256:256 + H2], C_("ones"), dtA[:], True, True, ["cst", "dtA"], ["ps6"])
            cp("dve", acs[:], ps[6][:, 128:128 + H2], ["ps6"], ["acs"])
            act(cd[:], ps[6][:, 256:256 + H2], AF.Exp, ["ps6"], ["cd"])
            tt("dve", dend[:], ps[6][:, 256:256 + H2], acs[:], ALU.subtract, ["ps6", "acs"], ["dend"])
            act(dend[:], dend[:], AF.Exp, ["dend"], ["dend"])
            tt("dve", coef[:], dt[:], dend[:], ALU.mult, ["dt", "dend"], ["coef"])
            xv = xt[:].rearrange("p (h q) -> p h q", q=64)
            for d in dirs:
                tt("pool", xcd[d][:], xv, coef[:, d * NH:(d + 1) * NH].unsqueeze(2).broadcast_to([128, NH, 64]), ALU.mult,
                   ["xt", "coef"], [f"xcd{d}"])
                if len(dirs) == 2:
                    tt("pool", xc[d][:], xv, dt[:, d * NH:(d + 1) * NH].unsqueeze(2).broadcast_to([128, NH, 64]), ALU.mult,
                       ["xt", "dt"], [f"xc{d}"])

        def upd(Hs, hk, d):
            for g in range(NG):
                mm(ps[7][:, 0:HG * 64], bt[:, g * 128:(g + 1) * 128], xcd[d][:, g * HG:(g + 1) * HG, :].rearrange("p h q -> p (h q)"),
                   True, True, ["bt", f"xcd{d}"], ["ps7"])
                hv = Hs[:, g * HG * 64:(g + 1) * HG * 64]
                tt("dve", hv.rearrange("p (h q) -> p h q", q=64), hv.rearrange("p (h q) -> p h q", q=64),
                   cd[:, d * NH + g * HG:d * NH + (g + 1) * HG].unsqueeze(2).broadcast_to([128, HG, 64]), ALU.mult, [hk, "cd"], [hk])
                tt("dve", hv, hv, ps[7][:, 0:HG * 64], ALU.add, [hk, "ps7"], [hk])

        Hb = S.sb("Hb", [128, D], F32)
        Hbb = [S.sb("Hbb", [128, D], BF16) for _ in range(2)]
        mset("dve", Hb[:], 0.0, [], ["Hb"])
        for c in reversed(range(NCH)):
            prep(c, [1])
            b = c % 2
            cp("act", Hbb[b][:], Hb[:], ["Hb"], [f"Hbb{b}"])
            stq(hb_d[c], Hbb[b][:], [f"Hbb{b}"])
            upd(Hb, "Hb", 1)
            if c == SCH:
                ts("dve", Hb[:], Hb[:], cplc[:, 0:1], None, ALU.mult, None, ["Hb", "cplc"], ["Hb"])
        S.barrier()

        S.sbuf_off = mark
        Hf = S.sb("Hf", [128, D], F32)
        Hfb = S.sb("Hfb", [128, D], BF16)
        hbt = S.sb("hbt", [128, D], BF16)
        bcT = S.sb("bcT", [128, 2 * NG, 128], BF16)
        xsz = S.sb("xsz", [128, 2, KD, 128], BF16)
        cbT = S.sb("cbT", [128, NG, 128], BF16)
        prod = [S.sb("prod", [128, QH, 128], BF16) for _ in range(4)]
        prol = [S.sb("prol", [128, QH, 128], BF16) for _ in range(4)]
        E2 = [S.sb("E2", [128, QH, 128], BF16) for _ in range(2)]
        E1 = [S.sb("E1", [128, QH, 128], BF16) for _ in range(2)]
        NQ = HG // QH
        Mt = [[[S.sb("M", [128, QH, 128], BF16) for _ in range(NQ)] for _ in range(2)] for _ in range(2)]
        Cs = [[[S.sb("Cs", [128, QH, 128], BF16) for _ in range(NQ)] for _ in range(2)] for _ in range(2)]
        NP = HG // 2
        y1 = S.sb("y1", [128, NP, 128], F32)
        y2 = S.sb("y2", [128, NP, 128], F32)
        sqy = S.sb("sqy", [128, NP, 128], F32)
        rsn = S.sb("rsn", [128, 128], F32)
        SUP = 2 if NCH % 2 == 0 else 1
        yst = S.sb("yst", [128, KD, SUP * 128], BF16)
        mset("dve", Hf[:], 0.0, [], ["Hf"])
        qc = 0
        gc = 0
        for c in range(NCH):
            csl = slice(c * 128, (c + 1) * 128)
            prep(c, [0, 1])
            ld(hbt[:], hb_d[c], ["hbt"])
            cp("act", Hfb[:], Hf[:], ["Hf"], ["Hfb"])
            ld(bcT[:], xcT[D:D + 2 * GN, csl].rearrange("(g p) t -> p g t", p=128), ["bcT"])
            ld(xsz[:, 0], xcT[0:D, csl].rearrange("(k p) t -> p k t", p=128), ["xsz"])
            ld(xsz[:, 1], pz[:, csl].rearrange("(k p) t -> p k t", p=128), ["xsz"])
            for g in range(NG):
                mm(ps[7][:, 0:128], bcT[:, g, :], bcT[:, NG + g, :], True, True, ["bcT"], ["ps7"])
                cp("act", cbT[:, g, :], ps[7][:, 0:128], ["ps7"], ["cbT"])
            items = [(g, d, q) for g in range(NG) for d in range(2) for q in range(NQ)]

            def emit_prod(it, pb):
                g, d, q = it
                h0 = d * NH + g * HG + q * QH
                tt("dve", prod[pb][:], trib[d][:].unsqueeze(1).broadcast_to([128, QH, 128]),
                   dth[:, h0:h0 + QH].unsqueeze(2).broadcast_to([128, QH, 128]), ALU.mult, [f"trib{d}", "dth"], [f"prod{pb}"])
                tt("dve", prol[pb][:], trib[d][:].unsqueeze(1).broadcast_to([128, QH, 128]),
                   dtl[:, h0:h0 + QH].unsqueeze(2).broadcast_to([128, QH, 128]), ALU.mult, [f"trib{d}", "dtl"], [f"prol{pb}"])

            emit_prod(items[0], qc % 4)
            emit_prod(items[1], (qc + 1) % 4)
            for g in range(NG):
                gb = gc % 2
                gc += 1
                drip(1)
                for d in range(2):
                    for q in range(NQ):
                        ii = items.index((g, d, q))
                        pb = qc % 4
                        ab = qc % 2
                        bA, bB = 2 * ab, 2 * ab + 1
                        if ii + 2 < len(items):
                            emit_prod(items[ii + 2], (qc + 2) % 4)
                        qc += 1
                        pf = prod[pb][:].rearrange("p a b -> p (a b)")
                        pl_ = prol[pb][:].rearrange("p a b -> p (a b)")
                        mm(ps[bA][:, 0:QH * 128], onesb[:], pf, True, False, ["onesb", f"prod{pb}"], [f"ps{bA}"])
                        mm(ps[bA][:, 0:QH * 128], onesb[:], pl_, False, True, ["onesb", f"prol{pb}"], [f"ps{bA}"])
                        act(E2[ab][:].rearrange("p a b -> p (a b)"), ps[bA][:, 0:QH * 128], AF.Exp, [f"ps{bA}"], [f"E2{ab}"])
                        tt("pool", Cs[gb][d][q][:], E2[ab][:], bcT[:, NG + g, :].unsqueeze(1).broadcast_to([128, QH, 128]), ALU.mult,
                           [f"E2{ab}", "bcT"], [f"Cs{gb}{d}{q}"])
                        mm(ps[bB][:, 0:QH * 128], onesb[:], pf, True, False, ["onesb", f"prod{pb}"], [f"ps{bB}"])
                        mm(ps[bB][:, 0:QH * 128], onesb[:], pl_, False, False, ["onesb", f"prol{pb}"], [f"ps{bB}"])
                        mm(ps[bB][:, 0:QH * 128], identb[:], neg4[d][:].rearrange("p a b -> p (a b)"), False, False, ["identb", f"neg4{d}"], [f"ps{bB}"])
                        for i in range(QH):
                            mm(ps[bB][:, i * 128:(i + 1) * 128], prod[pb][:, i, :], negob[:], False, False,
                               ["negob", f"prod{pb}"], [f"ps{bB}"])
                            mm(ps[bB][:, i * 128:(i + 1) * 128], prol[pb][:, i, :], negob[:], False, i == QH - 1,
                               ["negob", f"prol{pb}"], [f"ps{bB}"])
                        act(E1[ab][:].rearrange("p a b -> p (a b)"), ps[bB][:, 0:QH * 128], AF.Exp, [f"ps{bB}"], [f"E1{ab}"])
                        tt("pool" if (qc % 2) else "dve", Mt[gb][d][q][:], E1[ab][:], cbT[:, g, :].unsqueeze(1).broadcast_to([128, QH, 128]), ALU.mult,
                           [f"E1{ab}", "cbT"], [f"M{gb}{d}{q}"])
                yb = 4 + gb
                rk = [f"M{gb}{d}{q}" for d in range(2) for q in range(NQ)] + [f"Cs{gb}{d}{q}" for d in range(2) for q in range(NQ)] + \
                     ["xc0", "xc1", "Hfb", "hbt"]
                for hp in range(NP):
                    for half in range(2):
                        hh = g * HG + hp * 2 + half
                        q, hi = (hp * 2 + half) // QH, (hp * 2 + half) % QH
                        out = ps[yb][half * 64:(half + 1) * 64, hp * 128:(hp + 1) * 128]
                        mm(out, xc[0][:, hh, :], Mt[gb][0][q][:, hi, :], True, False, rk, [f"ps{yb}"])
                        mm(out, Hfb[:, hh * 64:(hh + 1) * 64], Cs[gb][0][q][:, hi, :], False, False, rk, [f"ps{yb}"])
                        mm(out, xc[1][:, hh, :], Mt[gb][1][q][:, hi, :], False, False, rk, [f"ps{yb}"])
                        mm(out, hbt[:, hh * 64:(hh + 1) * 64], Cs[gb][1][q][:, hi, :], False, True, rk, [f"ps{yb}"])
                for hp in range(NP):
                    kch = g * NP + hp
                    stt(y1[:, hp, :], xsz[:, 0, kch, :], pcs[:, kch:kch + 1], ps[yb][:, hp * 128:(hp + 1) * 128], ALU.mult, ALU.add,
                        ["xsz", "pcs", f"ps{yb}"], ["y1"])
                tt("dve", y2[:], y1[:], xsz[:, 1, g * NP:(g + 1) * NP, :], ALU.mult, ["y1", "xsz"], ["y2"])
                tt("pool", sqy[:], y2[:], y2[:], ALU.mult, ["y2"], ["sqy"])
                for hp in range(NP):
                    mm(ps[6][:, 384:512], C_("ones"), sqy[:, hp, :], hp == 0, hp == NP - 1, ["cst", "sqy"], ["ps6"])
                ts("dve", rsn[:], ps[6][:, 384:512], 1.0 / (NP * 128), 1e-6, ALU.mult, ALU.add, ["ps6"], ["rsn"])
                act(rsn[:], rsn[:], AF.Ln, ["rsn"], ["rsn"])
                act(rsn[:], rsn[:], AF.Exp, ["rsn"], ["rsn"], scale=-0.5)
                for hp in range(NP):
                    kch = g * NP + hp
                    stt(yst[:, kch, (c % SUP) * 128:(c % SUP + 1) * 128], y2[:, hp, :], pcs[:, KD + kch:KD + kch + 1], rsn[:], ALU.mult, ALU.mult,
                        ["y2", "pcs", "rsn"], ["yst"])
            if c % SUP == SUP - 1:
                t0 = (c - SUP + 1) * 128
                stq(ycat[0:D, :].rearrange("(k p) t -> p k t", p=128)[:, :, t0:t0 + SUP * 128], yst[:], ["yst"])
            upd(Hf, "Hf", 0)
            if c == SCH - 1:
                ts("dve", Hf[:], Hf[:], cplc[:, 0:1], None, ALU.mult, None, ["Hf", "cplc"], ["Hf"])
        S.barrier()

    for l in range(L):
        norm_phase(l, "g1", hT)

        o_z, o_xbc, o_dt, o_hy, o_sg, o_gate = 0, D, D + XBC, D + XBC + 2 * NH, D + XBC + 2 * NH + 3 * HYW, \
            D + XBC + 2 * NH + 3 * HYW + 2 * SGW
        chunks = (chunks_of(o_z, o_xbc, "z") + chunks_of(o_xbc, o_dt, "xbc") + chunks_of(o_dt, o_hy, "dt") +
                  chunks_of(o_hy, o_sg, "hy") + chunks_of(o_sg, o_gate, "sg") + chunks_of(o_gate, NIN, "gate"))

        def win_setup(l=l):
            bg = S.sb("bg", [128, 3 * KD], F32)
            ld(bg[:], pcol[l, :, PO["bgate"]:PO["bgate"] + 3 * KD], ["bg"])
            ob = [S.sb("ob", [128, TT], BF16) for _ in range(3)]
            of = [S.sb("of", [128, TT], F32) for _ in range(2)]
            return bg, ob, of

        def win_handler(tag, c0, m, ti, pss, extra, cnt):
            bg, ob, of = extra
            pa, pk = pss[0]
            tsl = slice(ti * TT, (ti + 1) * TT)
            if tag == "dt":
                b = cnt % 2
                cp("act", of[b][0:m, :], pa, [pk], [f"of{b}"])
                stq(pdt[c0 - o_dt:c0 - o_dt + m, tsl], of[b][0:m, :], [f"of{b}"])
                return
            b = cnt % 3
            if tag == "z":
                act(ob[b][0:m, :], pa, AF.Silu, [pk], [f"ob{b}"])
                dst = pz[c0 - o_z:c0 - o_z + m, tsl]
            elif tag == "xbc":
                cp("dve", ob[b][0:m, :], pa, [pk], [f"ob{b}"])
                dst = pxbc[c0 - o_xbc:c0 - o_xbc + m, tsl]
            elif tag == "hy":
                cp("dve", ob[b][0:m, :], pa, [pk], [f"ob{b}"])
                dst = phy[c0 - o_hy:c0 - o_hy + m, tsl]
            elif tag == "sg":
                act(ob[b][0:m, :], pa, AF.Gelu, [pk], [f"ob{b}"])
                dst = psg[c0 - o_sg:c0 - o_sg + m, tsl]
            else:
                j = (c0 - o_gate) // 128
                act(ob[b][0:m, :], pa, AF.Sigmoid, [pk, "bg"], [f"ob{b}"], bias=bg[0:m, j:j + 1])
                dst = pgate[c0 - o_gate:c0 - o_gate + m, tsl]
            stq(dst, ob[b][0:m, :], [f"ob{b}"])

        gemm([(hT, wb_in[l])], chunks, 512, win_handler, win_setup, wkey=("in", l))
        if "stop_win" in debug and l == 0:
            break

        S.sb_reset()
        XK = XBC // 128
        cw = S.sb("cw", [128, 6 * XK], F32)
        ld(cw[:], pcol[l, :, PO["scw0"]:PO["scw0"] + 6 * XK], ["cw"])
        cin = [S.sb("cin", [128, 2, SEG + 4], BF16) for _ in range(2)]
        cacc = [S.sb("cacc", [128, 2, SEG], F32) for _ in range(2)]
        cout = [S.sb("cout", [128, T], BF16) for _ in range(2)]
        ctk = [S.sb("ctk", [128, 8, 128], BF16) for _ in range(2)]
        for b in range(2):
            mset("dve", cin[b][:, 0, 0:2], 0.0, [], [f"cin{b}"])
            mset("dve", cin[b][:, 1, SEG + 2:SEG + 4], 0.0, [], [f"cin{b}"])
        tcnt = 0
        for c in range(XK):
            b = c % 2
            ld(cin[b][:, :, 2:SEG + 2], pxbc[c * 128:(c + 1) * 128, :].rearrange("p (s t) -> p s t", s=2), [f"cin{b}"])
            ts("dve", cin[b][:, 0, SEG + 2:SEG + 4], cin[b][:, 1, 2:4], cplc[:, 0:1], None, ALU.mult, None, [f"cin{b}", "cplc"], [f"cin{b}"])
            ts("dve", cin[b][:, 1, 0:2], cin[b][:, 0, SEG:SEG + 2], cplc[:, 0:1], None, ALU.mult, None, [f"cin{b}", "cplc"], [f"cin{b}"])
            ts("dve", cacc[b][:], cin[b][:, :, 0:SEG], cw[:, c:c + 1], cw[:, 5 * XK + c:5 * XK + c + 1], ALU.mult, ALU.add,
               [f"cin{b}", "cw"], [f"cacc{b}"])
            for j in range(1, 5):
                stt(cacc[b][:], cin[b][:, :, j:j + SEG], cw[:, j * XK + c:j * XK + c + 1], cacc[b][:], ALU.mult, ALU.add,
                    [f"cin{b}", "cw", f"cacc{b}"], [f"cacc{b}"])
            act(cout[b][:], cacc[b][:].rearrange("p s t -> p (s t)"), AF.Silu, [f"cacc{b}"], [f"cout{b}"])
            stq(xcT[c * 128:(c + 1) * 128, :], cout[b][:], [f"cout{b}"])
            if c < KD + GN // 128:
                for t8 in range(0, NCH, 8):
                    n8 = min(8, NCH - t8)
                    bank = 2 + tcnt % 2
                    kb = tcnt % 2
                    tcnt += 1
                    pst = ps[bank][:].bitcast(BF16)
                    for i in range(n8):
                        tr(pst[:, i * 128:(i + 1) * 128], cout[b][:, (t8 + i) * 128:(t8 + i + 1) * 128], identb[:],
                           [f"cout{b}", "identb"], [f"ps{bank}"])
                    cp("act", ctk[kb][:, 0:n8, :], pst[:, 0:n8 * 128].rearrange("p (a b) -> p a b", b=128), [f"ps{bank}"], [f"ctk{kb}"])
                    if c < KD:
                        dst = x_tok[t8 * 128:(t8 + n8) * 128, c * 128:(c + 1) * 128]
                    else:
                        dst = b_tok[t8 * 128:(t8 + n8) * 128, (c - KD) * 128:(c - KD + 1) * 128]
                    stq(dst.rearrange("(a p) c -> p a c", p=128), ctk[kb][:, 0:n8, :], [f"ctk{kb}"])
        S.barrier()

        ssd_phase(l)
        hyena_phase(l)
        sgu_phase(l)

        def wbr_setup():
            gt = [[S.sb("gt", [128, TT], BF16) for _ in range(3)] for _ in range(2)]
            t3 = [S.sb("t3", [128, TT], F32) for _ in range(3)]
            ob = [S.sb("mo", [128, TT], BF16) for _ in range(2)]
            return gt, t3, ob

        def wbr_handler(tag, c0, m, ti, pss, extra, cnt):
            gt, t3, ob = extra
            b = cnt % 2
            tsl = slice(ti * TT, (ti + 1) * TT)
            for g in range(3):
                ld(gt[b][g][0:m, :], pgate[g * D + c0:g * D + c0 + m, tsl], [f"gt{b}_{g}"])
            for g in range(3):
                tt("dve", t3[g][0:m, :], pss[g][0], gt[b][g][0:m, :], ALU.mult, [pss[g][1], f"gt{b}_{g}"], [f"t3_{g}"])
            tt("dve", t3[0][0:m, :], t3[0][0:m, :], t3[1][0:m, :], ALU.add, ["t3_0", "t3_1"], ["t3_0"])
            tt("dve", ob[b][0:m, :], t3[0][0:m, :], t3[2][0:m, :], ALU.add, ["t3_0", "t3_2"], [f"mo{b}"])
            stq(mergedT[c0:c0 + m, tsl], ob[b][0:m, :], [f"mo{b}"])

        gemm([(ycat[0:D, :], wrows(wb_br[l], 0, D)), (ycat[D:D + HYW, :], wrows(wb_br[l], D, D + HYW)),
              (ycat[D + HYW:2 * D, :], wrows(wb_br[l], D + HYW, 2 * D))], chunks_of(0, D, "m"), 256, wbr_handler, wbr_setup, wkey=("br", l))

        def res_setup():
            xr = [S.sb("xr", [128, TT], F32) for _ in range(3)]
            return xr

        def res_handler(tag, c0, m, ti, pss, extra, cnt):
            xr = extra
            b = cnt % 3
            tsl = slice(ti * TT, (ti + 1) * TT)
            ld(xr[b][0:m, :], xT[c0:c0 + m, tsl], [f"xr{b}"])
            tt("dve", xr[b][0:m, :], pss[0][0], xr[b][0:m, :], ALU.add, [pss[0][1], f"xr{b}"], [f"xr{b}"])
            stq(xT[c0:c0 + m, tsl], xr[b][0:m, :], [f"xr{b}"])

        gemm([(mergedT, wb_out[l])], chunks_of(0, D, "r"), 512, res_handler, res_setup, wkey=("out", l))

        norm_phase(l, "g2", hT)

        def up_setup():
            return [S.sb("uo", [128, TT], BF16) for _ in range(3)]

        def up_handler(tag, c0, m, ti, pss, extra, cnt):
            b = cnt % 3
            eng = "act" if cnt % 2 else "dve"
            cp(eng, extra[b][0:m, :], pss[0][0], [pss[0][1]], [f"uo{b}"])
            stq(upT[c0:c0 + m, ti * TT:(ti + 1) * TT], extra[b][0:m, :], [f"uo{b}"])

        gemm([(hT, wb_up[l])], chunks_of(0, 2 * DFF, "u"), 512, up_handler, up_setup, wkey=("up", l))

        S.sb_reset()
        FK = DFF // 128
        fw = S.sb("fw", [128, 8 * FK], F32)
        ld(fw[:], pcol[l, :, PO["fcw0"]:PO["fcw0"] + 8 * FK], ["fw"])
        fin = [[S.sb("fin", [128, 2, SEG + 2], BF16) for _ in range(2)] for _ in range(2)]
        dg = [[[S.sb("dg", [128, 128], BF16) for _ in range(3)] for _ in range(2)] for _ in range(2)]
        fsg = [S.sb("fsg", [128, TT], F32) for _ in range(2)]
        fo = [S.sb("fo", [128, T], BF16) for _ in range(2)]
        for b in range(2):
            for h in range(2):
                mset("dve", fin[b][h][:, 0, 0:1], 0.0, [], [f"fin{b}{h}"])
                mset("dve", fin[b][h][:, 1, SEG + 1:SEG + 2], 0.0, [], [f"fin{b}{h}"])
        fcnt_ = 0
        for c in range(FK):
            b = c % 2
            for h in range(2):
                cc = c + h * FK
                k_in = f"fin{b}{h}"
                ld(fin[b][h][:, :, 1:SEG + 1], upT[cc * 128:(cc + 1) * 128, :].rearrange("p (s t) -> p s t", s=2), [k_in])
                ts("dve", fin[b][h][:, 0, SEG + 1:SEG + 2], fin[b][h][:, 1, 1:2], cplc[:, 0:1], None, ALU.mult, None, [k_in, "cplc"], [k_in])
                ts("dve", fin[b][h][:, 1, 0:1], fin[b][h][:, 0, SEG:SEG + 1], cplc[:, 0:1], None, ALU.mult, None, [k_in, "cplc"], [k_in])
                for j in range(3):
                    ts("dve", dg[b][h][j][:], C_("ident"), fw[:, j * 2 * FK + cc:j * 2 * FK + cc + 1], None, ALU.mult, None,
                       ["cst", "fw"], [f"dg{b}{h}{j}"])
            for s_ in range(2):
                for t0 in range(0, SEG, TT):
                    fb_ = fcnt_ % 2
                    fcnt_ += 1
                    bg_, bv_ = fb_, 2 + fb_
                    for h, bank in ((0, bg_), (1, bv_)):
                        for j in range(3):
                            mm(ps[bank][:, 0:TT], dg[b][h][j][:], fin[b][h][:, s_, t0 + j:t0 + j + TT], j == 0, j == 2,
                               [f"dg{b}{h}{j}", f"fin{b}{h}"], [f"ps{bank}"])
                    act(fsg[fb_][:], ps[bg_][:, 0:TT], AF.Silu, [f"ps{bg_}", "fw"], [f"fsg{fb_}"], bias=fw[:, 6 * FK + c:6 * FK + c + 1])
                    stt(fo[b][:, s_ * SEG + t0:s_ * SEG + t0 + TT], ps[bv_][:, 0:TT], fw[:, 6 * FK + FK + c:6 * FK + FK + c + 1], fsg[fb_][:],
                        ALU.add, ALU.mult, [f"ps{bv_}", "fw", f"fsg{fb_}"], [f"fo{b}"])
            stq(actT[c * 128:(c + 1) * 128, :], fo[b][:], [f"fo{b}"])
            drip(1)
            drip(1)
        S.barrier()

        gemm([(actT, wb_down[l])], chunks_of(0, D, "r"), 256, res_handler, res_setup, wkey=("down", l))

    if "stop_win" not in debug:
        norm_phase(0, "gf", None, final=True)
    S.emit()
    return nc


def _col(v):
    v = np.asarray(v, np.float32)
    return v.reshape(-1, 128).T


def prep_inputs(cfg, inp, n_prompt=8, n_sample=2):
    D, SEG, T, L, NH = cfg["D"], cfg["SEG"], cfg["T"], cfg["L"], cfg["NH"]
    PO = cfg["pcol_off"]
    f = lambda k: np.asarray(inp[k], np.float32)
    pcol = np.zeros((L, 128, cfg["NCOL"]), np.float32)
    prow = np.zeros((L, cfg["NROW"]), np.float32)
    p64 = np.zeros((L, 64, 4), np.float32)

    def put(l, name, v):
        c = _col(v)
        pcol[l, :, PO[name]:PO[name] + c.shape[1]] = c

    for l in range(L):
        put(l, "g1", f("norm1_g")[l])
        put(l, "g2", f("norm2_g")[l])
        put(l, "bgate", f("b_gate")[l])
        for j in range(5):
            put(l, f"scw{j}", f("ssm_conv_w")[l, j])
        put(l, "scb", f("ssm_conv_b")[l])
        put(l, "dskip", np.repeat(f("ssm_d")[l], 64))
        put(l, "ng", f("ssm_norm_g")[l])
        for j in range(3):
            put(l, f"hcw{j}", f("hy_conv_w")[l, j])
        put(l, "hcb", f("hy_conv_b")[l])
        put(l, "hbias", f("hy_bias")[l])
        put(l, "lng", f("sg_ln_g")[l])
        put(l, "lnb", f("sg_ln_b")[l])
        for j in range(3):
            put(l, f"fcw{j}", f("ffn_conv_w")[l, j])
        put(l, "fcb", f("ffn_conv_b")[l])
        put(l, "gf", f("normf_g"))
        prow[l, 0:2 * NH] = f("ssm_dt_bias")[l].reshape(-1)
        prow[l, 2 * NH:4 * NH] = f("ssm_a_log")[l].reshape(-1)
        prow[l, 4 * NH:] = f("sg_bs")[l].reshape(-1)
        p64[l, :, 0] = f("hy_b1")[l]
        p64[l, :, 1] = f("hy_b2")[l]
        p64[l, :, 2] = f("hy_b3")[l]
        p64[l, :, 3] = f("hy_freq")[l]
    hc = host_consts(cfg)
    shared = dict(w_in=f("w_in"), w_br=f("w_br"), w_out=f("w_out"), w_up=f("w_up"), w_down=f("w_down"),
                  hy_w1=f("hy_w1"), hy_w2=f("hy_w2"), hy_w3=f("hy_w3"), hy_w4=f("hy_w4"), p64=p64,
                  sg_ws=f("sg_ws"), pcol=pcol, prow=prow, cst=hc["cst"], deltas=hc["deltas"],
                  dftf=hc["dftf"], dfti=hc["dfti"])
    xp, xs = f("x_prompt"), f("x_sample")
    zp_s, tn_s = host_zpos(2 * SEG, T)
    zp_p, tn_p = host_zpos(SEG, T)
    plan = []
    for i in range(n_sample):
        plan.append(("s", i))
    rest = 8 - n_sample
    npair = n_prompt - rest
    pi = 0
    for c in range(rest):
        if c < npair:
            plan.append(("pp", pi, pi + 1))
            pi += 2
        else:
            plan.append(("p", pi))
            pi += 1
    maps = []
    for pl in plan:
        m = dict(shared)
        if pl[0] == "s":
            m["x_in"] = np.ascontiguousarray(xs[pl[1]])
            m["cpl"] = np.ones((128, 1), np.float32)
            m["zposT"], m["tneg"] = zp_s, tn_s
        else:
            x = np.zeros((T, D), np.float32)
            x[0:SEG] = xp[pl[1]]
            if pl[0] == "pp":
                x[SEG:] = xp[pl[2]]
            m["x_in"] = x
            m["cpl"] = np.zeros((128, 1), np.float32)
            m["zposT"], m["tneg"] = zp_p, tn_p
        maps.append(m)
    return maps, plan


def assemble(cfg, plan, results, n_prompt=8, n_sample=2, key="y_out"):
    D, SEG, T = cfg["D"], cfg["SEG"], cfg["T"]
    yp = np.zeros((n_prompt, SEG, D), np.float32)
    ys = np.zeros((n_sample, T, D), np.float32)
    for pl, r in zip(plan, results):
        y = np.asarray(r[key], np.float32)
        if pl[0] == "s":
            ys[pl[1]] = y
        else:
            yp[pl[1]] = y[0:SEG]
            if pl[0] == "pp":
                yp[pl[2]] = y[SEG:]
    return yp, ys


def kernel(**inputs):
    cfg = make_cfg()
    nc = build(cfg)
    maps, plan = prep_inputs(cfg, inputs)
    res = run_bass_kernel_spmd(nc, maps, core_ids=list(range(8)))
    yp, ys = assemble(cfg, plan, res.results)
    return (yp, ys)
```
